# Optimizing a Trainium2 kernel written in Bass

```python
import jax, jax.numpy as jnp
from jax import lax
import numpy as np

D_MODEL = 1024
BATCH = 32
SEQ = 256
DEPTH = 2
DEC_BATCH = 4
DEC_SEQ = 1024
PAST_LEN = 512

GRID_W = 64
N_EVEN = (DEPTH + 1) // 2
N_ODD = DEPTH // 2
N_MOD = 9
D_FF = ((8 * D_MODEL // 3 + 127) // 128) * 128
D_CONV = D_MODEL // 2
CONV_WIDTH = 31
D_POOL = D_MODEL // 2
POOL_WINDOWS = (2, 4, 8, 16)
POOL_GROUP = D_POOL // len(POOL_WINDOWS)
N_HEADS_C = 8
QK_NOPE = 128
QK_ROPE = 64
V_DIM = 128
KV_LORA = D_MODEL // 4
Q_LORA = 3 * D_MODEL // 8
ROPE_AXIS_PAIRS = QK_ROPE // 4
ROPE_BASE = 10000.0
Q_BLOCK = 128
ALPHA = (2 * DEPTH) ** 0.25
BETA = (8 * DEPTH) ** -0.25
LN_EPS = 1e-5
RMS_EPS = 1e-6

kernel_name = 'hybrid_conv_pool_mla_prefix_dit'


def layer_norm(x, g, b):
    xf = x.astype(jnp.float32)
    mu = jnp.mean(xf, -1, keepdims=True)
    var = jnp.mean(jnp.square(xf - mu), -1, keepdims=True)
    return ((xf - mu) * lax.rsqrt(var + LN_EPS)).astype(x.dtype) * g + b


def rms_norm(x, g):
    xf = x.astype(jnp.float32)
    return (xf * lax.rsqrt(jnp.mean(xf * xf, -1, keepdims=True) + RMS_EPS)).astype(x.dtype) * g


def modulate(x, shift, scale):
    return x * (1 + scale) + shift


def swiglu(x, w1, w3, w2):
    return (jax.nn.silu(x @ w1) * (x @ w3)) @ w2


def ffn_sub(x, shift, scale, gate, g, b, w1, w3, w2):
    h = modulate(x, shift, scale)
    return layer_norm(ALPHA * x + 0.5 * gate * swiglu(h, w1, w3, w2), g, b)


def conv_module(u, gate, conv_w, conv_b, norm_g, norm_b):
    h = u * jax.nn.sigmoid(gate)
    pad = CONV_WIDTH // 2
    h = lax.conv_general_dilated(h, conv_w[:, None, :], window_strides=(1,), padding=((pad, pad),),
                                 dimension_numbers=('NWC', 'WIO', 'NWC'),
                                 feature_group_count=h.shape[-1]) + conv_b
    return jax.nn.silu(layer_norm(h, norm_g, norm_b))


def pool_module(h, pool_w, pool_scale):
    n = h.shape[1]
    t = jnp.arange(n)
    cs = jnp.pad(jnp.cumsum(h.astype(jnp.float32), axis=1), ((0, 0), (1, 0), (0, 0)))
    outs = []
    for gi, w in enumerate(POOL_WINDOWS):
        left = w // 2
        right = w - 1 - left
        lo = jnp.clip(t - left, 0, n - 1)
        hi = jnp.clip(t + right, 0, n - 1)
        sl = slice(gi * POOL_GROUP, (gi + 1) * POOL_GROUP)
        cg = cs[..., sl]
        mean = (jnp.take(cg, hi + 1, axis=1) - jnp.take(cg, lo, axis=1)) / (hi - lo + 1).astype(jnp.float32)[:, None]
        outs.append((mean.astype(h.dtype) - h[..., sl]) @ pool_w[gi])
    return jnp.concatenate(outs, -1) * pool_scale


def conv_pool_mixer(x, w_in, conv_w, conv_b, norm_g, norm_b, pool_w, pool_scale, w_out):
    proj = x @ w_in
    u = proj[..., :D_CONV]
    gate = proj[..., D_CONV:2 * D_CONV]
    hp = proj[..., 2 * D_CONV:]
    a = conv_module(u, gate, conv_w, conv_b, norm_g, norm_b)
    b = pool_module(hp, pool_w, pool_scale)
    return jnp.concatenate([a, b], -1) @ w_out


def axial_angles(n):
    rows = n // GRID_W
    row = jnp.repeat(jnp.arange(rows), GRID_W)
    col = jnp.tile(jnp.arange(GRID_W), rows)
    inv = ROPE_BASE ** (-jnp.arange(ROPE_AXIS_PAIRS, dtype=jnp.float32) / ROPE_AXIS_PAIRS)
    ang = jnp.concatenate([row[:, None] * inv, col[:, None] * inv], -1)
    return jnp.cos(ang), jnp.sin(ang)


def apply_rope_2d(x, cos, sin):
    n = cos.shape[0]
    xs = x.reshape(x.shape[:-1] + (2, 2, ROPE_AXIS_PAIRS))
    x1, x2 = xs[..., 0, :], xs[..., 1, :]
    bshape = (n,) + (1,) * (x.ndim - 3) + (2, ROPE_AXIS_PAIRS)
    c = cos.reshape(bshape)
    s = sin.reshape(bshape)
    out = jnp.stack([x1 * c - x2 * s, x1 * s + x2 * c], axis=-2)
    return out.reshape(x.shape).astype(x.dtype)


def mla_q(x, w_dq, q_norm_g, w_uq):
    q = rms_norm(x @ w_dq, q_norm_g) @ w_uq
    q = q.reshape(x.shape[:2] + (N_HEADS_C, QK_NOPE + QK_ROPE))
    return q[..., :QK_NOPE], q[..., QK_NOPE:]


def mla_compress_kv(x, w_dkv, kv_norm_g):
    kv = x @ w_dkv
    return rms_norm(kv[..., :KV_LORA], kv_norm_g), kv[..., KV_LORA:]


def mla_expand(c_kv, w_ukv):
    kv = (c_kv @ w_ukv).reshape(c_kv.shape[:2] + (N_HEADS_C, QK_NOPE + V_DIM))
    return kv[..., :QK_NOPE], kv[..., QK_NOPE:]


def mla_attend(q_nope, q_rope, k_nope, k_rope, v):
    b, n = q_nope.shape[:2]
    qb = min(Q_BLOCK, n)
    nb = n // qb
    scale = (QK_NOPE + QK_ROPE) ** -0.5

    def block(qs):
        qn, qr = qs
        s = (jnp.einsum('bqhd,bkhd->bhqk', qn, k_nope, preferred_element_type=jnp.float32)
             + jnp.einsum('bqhr,bkr->bhqk', qr, k_rope, preferred_element_type=jnp.float32))
        p = jax.nn.softmax(s * scale, axis=-1).astype(v.dtype)
        return jnp.einsum('bhqk,bkhd->bqhd', p, v)

    def to_blocks(a):
        return jnp.moveaxis(a.reshape((b, nb, qb) + a.shape[2:]), 1, 0)

    out = lax.map(block, (to_blocks(q_nope), to_blocks(q_rope)))
    return jnp.moveaxis(out, 0, 1).reshape(b, n, N_HEADS_C * V_DIM)


def setup_inputs(seed: int = 0) -> dict:
    key = jax.random.key(seed)
    ks = jax.random.split(key, 32)

    def nrm(k, shape, scale):
        return jax.random.normal(k, shape, jnp.float32) * scale

    d = D_MODEL
    return {
        'x_prompt': nrm(ks[0], (BATCH, SEQ, d), 1.0),
        'x_sample': nrm(ks[1], (DEC_BATCH, DEC_SEQ, d), 1.0),
        'cache_mla_ckv': nrm(ks[2], (DEC_BATCH, N_ODD, PAST_LEN, KV_LORA), 1.0),
        'cache_mla_krope': nrm(ks[3], (DEC_BATCH, N_ODD, PAST_LEN, QK_ROPE), 1.0),
        'c': nrm(ks[4], (DEC_BATCH, d), 1.0),
        'c_ctx': nrm(ks[5], (d,), 1.0),
        'w_ada': nrm(ks[6], (DEPTH, d, N_MOD * d), d ** -0.5),
        'b_ada': nrm(ks[7], (DEPTH, N_MOD * d), 0.01),
        'ln_g': 1.0 + nrm(ks[8], (DEPTH, 3, d), 0.05),
        'ln_b': nrm(ks[9], (DEPTH, 3, d), 0.02),
        'ffn_w1': nrm(ks[10], (DEPTH, 2, d, D_FF), d ** -0.5),
        'ffn_w3': nrm(ks[11], (DEPTH, 2, d, D_FF), d ** -0.5),
        'ffn_w2': nrm(ks[12], (DEPTH, 2, D_FF, d), BETA * D_FF ** -0.5),
        'cp_w_in': nrm(ks[13], (N_EVEN, d, 2 * D_CONV + D_POOL), d ** -0.5),
        'conv_w': nrm(ks[14], (N_EVEN, CONV_WIDTH, D_CONV), CONV_WIDTH ** -0.5),
        'conv_b': nrm(ks[15], (N_EVEN, D_CONV), 0.02),
        'conv_norm_g': 1.0 + nrm(ks[16], (N_EVEN, D_CONV), 0.05),
        'conv_norm_b': nrm(ks[17], (N_EVEN, D_CONV), 0.02),
        'pool_w': nrm(ks[18], (N_EVEN, len(POOL_WINDOWS), POOL_GROUP, POOL_GROUP), POOL_GROUP ** -0.5),
        'pool_scale': 1.0 + nrm(ks[19], (N_EVEN, D_POOL), 0.1),
        'cp_w_out': nrm(ks[20], (N_EVEN, D_CONV + D_POOL, d), BETA * (D_CONV + D_POOL) ** -0.5),
        'mla_w_dq': nrm(ks[21], (N_ODD, d, Q_LORA), d ** -0.5),
        'mla_q_norm_g': 1.0 + nrm(ks[22], (N_ODD, Q_LORA), 0.05),
        'mla_w_uq': nrm(ks[23], (N_ODD, Q_LORA, N_HEADS_C * (QK_NOPE + QK_ROPE)), Q_LORA ** -0.5),
        'mla_w_dkv': nrm(ks[24], (N_ODD, d, KV_LORA + QK_ROPE), d ** -0.5),
        'mla_kv_norm_g': 1.0 + nrm(ks[25], (N_ODD, KV_LORA), 0.05),
        'mla_w_ukv': nrm(ks[26], (N_ODD, KV_LORA, N_HEADS_C * (QK_NOPE + V_DIM)), KV_LORA ** -0.5),
        'mla_w_o': nrm(ks[27], (N_ODD, N_HEADS_C * V_DIM, d), BETA * (N_HEADS_C * V_DIM) ** -0.5),
    }


def reference(x_prompt, x_sample, cache_mla_ckv, cache_mla_krope, c, c_ctx,
              w_ada, b_ada, ln_g, ln_b, ffn_w1, ffn_w3, ffn_w2,
              cp_w_in, conv_w, conv_b, conv_norm_g, conv_norm_b, pool_w, pool_scale, cp_w_out,
              mla_w_dq, mla_q_norm_g, mla_w_uq, mla_w_dkv, mla_kv_norm_g, mla_w_ukv, mla_w_o):
    ada_ctx = jnp.einsum('d,lde->le', jax.nn.silu(c_ctx), w_ada) + b_ada
    ada_lat = jnp.einsum('bd,lde->lbe', jax.nn.silu(c), w_ada) + b_ada[:, None, :]
    cos, sin = axial_angles(x_sample.shape[1])

    xp, xs = x_prompt, x_sample
    new_ckv, new_krope = [], []
    for i in range(DEPTH):
        mp = jnp.split(ada_ctx[i][None, None, :], N_MOD, axis=-1)
        ms = jnp.split(ada_lat[i][:, None, :], N_MOD, axis=-1)
        xp = ffn_sub(xp, mp[0], mp[1], mp[2], ln_g[i, 0], ln_b[i, 0], ffn_w1[i, 0], ffn_w3[i, 0], ffn_w2[i, 0])
        xs = ffn_sub(xs, ms[0], ms[1], ms[2], ln_g[i, 0], ln_b[i, 0], ffn_w1[i, 0], ffn_w3[i, 0], ffn_w2[i, 0])
        hp = modulate(xp, mp[3], mp[4])
        hs = modulate(xs, ms[3], ms[4])
        j = i // 2
        if i % 2 == 0:
            yp = conv_pool_mixer(hp, cp_w_in[j], conv_w[j], conv_b[j], conv_norm_g[j], conv_norm_b[j],
                                 pool_w[j], pool_scale[j], cp_w_out[j])
            ys = conv_pool_mixer(hs, cp_w_in[j], conv_w[j], conv_b[j], conv_norm_g[j], conv_norm_b[j],
                                 pool_w[j], pool_scale[j], cp_w_out[j])
        else:
            ckv_p, kr_p = mla_compress_kv(hp, mla_w_dkv[j], mla_kv_norm_g[j])
            new_ckv.append(ckv_p)
            new_krope.append(kr_p)
            qn_p, qr_p = mla_q(hp, mla_w_dq[j], mla_q_norm_g[j], mla_w_uq[j])
            kn_p, v_p = mla_expand(ckv_p, mla_w_ukv[j])
            yp = mla_attend(qn_p, qr_p, kn_p, kr_p, v_p) @ mla_w_o[j]
            ckv_s, kr_s = mla_compress_kv(hs, mla_w_dkv[j], mla_kv_norm_g[j])
            kr_s = apply_rope_2d(kr_s, cos, sin)
            qn_s, qr_s = mla_q(hs, mla_w_dq[j], mla_q_norm_g[j], mla_w_uq[j])
            qr_s = apply_rope_2d(qr_s, cos, sin)
            ckv_all = jnp.concatenate([ckv_s, cache_mla_ckv[:, j]], axis=1)
            kr_all = jnp.concatenate([kr_s, cache_mla_krope[:, j]], axis=1)
            kn_s, v_s = mla_expand(ckv_all, mla_w_ukv[j])
            ys = mla_attend(qn_s, qr_s, kn_s, kr_all, v_s) @ mla_w_o[j]
        xp = layer_norm(ALPHA * xp + mp[5] * yp, ln_g[i, 1], ln_b[i, 1])
        xs = layer_norm(ALPHA * xs + ms[5] * ys, ln_g[i, 1], ln_b[i, 1])
        xp = ffn_sub(xp, mp[6], mp[7], mp[8], ln_g[i, 2], ln_b[i, 2], ffn_w1[i, 1], ffn_w3[i, 1], ffn_w2[i, 1])
        xs = ffn_sub(xs, ms[6], ms[7], ms[8], ln_g[i, 2], ln_b[i, 2], ffn_w1[i, 1], ffn_w3[i, 1], ffn_w2[i, 1])

    new_mla_ckv = jnp.stack(new_ckv, axis=1)
    new_mla_krope = jnp.stack(new_krope, axis=1)
    return (xp, xs, new_mla_ckv, new_mla_krope)
```

```python
import numpy as np
import concourse.bass as bass
import concourse.mybir as mybir
from concourse.bass_utils import run_bass_kernel_spmd

F32 = mybir.dt.float32
BF16 = mybir.dt.bfloat16
AF = mybir.ActivationFunctionType
ALU = mybir.AluOpType

T = 1536
D = 1024
KC = 8
TB = 512
NTB = 3
DFF = 2816
NJ = 22
ALPHA = float((2 * 2) ** 0.25)
LN_EPS = 1e-5
RMS_EPS = 1e-6
NEG = -30000.0

CO_COND = 0
CO_BADA = CO_COND + 16
CO_LNG = CO_BADA + 144
CO_LNB = CO_LNG + 48
CO_CW = CO_LNB + 48
CO_CB = CO_CW + 124
CO_CNG = CO_CB + 4
CO_CNB = CO_CNG + 4
CO_PSC = CO_CNB + 4
CO_QG = CO_PSC + 4
CO_KVG = CO_QG + 3
CO_FLAG = CO_KVG + 2
NV = CO_FLAG + 1

ENGS = ('pe', 'act', 'dve', 'pool', 'sp')


class Prog:
    def __init__(self):
        self.ops = {e: [] for e in ENGS}
        self.cnt = {}
        self.seen = {e: {} for e in ENGS}
        self.res = {}

    def op(self, eng, fn, reads=(), writes=(), inc=True, dma=None):
        need = {}

        def add(tok):
            if tok is not None:
                s, v = tok
                if need.get(s, 0) < v:
                    need[s] = v
        for r in reads:
            e = self.res.get(r)
            if e is not None:
                add(e[0])
        for w in writes:
            e = self.res.get(w)
            if e is not None:
                add(e[0])
                for s, v in e[1].items():
                    add((s, v))
        if dma is not None and self.cnt.get(dma, 0) > 0:
            add((dma, self.cnt[dma]))
        waits = []
        for s, v in need.items():
            if eng == 'pe' and s == 'pe':
                continue
            if self.seen[eng].get(s, 0) >= v:
                continue
            self.seen[eng][s] = v
            waits.append((s, v))
        if dma is not None:
            self.cnt[dma] = self.cnt.get(dma, 0) + 16
            tok = (dma, self.cnt[dma])
            incspec = (dma, 16)
        else:
            before = self.cnt.get(eng, 0)
            tok = (eng, before + 1)
            if inc:
                self.cnt[eng] = before + 1
                incspec = (eng, 1)
            else:
                incspec = None
        for r in reads:
            e = self.res.setdefault(r, [None, {}])
            if e[1].get(tok[0], 0) < tok[1]:
                e[1][tok[0]] = tok[1]
        for w in writes:
            self.res[w] = [tok, {}]
        self.ops[eng].append((waits, fn, incspec))


    def fence(self, engines=('pe', 'act', 'dve', 'sp')):
        for eng in engines:
            waits = []
            for s, v in self.cnt.items():
                if v == 0 or (eng == 'pe' and s == 'pe'):
                    continue
                if self.seen[eng].get(s, 0) >= v:
                    continue
                self.seen[eng][s] = v
                waits.append((s, v))
            if waits:
                self.ops[eng].append((waits, None, None))


class Builder:
    def __init__(self, debug=None):
        self.debug = debug
        nc = self.nc = bass.Bass("TRN2", target_bir_lowering=False)
        self.P = Prog()
        self.dr = {}

        def din(name, shape):
            self.dr[name] = nc.dram_tensor(name, list(shape), F32, kind="ExternalInput").ap()

        def dout(name, shape):
            self.dr[name] = nc.dram_tensor(name, list(shape), F32, kind="ExternalOutput").ap()
        din('x', [T, D]); din('vecs', [128, NV]); din('ident', [128, 128])
        din('cache_ckv', [512, 256]); din('cache_kr', [512, 64])
        din('ropeC', [64, 1024]); din('ropeS', [64, 1024])
        din('maskk', [4, 1536]); din('onehot', [4, 1024])
        din('w_ada', [2, 1024, 9216])
        din('ffn_w1', [2, 2, 1024, DFF]); din('ffn_w3', [2, 2, 1024, DFF]); din('ffn_w2', [2, 2, DFF, 1024])
        din('cp_w_in', [1, 1024, 1536]); din('pool_w', [1, 4, 128, 128]); din('cp_w_out', [1, 1024, 1024])
        din('mla_w_dq', [1, 1024, 384]); din('mla_w_uq', [1, 384, 1536]); din('w_uq_sw', [384, 512])
        din('mla_w_dkv', [1, 1024, 320]); din('w_dkv_sw', [1024, 64])
        din('mla_w_ukv', [1, 256, 2048]); din('mla_w_o', [1, 1024, 1024])
        dout('y', [T, D]); dout('ockv', [T, 256]); dout('okr', [T, 64])

        self.big = nc.alloc_sbuf_tensor("big", [128, 103 * 1024], BF16)
        self.off = 0
        self.X = self.alloc((KC, T), F32)
        self.H = self.alloc((KC, T), BF16)
        self.g_off = self.off
        self.G = self.alloc((NJ, T), BF16)
        self.g_end = self.off
        self.WF = [self.alloc((2048,), BF16) for _ in range(8)]
        self.WA = [self.alloc((8, 256), BF16) for _ in range(2)]
        self.VEC = self.alloc((NV,), F32)
        self.ADAT = self.alloc((2, 72, 2), F32)
        self.IDF = self.alloc((128,), F32)
        self.IDB = self.alloc((128,), BF16)
        self.ONES = self.alloc((128,), BF16)
        self.SC = self.alloc((8, 2), BF16)
        self.SM = self.alloc((16,), F32)
        self.DRF = self.alloc((128,), F32)
        self.ONESF = self.alloc((128,), F32)
        self.rb_bank = 7
        self.EPS_LN = self.alloc((1,), F32)
        self.EPS_RMS = self.alloc((1,), F32)
        self.ST = [self.alloc((TB,), F32) for _ in range(4)]
        self.TMP = [self.alloc((TB,), F32) for _ in range(4)]
        self.XB = [self.alloc((TB,), BF16) for _ in range(4)]
        print("sbuf used bytes/partition:", self.off)
        assert self.off <= 206 * 1024
        self.ps = [nc.alloc_psum_tensor("ps%d" % b, [128, 512], F32)[:, :] for b in range(8)]
        self.stage_off = self.g_end - 8192
        self.bank_rr = 0
        self.nbank_rr = 7
        self.live_banks = set()
        self.wf_rr = 0
        self.wa_rr = 0
        self.tmp_rr = 0
        self.xb_rr = 0
        self.out_rr = 0

    def view(self, off, shape, dtype):
        n = int(np.prod(shape))
        esz = 4 if dtype == F32 else 2
        assert off % 4 == 0
        ap = self.big[:, off // 2: (off + n * esz) // 2]
        if dtype == F32:
            ap = ap.bitcast(F32)
        if len(shape) == 2:
            ap = ap.rearrange("p (a b) -> p a b", a=shape[0])
        elif len(shape) == 3:
            ap = ap.rearrange("p (a b c) -> p a b c", a=shape[0], b=shape[1])
        return ap

    def alloc(self, shape, dtype):
        n = int(np.prod(shape))
        esz = 4 if dtype == F32 else 2
        off = (self.off + 31) // 32 * 32
        ap = self.view(off, shape, dtype)
        self.off = off + n * esz
        return ap

    def bank(self):
        while True:
            b = self.bank_rr
            self.bank_rr = (b + 1) % self.nbank_rr
            if b not in self.live_banks:
                return b

    def tmp(self):
        i = self.tmp_rr
        self.tmp_rr = (i + 1) % 4
        return i

    def act(self, out, in_, func, reads, writes, **kw):
        self.P.op('act', lambda e: e.activation(out=out, in_=in_, func=func, **kw), reads, writes)

    def dve(self, fn, reads, writes):
        self.P.op('dve', fn, reads, writes)

    def mm(self, out, lhsT, rhs, start, stop, reads, writes, inc=None):
        self.P.op('pe', lambda e: e.matmul(out, lhsT=lhsT, rhs=rhs, start=start, stop=stop),
                  reads, writes, inc=(stop if inc is None else inc))

    def tr(self, out, in_, ident, reads, writes, inc=True):
        self.P.op('pe', lambda e: e.transpose(out=out, in_=in_, identity=ident), reads, writes, inc=inc)

    def wload(self, src_ap, rows=8, cols=256, slot=None):
        if slot is None:
            i = self.wf_rr
            self.wf_rr = (i + 1) % 8
        else:
            i = slot
        dst = self.WF[i][:, 0:rows * cols].rearrange("p (a b) -> p a b", a=rows)
        self.P.op('pool', lambda e: e.dma_start(out=dst, in_=src_ap), reads=(), writes=[('wf', i)], dma='wf%d' % i)
        return (i, dst)

    def vcol(self, col, n=1):
        return self.VEC[:, col:col + n]

    def constants(self):
        P = self.P
        P.op('sp', lambda e: e.dma_start(out=self.VEC, in_=self.dr['vecs']), writes=['vec'], dma='c0')
        P.op('sp', lambda e: e.dma_start(out=self.IDF, in_=self.dr['ident']), writes=['idf'], dma='c1')
        self.dve(lambda e: e.tensor_copy(out=self.IDB, in_=self.IDF), ['idf'], ['idb'])
        self.dve(lambda e: e.memset(self.ONES, 1.0), [], ['ones'])
        self.dve(lambda e: e.memset(self.ONESF, 1.0), [], ['onesf'])
        self.dve(lambda e: e.memset(self.EPS_LN, LN_EPS), [], ['eps'])
        self.dve(lambda e: e.memset(self.EPS_RMS, RMS_EPS), [], ['eps2'])
        cond = self.VEC[:, CO_COND:CO_COND + 16].rearrange("p (r c) -> p c r", r=2)
        self.act(self.SC, cond, AF.Silu, ['vec'], ['sc'])

    def load_x(self, hook=None):
        xin = [self.view(self.stage_off - 8192 + i * 4096, (D,), F32) for i in range(4)]
        for tt in range(T // 128):
            if hook is not None and tt in (1, 4, 7, 10):
                hook()
            s = tt % 4
            src = self.dr['x'][tt * 128:(tt + 1) * 128, :]
            dst = xin[s]
            self.P.op('sp', lambda e, dst=dst, src=src: e.dma_start(out=dst, in_=src), writes=[('xin', s)], dma='xin%d' % s)
            for half in range(2):
                b = self.bank()
                for q in range(4):
                    c = half * 4 + q
                    self.tr(self.ps[b][:, q * 128:(q + 1) * 128], xin[s][:, c * 128:(c + 1) * 128], self.IDF,
                            [('xin', s), 'idf'], [('ps', b)], inc=(q == 3))
                src_ps = self.ps[b][:, :].rearrange("p (q t) -> p q t", q=4)
                out = self.X[:, half * 4:half * 4 + 4, tt * 128:(tt + 1) * 128]
                wr = [('X', half * 4 + q, tt // 4) for q in range(4)]
                if half == 0:
                    self.act(out, src_ps, AF.Copy, [('ps', b)], [('ps', b)] + wr)
                else:
                    self.dve(lambda e, out=out, src_ps=src_ps: e.tensor_copy(out=out, in_=src_ps), [], [('ps', b)] + wr)

    def ada_units(self, units):
        b = 7
        for (l, u) in units:
            i = self.wa_rr
            self.wa_rr = (i + 1) % 2
            src = self.dr['w_ada'][l, :, u * 256:(u + 1) * 256].rearrange("(k p) n -> p k n", p=128)
            dst = self.WA[i]
            self.P.op('pool', lambda e, dst=dst, src=src: e.dma_start(out=dst, in_=src), writes=[('wa', i)], dma='wa%d' % i)
            for fl in range(2):
                fc = u * 2 + fl
                for k in range(KC):
                    self.mm(self.ps[b][:, fc * 2:fc * 2 + 2], self.WA[i][:, k, fl * 128:(fl + 1) * 128], self.SC[:, k, :],
                            k == 0, k == KC - 1, [('wa', i), 'sc'], [('ps', b)])
        for (l, u) in units:
            m = (u * 2) // 8
            for r in range(2):
                out = self.ADAT[:, l, u * 2:u * 2 + 2, r]
                in0 = self.ps[b][:, u * 4:u * 4 + 4].rearrange("p (f r) -> p f r", r=2)[:, :, r]
                in1 = self.VEC[:, CO_BADA + l * 72 + u * 2: CO_BADA + l * 72 + u * 2 + 2]
                self.dve(lambda e, out=out, in0=in0, in1=in1: e.tensor_tensor(out=out, in0=in0, in1=in1, op=ALU.add),
                         ['vec'], [('ps', b), ('ada', l, u)])
            out = self.ADAT[:, l, u * 2:u * 2 + 2, :]
            if m in (1, 4, 7):
                self.dve(lambda e, out=out: e.tensor_scalar_add(out=out, in0=out, scalar1=1.0), [], [('ada', l, u)])
            elif m in (2, 8):
                self.dve(lambda e, out=out: e.tensor_scalar_mul(out=out, in0=out, scalar1=0.5), [], [('ada', l, u)])

    def ada_res(self, l, m):
        return [('ada', l, m * 4 + j) for j in range(4)]

    def mod(self, l, m, c, r):
        return self.ADAT[:, l, m * 8 + c, r:r + 1]

    @staticmethod
    def cond_of(tb):
        return 1 if tb < 2 else 0

    def modulate(self, l, m_shift, m_scale, tbs=None):
        rd_ada = self.ada_res(l, m_shift) + self.ada_res(l, m_scale)
        for tb in (range(NTB) if tbs is None else tbs):
            r = self.cond_of(tb)
            for c in range(KC):
                out = self.H[:, c, tb * TB:(tb + 1) * TB]
                in_ = self.X[:, c, tb * TB:(tb + 1) * TB]
                sc = self.mod(l, m_scale, c, r)
                sh = self.mod(l, m_shift, c, r)
                if c % 2 == 0:
                    self.act(out, in_, AF.Identity, [('X', c, tb)] + rd_ada, [('H', c, tb)], scale=sc, bias=sh)
                else:
                    self.dve(lambda e, out=out, in_=in_, sc=sc, sh=sh: e.tensor_scalar(
                        out=out, in0=in_, scalar1=sc, scalar2=sh, op0=ALU.mult, op1=ALU.add),
                        [('X', c, tb)] + rd_ada, [('H', c, tb)])

    def _ln_args(self, tb, src, skey, dst, dkey, func):
        sl = slice(tb * TB, (tb + 1) * TB)
        if src is None:
            src = lambda c: self.X[:, c, sl]
            skey = lambda c: ('X', c, tb)
        if dst is None:
            dst, dkey = src, skey
        if func is None:
            func = AF.Identity
        return src, skey, dst, dkey, func

    def ln_stats(self, tb, nch=KC, src=None, skey=None):
        src, skey, _, _, _ = self._ln_args(tb, src, skey, None, None, None)
        bs, bq = self.bank(), self.bank()
        for c in range(nch):
            i = (self.xb_rr // 2 * 2) % 4
            self.xb_rr = (i + 2) % 4
            xq = self.XB[i + 1]
            self.act(xq, src(c), AF.Square, [skey(c)], [('xb', i + 1)])
            self.mm(self.ps[bs], self.ONESF, src(c), c == 0, c == nch - 1, ['onesf', skey(c)], [('ps', bs)], inc=True)
            self.mm(self.ps[bq], self.ONES, xq, c == 0, c == nch - 1, ['ones', ('xb', i + 1)], [('ps', bq)], inc=True)
        return bs, bq

    def ln_apply(self, tb, banks, g_col, b_col, nch=KC, src=None, skey=None, dst=None, dkey=None, func=None):
        src, skey, dst, dkey, func = self._ln_args(tb, src, skey, dst, dkey, func)
        bs, bq = banks
        mean, msq, var, rstd = self.ST
        n = float(nch * 128)
        self.dve(lambda e: e.tensor_scalar_mul(out=mean, in0=self.ps[bs], scalar1=1.0 / n), [], [('ps', bs), ('st', 0)])
        self.dve(lambda e: e.tensor_tensor(out=msq, in0=mean, in1=mean, op=ALU.mult), [('st', 0)], [('st', 1)])
        self.dve(lambda e: e.scalar_tensor_tensor(out=var, in0=self.ps[bq], scalar=1.0 / n, in1=msq,
                                                 op0=ALU.mult, op1=ALU.subtract), [('st', 1)], [('ps', bq), ('st', 2)])
        self.act(var, var, AF.Ln, ['eps'], [('st', 2)], bias=self.EPS_LN, scale=1.0)
        self.act(rstd, var, AF.Exp, [('st', 2)], [('st', 3)], scale=-0.5)
        for c in range(nch):
            xs = src(c)
            t = self.tmp()
            tt_ = self.TMP[t]
            self.dve(lambda e, xs=xs, tt_=tt_: e.tensor_tensor(out=tt_, in0=xs, in1=mean, op=ALU.subtract),
                     [skey(c), ('st', 0)], [('tmp', t)])
            self.dve(lambda e, tt_=tt_: e.tensor_tensor(out=tt_, in0=tt_, in1=rstd, op=ALU.mult),
                     [('st', 3)], [('tmp', t)])
            self.act(dst(c), tt_, func, [('tmp', t), 'vec'], [dkey(c)],
                     scale=self.vcol(g_col + c), bias=self.vcol(b_col + c))

    def layernorm_all(self, g_col, b_col, post=None, nch=KC, mk=None):
        kw = [(mk(tb) if mk is not None else {}) for tb in range(NTB)]
        banks = {}

        def stats(tb):
            banks[tb] = self.ln_stats(tb, nch=nch, src=kw[tb].get('src'), skey=kw[tb].get('skey'))
            self.live_banks.update(banks[tb])

        def apply(tb):
            self.ln_apply(tb, banks[tb], g_col, b_col, nch=nch, **kw[tb])
            self.live_banks.difference_update(banks[tb])
            if post is not None:
                post(tb)
        stats(0); stats(1); apply(0); stats(2); apply(1); apply(2)

    def ffn(self, l, f, m0, ln_idx, ada_hook=None, parts=3, premod=False, post=None):
        if not premod:
            self.modulate(l, m0, m0 + 1)
        w1 = self.dr['ffn_w1'][l, f]
        w3 = self.dr['ffn_w3'][l, f]
        w2 = self.dr['ffn_w2'][l, f]
        loads = []
        for ng in range(NJ // 2):
            loads.append((w1[:, ng * 256:(ng + 1) * 256].rearrange("(k p) n -> p k n", p=128), 8, 256))
            loads.append((w3[:, ng * 256:(ng + 1) * 256].rearrange("(k p) n -> p k n", p=128), 8, 256))
        for c in range(KC):
            loads.append((w2[0:11 * 128, c * 128:(c + 1) * 128].rearrange("(j p) n -> p j n", p=128), 11, 128))
            loads.append((w2[11 * 128:22 * 128, c * 128:(c + 1) * 128].rearrange("(j p) n -> p j n", p=128), 11, 128))
        slots = []

        def issue(upto):
            while len(slots) <= min(upto, len(loads) - 1):
                a, rws, cls = loads[len(slots)]
                slots.append(self.wload(a, rows=rws, cols=cls))
        def up_unit(ng, jl, tb):
            s1, s3 = slots[2 * ng], slots[2 * ng + 1]
            j = ng * 2 + jl
            sl = slice(tb * TB, (tb + 1) * TB)
            ba, bb = self.bank(), self.bank()
            for (bk, sw) in ((ba, s1), (bb, s3)):
                for k in range(KC):
                    self.mm(self.ps[bk], sw[1][:, k, jl * 128:(jl + 1) * 128], self.H[:, k, sl],
                            k == 0, k == KC - 1, [('wf', sw[0]), ('H', k, tb)], [('ps', bk)])
            t = self.tmp()
            s_ = self.TMP[t]
            self.act(s_, self.ps[ba], AF.Silu, [], [('ps', ba), ('tmp', t)])
            out = self.G[:, j, sl]
            self.dve(lambda e, out=out, s_=s_, bb=bb: e.tensor_tensor(out=out, in0=self.ps[bb], in1=s_, op=ALU.mult),
                     [('tmp', t)], [('ps', bb), ('G', j, tb)])
        NFIRST = 2
        issue(min(2 * (NFIRST + 1) + 1, 21 if parts < 2 else 99))
        for tb in range(NTB):
            for ng in range(NFIRST):
                for jl in range(2):
                    up_unit(ng, jl, tb)
        if ada_hook is not None:
            for _ in range(NFIRST):
                ada_hook()
        for ng in range(NFIRST, NJ // 2):
            issue(min(2 * (ng + 2) + 1, 21 if parts < 2 else 99))
            for jl in range(2):
                for tb in range(NTB):
                    up_unit(ng, jl, tb)
            if ada_hook is not None:
                ada_hook()
        if parts < 2:
            return
        rd_gate = self.ada_res(l, m0 + 2)
        for c in range(KC):
            issue(22 + 2 * (c + 2) + 1)
            sa, sb = slots[22 + 2 * c], slots[22 + 2 * c + 1]
            for tb in range(NTB):
                sl = slice(tb * TB, (tb + 1) * TB)
                r = self.cond_of(tb)
                b = self.bank()
                for j in range(NJ):
                    sw = sa if j < 11 else sb
                    self.mm(self.ps[b], sw[1][:, j % 11, :], self.G[:, j, sl],
                            j == 0, j == NJ - 1, [('wf', sw[0]), ('G', j, tb)], [('ps', b)])
                t = self.tmp()
                y_ = self.TMP[t]
                self.act(y_, self.ps[b], AF.Identity, rd_gate, [('ps', b), ('tmp', t)], scale=self.mod(l, m0 + 2, c, r))
                xs = self.X[:, c, sl]
                self.dve(lambda e, xs=xs, y_=y_: e.scalar_tensor_tensor(out=xs, in0=xs, scalar=ALPHA, in1=y_,
                                                                       op0=ALU.mult, op1=ALU.add),
                         [('tmp', t)], [('X', c, tb)])
            if ada_hook is not None:
                ada_hook()
        if parts < 3:
            return
        self.layernorm_all(CO_LNG + ln_idx * 8, CO_LNB + ln_idx * 8, post=post)

    def resid_ln(self, l, m_gate, ps_b, c, tb, rd_gate):
        sl = slice(tb * TB, (tb + 1) * TB)
        r = self.cond_of(tb)
        t = self.tmp()
        y_ = self.TMP[t]
        self.act(y_, self.ps[ps_b], AF.Identity, rd_gate, [('ps', ps_b), ('tmp', t)], scale=self.mod(l, m_gate, c, r))
        xs = self.X[:, c, sl]
        self.dve(lambda e, xs=xs, y_=y_: e.scalar_tensor_tensor(out=xs, in0=xs, scalar=ALPHA, in1=y_,
                                                               op0=ALU.mult, op1=ALU.add),
                 [('tmp', t)], [('X', c, tb)])

    def mixer0(self, l=0, premod=False, post=None):
        if not premod:
            self.modulate(l, 3, 4)
        go = self.g_off
        CB = self.view(go, (4, 6, 286), BF16)
        DT = self.view(go + 13824, (4, T), BF16)
        CO = self.view(go + 26112, (4, T), F32)
        DG = [self.view(go + 50688 + i * 7936, (31, 128), BF16) for i in range(2)]
        PA = self.view(go + 26112, (2, 6, 271), F32)
        PBf = self.view(go + 26112 + 13024, (2, 6, 271), F32)
        HP = self.view(go + 26112 + 26048, (6, 256), F32)
        AB = self.H
        flag = self.vcol(CO_FLAG)
        w_in = self.dr['cp_w_in'][0]

        def win(c0):
            return self.wload(w_in[:, c0:c0 + 256].rearrange("(k p) n -> p k n", p=128))
        self.dve(lambda e: e.memset(CB, 0.0), [], ['CB'])
        for pair in range(2):
            su = win(pair * 256)
            sg = win(512 + pair * 256)
            for il in range(2):
                i = pair * 2 + il
                for tb in range(NTB):
                    sl = slice(tb * TB, (tb + 1) * TB)
                    bu, bg = self.bank(), self.bank()
                    for (bk, sw) in ((bu, su), (bg, sg)):
                        for k in range(KC):
                            self.mm(self.ps[bk], sw[1][:, k, il * 128:(il + 1) * 128], self.H[:, k, sl],
                                    k == 0, k == KC - 1, [('wf', sw[0]), ('H', k, tb)], [('ps', bk)])
                    t = self.tmp()
                    s_ = self.TMP[t]
                    self.act(s_, self.ps[bg], AF.Sigmoid, [], [('ps', bg), ('tmp', t)])
                    out = CB[:, i, 2 * tb:2 * tb + 2, 15:271]
                    in0 = self.ps[bu].rearrange("p (s u) -> p s u", s=2)
                    in1 = s_.rearrange("p (s u) -> p s u", s=2)
                    self.dve(lambda e, out=out, in0=in0, in1=in1: e.tensor_tensor(out=out, in0=in0, in1=in1, op=ALU.mult),
                             [('tmp', t), 'CB'], [('ps', bu), ('CBc', i, tb)])
        for i in range(4):
            cbk = [('CBc', i, tb) for tb in range(NTB)]
            o1, i1 = CB[:, i, 1:4, 0:15], CB[:, i, 0:3, 256:271]
            o2, i2 = CB[:, i, 0:3, 271:286], CB[:, i, 1:4, 15:30]
            self.dve(lambda e, o1=o1, i1=i1: e.tensor_scalar_mul(out=o1, in0=i1, scalar1=flag), cbk + ['vec'], [('CBh', i, 0)])
            self.dve(lambda e, o2=o2, i2=i2: e.tensor_scalar_mul(out=o2, in0=i2, scalar1=flag), cbk + ['vec'], [('CBh', i, 1)])
        spw = self.wload(self.dr['pool_w'][0].rearrange("g p n -> p g n"), rows=4, cols=128)
        for pair in range(2):
            sh = win(1024 + pair * 256)
            for il in range(2):
                gi = pair * 2 + il
                self.dve(lambda e: e.memset(PA, 0.0), [], ['PA'])
                self.dve(lambda e: e.memset(PA[:, 1, :, 8:264], 1.0), [], ['PA'])
                for tb in range(NTB):
                    sl = slice(tb * TB, (tb + 1) * TB)
                    b = self.bank()
                    for k in range(KC):
                        self.mm(self.ps[b], sh[1][:, k, il * 128:(il + 1) * 128], self.H[:, k, sl],
                                k == 0, k == KC - 1, [('wf', sh[0]), ('H', k, tb)], [('ps', b)])
                    src = self.ps[b].rearrange("p (s u) -> p s u", s=2)
                    self.act(PA[:, 0, 2 * tb:2 * tb + 2, 8:264], src, AF.Copy, ['PA'], [('ps', b), ('PAc', tb)])
                    self.act(HP[:, 2 * tb:2 * tb + 2, :], src, AF.Copy, [], [('ps', b), ('HP', tb)])
                pak = [('PAc', tb) for tb in range(NTB)]
                self.dve(lambda e: e.tensor_scalar_mul(out=PA[:, :, 1:4, 0:8], in0=PA[:, :, 0:3, 256:264], scalar1=flag),
                         pak + ['PA', 'vec'], ['PAh0'])
                self.dve(lambda e: e.tensor_scalar_mul(out=PA[:, :, 0:3, 264:271], in0=PA[:, :, 1:4, 8:15], scalar1=flag),
                         pak + ['PA', 'vec'], ['PAh1'])
                steps = [(PBf, PA, 1, 271, 0, 1), (PA, PBf, 2, 270, 1, 3), (PBf, PA, 4, 268, 2, 6), (PA, PBf, 8, 264, 4, 12)]
                cur = 'PAfull'
                prev_keys = pak + ['PA', 'PAh0', 'PAh1']
                for si in range(gi + 1):
                    dst, srcb, lo, hi, a0, a1 = steps[si]
                    n = hi - lo
                    o = dst[:, :, :, lo:hi]
                    x0 = srcb[:, :, :, a0:a0 + n]
                    x1 = srcb[:, :, :, a1:a1 + n]
                    key = ('PS', si % 2)
                    self.dve(lambda e, o=o, x0=x0, x1=x1: e.tensor_tensor(out=o, in0=x0, in1=x1, op=ALU.add),
                             prev_keys, [key])
                    prev_keys = [key]
                fin = steps[gi][0]
                rc = fin[:, 1, :, 8:264]
                sd = fin[:, 0, :, 8:264]
                w_ = float(2 ** (gi + 1))
                rcl, rcr, rcm = fin[:, 1, :, 8:16], fin[:, 1, :, 256:264], fin[:, 1, :, 16:256]
                self.dve(lambda e, rcl=rcl: e.reciprocal(out=rcl, in_=rcl), prev_keys, [('PS', gi % 2)])
                self.dve(lambda e, rcr=rcr: e.reciprocal(out=rcr, in_=rcr), [], [('PS', gi % 2)])
                self.dve(lambda e, rcm=rcm, w_=w_: e.memset(rcm, 1.0 / w_), [], [('PS', gi % 2)])
                self.dve(lambda e, sd=sd, rc=rc: e.tensor_tensor(out=sd, in0=sd, in1=rc, op=ALU.mult),
                         [], [('PS', gi % 2)])
                dt = DT[:, gi, :].rearrange("p (s u) -> p s u", s=6)
                self.dve(lambda e, dt=dt, sd=sd: e.tensor_tensor(out=dt, in0=sd, in1=HP, op=ALU.subtract),
                         [('PS', gi % 2)] + [('HP', tb) for tb in range(NTB)], [('DT', gi)])
                self.P.res['PA'] = [self.P.res[('DT', gi)][0], {}]
                for tb in range(NTB):
                    sl = slice(tb * TB, (tb + 1) * TB)
                    b = self.bank()
                    self.mm(self.ps[b], spw[1][:, gi, :], DT[:, gi, sl], True, True, [('wf', spw[0]), ('DT', gi)], [('ps', b)])
                    self.P.res.setdefault(('DTB', gi, tb), [None, {}])[1]['pe'] = self.P.cnt['pe']
                    self.act(DT[:, gi, sl], self.ps[b], AF.Identity, ['vec'], [('ps', b), ('DTB', gi, tb)],
                             scale=self.vcol(CO_PSC + gi))
        for i in range(4):
            dg = DG[i % 2]
            for k in range(31):
                o = dg[:, k, :]
                wc = self.vcol(CO_CW + i * 31 + k)
                self.dve(lambda e, o=o, wc=wc: e.tensor_scalar_mul(out=o, in0=self.IDB, scalar1=wc),
                         ['idb', 'vec'], [('DG', i % 2)])
            for tb in range(NTB):
                sl = slice(tb * TB, (tb + 1) * TB)
                b = self.bank()
                for k in range(31):
                    self.mm(self.ps[b], dg[:, k, :], CB[:, i, 2 * tb:2 * tb + 2, k:k + 256], k == 0, k == 30,
                            [('DG', i % 2), ('CBh', i, 0), ('CBh', i, 1)] + [('CBc', i, t2) for t2 in range(NTB)], [('ps', b)])
                self.act(CO[:, i, sl], self.ps[b], AF.Identity, ['vec'], [('ps', b), ('CO', i, tb)],
                         bias=self.vcol(CO_CB + i))
        def conv_ln_args(tb):
            sl = slice(tb * TB, (tb + 1) * TB)
            return dict(src=lambda c, sl=sl: CO[:, c, sl], skey=lambda c, tb=tb: ('CO', c, tb),
                        dst=lambda c, sl=sl: AB[:, c, sl], dkey=lambda c, tb=tb: ('H', c, tb), func=AF.Silu)
        self.layernorm_all(CO_CNG, CO_CNB, nch=4, mk=conv_ln_args)
        w_out = self.dr['cp_w_out'][0]
        rd_gate = self.ada_res(l, 5)
        for pair in range(4):
            so = self.wload(w_out[:, pair * 256:(pair + 1) * 256].rearrange("(k p) n -> p k n", p=128))
            for il in range(2):
                c = pair * 2 + il
                for tb in range(NTB):
                    sl = slice(tb * TB, (tb + 1) * TB)
                    b = self.bank()
                    for j in range(KC):
                        rhs = AB[:, j, sl] if j < 4 else DT[:, j - 4, sl]
                        rk = ('H', j, tb) if j < 4 else ('DTB', j - 4, tb)
                        self.mm(self.ps[b], so[1][:, j, il * 128:(il + 1) * 128], rhs, j == 0, j == KC - 1,
                                [('wf', so[0]), rk], [('ps', b)])
                    self.resid_ln(l, 5, b, c, tb, rd_gate)
        self.layernorm_all(CO_LNG + (l * 3 + 1) * 8, CO_LNB + (l * 3 + 1) * 8, post=post)

    def rms_block(self, tb, nch, wslices, g_col, dst_f32=None, dst_bf=None, key=None, alt=0):
        sl = slice(tb * TB, (tb + 1) * TB)
        QD = self.mla_bufs['QD'][alt]
        qk = 'QD%d' % alt
        bq = self.bank()
        for c in range(nch):
            b = self.bank()
            slot, lf = wslices[c]
            for k in range(KC):
                self.mm(self.ps[b], lf(k), self.H[:, k, sl], k == 0, k == KC - 1, [('wf', slot), ('H', k, tb)], [('ps', b)])
            self.act(QD[:, c, :], self.ps[b], AF.Copy, [], [('ps', b), (qk, c)])
            i = self.xb_rr
            self.xb_rr = (i + 1) % 4
            self.act(self.XB[i], QD[:, c, :], AF.Square, [(qk, c)], [('xb', i)])
            self.mm(self.ps[bq], self.ONES, self.XB[i], c == 0, c == nch - 1, ['ones', ('xb', i)], [('ps', bq)])
        var, rstd = (self.ST[2], self.ST[3]) if alt == 0 else (self.ST[0], self.ST[1])
        k2, k3 = (('st', 2), ('st', 3)) if alt == 0 else (('st', 0), ('st', 1))
        self.act(var, self.ps[bq], AF.Sqrt, ['eps2'], [('ps', bq), k2], bias=self.EPS_RMS, scale=1.0 / (nch * 128))
        self.dve(lambda e: e.reciprocal(out=rstd, in_=var), [k2], [k3])
        for c in range(nch):
            qd = QD[:, c, :]
            g = self.vcol(g_col + c)
            if dst_f32 is not None:
                o = dst_f32[:, c, sl]
                self.dve(lambda e, o=o, qd=qd, g=g: e.scalar_tensor_tensor(out=o, in0=qd, scalar=g, in1=rstd,
                                                                          op0=ALU.mult, op1=ALU.mult),
                         [(qk, c), k3, 'vec'], [(key + 'f', c, tb)])
                ob = dst_bf[:, c, sl]
                self.act(ob, o, AF.Copy, [(key + 'f', c, tb)], [(key, c, tb)])
            else:
                ob = dst_bf[:, c, sl]
                self.dve(lambda e, ob=ob, qd=qd, g=g: e.scalar_tensor_tensor(out=ob, in0=qd, scalar=g, in1=rstd,
                                                                            op0=ALU.mult, op1=ALU.mult),
                         [(qk, c), k3, 'vec'], [(key, c, tb)])

    def mla(self, l=1, premod=False, post=None):
        if not premod:
            self.modulate(l, 3, 4)
        self.wf_rr = 0
        self.nbank_rr = 8
        go = self.g_off
        QN = self.view(go, (3, T), BF16)
        CKVb = self.view(go + 9216, (2, 2048), BF16)
        KRb = self.view(go + 17408, (2048,), BF16)
        KT = [self.view(go + 21504 + i * 4096, (2048,), BF16) for i in range(2)]
        VH = [self.view(go + 29696 + i * 4096, (16, 128), BF16) for i in range(2)]
        QT = self.view(go + 37888, (T,), BF16)
        QR = [self.view(go + 40960 + i * 3072, (T,), BF16) for i in range(2)]
        PP = [self.view(go + 47104 + i * 3072, (T,), BF16) for i in range(2)]
        PT = self.view(go + 53248, (12, 512), BF16)
        RB = self.view(go + 65536, (512,), F32)
        CKVf = self.view(go + 21504, (2, T), F32)
        KRraw = self.view(go + 33792, (T,), F32)
        QDall = self.view(go + 39936, (3, 3, TB), F32)
        OST = [self.view(go + 58368 + i * 1280, (320,), F32) for i in range(2)]
        CST = [self.view(go + 60928 + i * 1280, (320,), F32) for i in range(4)]
        ROC = self.WA[0].rearrange("p a b -> p (a b)").bitcast(F32)
        ROS = self.WA[1].rearrange("p a b -> p (a b)").bitcast(F32)
        OT = self.H
        P = self.P
        P.op('sp', lambda e: e.dma_start(out=ROC[0:64, :], in_=self.dr['ropeC']), writes=[('wa', 0)], dma='c0')
        P.op('sp', lambda e: e.dma_start(out=ROS[0:64, :], in_=self.dr['ropeS']), writes=[('wa', 1)], dma='c1')
        P.op('pool', lambda e: e.dma_start(out=KRb[64:68, 0:1024], in_=self.dr['maskk'][:, 0:1024]), writes=['KRm0'], dma='m0')
        P.op('pool', lambda e: e.dma_start(out=KRb[64:68, 1536:2048], in_=self.dr['maskk'][:, 1024:1536]), writes=['KRm1'], dma='m1')
        for ct in range(4):
            P.op('sp', lambda e, ct=ct: e.dma_start(out=CST[ct][:, 0:256], in_=self.dr['cache_ckv'][ct * 128:(ct + 1) * 128, :]),
                 writes=[('cst', ct, 0)], dma='cs%d' % ct)
            P.op('sp', lambda e, ct=ct: e.dma_start(out=CST[ct][:, 256:320], in_=self.dr['cache_kr'][ct * 128:(ct + 1) * 128, :]),
                 writes=[('cst', ct, 1)], dma='ck%d' % ct)
        for ct in range(4):
            b = self.bank()
            for c in range(2):
                self.tr(self.ps[b][:, c * 128:(c + 1) * 128], CST[ct][:, c * 128:(c + 1) * 128], self.IDF,
                        [('cst', ct, 0), 'idf'], [('ps', b)], inc=False)
            self.tr(self.ps[b][0:64, 256:384], CST[ct][:, 256:320], self.IDF, [('cst', ct, 1), 'idf'], [('ps', b)])
            ks = slice(1536 + ct * 128, 1536 + (ct + 1) * 128)
            self.act(CKVb[:, :, ks], self.ps[b][:, 0:256].rearrange("p (c t) -> p c t", c=2), AF.Copy, [],
                     [('ps', b), ('CKVb', 3)])
            self.dve(lambda e, ks=ks, b=b: e.tensor_copy(out=KRb[0:64, ks], in_=self.ps[b][0:64, 256:384]), [],
                     [('ps', b), ('KRb', 3)])
        wdq = self.dr['mla_w_dq'][0]
        wdkv = self.dr['mla_w_dkv'][0]
        sq0 = self.wload(wdq[:, 0:256].rearrange("(k p) n -> p k n", p=128))
        sq1 = self.wload(wdq[:, 256:384].rearrange("(k p) n -> p k n", p=128), rows=8, cols=128)
        sk0 = self.wload(wdkv[:, 0:256].rearrange("(k p) n -> p k n", p=128))
        sk1 = self.wload(wdkv[:, 256:320].rearrange("(k p) n -> p k n", p=128), rows=8, cols=64)
        sk2 = self.wload(self.dr['w_dkv_sw'].rearrange("(k p) n -> p k n", p=128), rows=8, cols=64)
        qsl = [(sq0[0], lambda k: sq0[1][:, k, 0:128]), (sq0[0], lambda k: sq0[1][:, k, 128:256]), (sq1[0], lambda k: sq1[1][:, k, :])]
        ksl = [(sk0[0], lambda k: sk0[1][:, k, 0:128]), (sk0[0], lambda k: sk0[1][:, k, 128:256])]
        pp_ = [6]

        def pbank():
            pp_[0] = 13 - pp_[0]
            return pp_[0]
        for tb in range(NTB):
            sl = slice(tb * TB, (tb + 1) * TB)
            for (nch, wsl, raw, rk, sbank) in ((3, qsl, lambda c: QDall[:, tb, c, :], 'QDr', tb),
                                               (2, ksl, lambda c: CKVf[:, c, sl], 'CKVr', 3 + tb)):
                for c in range(nch):
                    b = pbank()
                    slot, lf = wsl[c]
                    for k in range(KC):
                        self.mm(self.ps[b], lf(k), self.H[:, k, sl], k == 0, k == KC - 1, [('wf', slot), ('H', k, tb)], [('ps', b)])
                    self.act(raw(c), self.ps[b], AF.Copy, [], [('ps', b), (rk, c, tb)])
                    i = self.xb_rr
                    self.xb_rr = (i + 1) % 4
                    self.act(self.XB[i], raw(c), AF.Square, [(rk, c, tb)], [('xb', i)])
                    self.mm(self.ps[sbank], self.ONES, self.XB[i], c == 0, c == nch - 1, ['ones', ('xb', i)], [('ps', sbank)], inc=True)
            br = pbank()
            for k in range(KC):
                self.mm(self.ps[br][0:64, :], sk1[1][:, k, :], self.H[:, k, sl], k == 0, k == KC - 1,
                        [('wf', sk1[0]), ('H', k, tb)], [('ps', br)])
            self.act(KRraw[0:64, sl], self.ps[br][0:64, :], AF.Copy, [], [('ps', br), ('KRraw', tb)])
            if tb < 2:
                bs = pbank()
                for k in range(KC):
                    self.mm(self.ps[bs][0:64, :], sk2[1][:, k, :], self.H[:, k, sl], k == 0, k == KC - 1,
                            [('wf', sk2[0]), ('H', k, tb)], [('ps', bs)])
                t1, t2 = self.tmp(), self.tmp()
                a1, a2 = self.TMP[t1][0:64, :], self.TMP[t2][0:64, :]
                self.dve(lambda e, a1=a1, sl=sl: e.tensor_tensor(out=a1, in0=KRraw[0:64, sl], in1=ROC[0:64, sl], op=ALU.mult),
                         [('KRraw', tb), ('wa', 0)], [('tmp', t1)])
                self.dve(lambda e, a2=a2, sl=sl, bs=bs: e.tensor_tensor(out=a2, in0=self.ps[bs][0:64, :], in1=ROS[0:64, sl], op=ALU.mult),
                         [('wa', 1)], [('ps', bs), ('tmp', t2)])
                self.dve(lambda e, a1=a1, a2=a2, sl=sl: e.tensor_tensor(out=KRb[0:64, sl], in0=a1, in1=a2, op=ALU.add),
                         [('tmp', t1), ('tmp', t2)], [('KRb', tb)])
            else:
                self.dve(lambda e, sl=sl: e.tensor_copy(out=KRb[0:64, sl], in_=KRraw[0:64, sl]), [('KRraw', tb)], [('KRb', tb)])
        for tb in range(NTB):
            sl = slice(tb * TB, (tb + 1) * TB)
            for (nch, g_col, raw, rk, sbank, vi) in ((3, CO_QG, lambda c: QDall[:, tb, c, :], 'QDr', tb, 0),
                                                     (2, CO_KVG, lambda c: CKVf[:, c, sl], 'CKVr', 3 + tb, 2)):
                var, rstd = self.ST[vi], self.ST[vi + 1]
                kv_, kr_ = ('st', vi), ('st', vi + 1)
                self.act(var, self.ps[sbank], AF.Identity, ['eps2'], [('ps', sbank), kv_], bias=self.EPS_RMS, scale=1.0 / (nch * 128))
                self.act(var, var, AF.Ln, [], [kv_])
                self.act(rstd, var, AF.Exp, [kv_], [kr_], scale=-0.5)
                for c in range(nch):
                    g = self.vcol(g_col + c)
                    x_ = raw(c)
                    if nch == 3:
                        ob = QN[:, c, sl]
                        self.dve(lambda e, ob=ob, x_=x_, g=g, rstd=rstd: e.scalar_tensor_tensor(out=ob, in0=x_, scalar=g, in1=rstd,
                                                                                              op0=ALU.mult, op1=ALU.mult),
                                 [(rk, c, tb), kr_, 'vec'], [('QN', c, tb)])
                    else:
                        self.dve(lambda e, x_=x_, g=g, rstd=rstd: e.scalar_tensor_tensor(out=x_, in0=x_, scalar=g, in1=rstd,
                                                                                        op0=ALU.mult, op1=ALU.mult),
                                 [kr_, 'vec'], [(rk, c, tb), ('CKVf', c, tb)])
                        self.act(CKVb[:, c, sl], x_, AF.Copy, [('CKVf', c, tb)], [('CKV', c, tb)])
        for tt in range(T // 128):
            s = tt % 2
            ts_ = slice(tt * 128, (tt + 1) * 128)
            tb = tt // 4
            b = self.bank()
            for c in range(2):
                self.tr(self.ps[b][:, c * 128:(c + 1) * 128], CKVf[:, c, ts_], self.IDF, [('CKVf', c, tb), 'idf'], [('ps', b)], inc=False)
            self.tr(self.ps[b][:, 256:320], KRraw[0:64, ts_], self.IDF[0:64, 0:64], [('KRraw', tb), 'idf'], [('ps', b)])
            self.act(OST[s], self.ps[b][:, 0:320], AF.Copy, [], [('ps', b), ('ost', s)])
            P.op('sp', lambda e, s=s, ts_=ts_: e.dma_start(out=self.dr['ockv'][ts_, :], in_=OST[s][:, 0:256]), reads=[('ost', s)], dma='oa%d' % s)
            P.op('sp', lambda e, s=s, ts_=ts_: e.dma_start(out=self.dr['okr'][ts_, :], in_=OST[s][:, 256:320]), reads=[('ost', s)], dma='ob%d' % s)
        wuq = self.dr['mla_w_uq'][0]
        wukv = self.dr['mla_w_ukv'][0]
        suq = [self.wload(wuq[:, j * 512:(j + 1) * 512].rearrange("(k p) n -> p k n", p=128), rows=3, cols=512, slot=j) for j in range(3)]
        ssw = self.wload(self.dr['w_uq_sw'].rearrange("(k p) n -> p k n", p=128), rows=3, cols=512, slot=3)
        P.fence(('pool',))
        for i in range(2):
            P.op('pool', lambda e, i=i: e.dma_start(out=QR[i][64:68, 0:1024], in_=self.dr['onehot']), writes=[('QRm', i)], dma='m%d' % (2 + i))

        def uq(col0, width):
            j, o = divmod(col0, 512)
            assert o + width <= 512
            return suq[j][0], (lambda k: suq[j][1][:, k, o:o + width])
        scale = float((128 + 64) ** -0.5)
        groups = [(0, 8, [(0, 512), (512, 512), (1536, 512)], 68),
                  (8, 2, [(1024, 256)], 64), (10, 2, [(1280, 256)], 64)]
        for h in range(8):
            hb = h % 2
            swv = self.wload(wukv[:, h * 256:(h + 1) * 256].rearrange("(k p) n -> p k n", p=128), rows=2, cols=256, slot=4 + hb)
            for kb in range(4):
                b = self.bank()
                ks = slice(kb * 512, (kb + 1) * 512)
                for c in range(2):
                    self.mm(self.ps[b], swv[1][:, c, 0:128], CKVb[:, c, ks], c == 0, c == 1,
                            [('wf', swv[0])] + [('CKVb', i) for i in range(4)] + [('CKV', c, i) for i in range(3)], [('ps', b)])
                if kb % 2 == 0:
                    self.act(KT[hb][:, ks], self.ps[b], AF.Copy, [], [('ps', b), ('KT', hb)])
                else:
                    self.dve(lambda e, hb=hb, ks=ks, b=b: e.tensor_copy(out=KT[hb][:, ks], in_=self.ps[b]), [], [('ps', b), ('KT', hb)])
            for kq in range(4):
                b = self.bank()
                for q in range(4):
                    kt = kq * 4 + q
                    for c in range(2):
                        self.mm(self.ps[b][:, q * 128:(q + 1) * 128], CKVb[:, c, kt * 128:(kt + 1) * 128], swv[1][:, c, 128:256],
                                c == 0, c == 1, [('wf', swv[0])], [('ps', b)])
                src = self.ps[b].rearrange("p (q d) -> p q d", q=4)
                if kq % 2 == 0:
                    self.dve(lambda e, hb=hb, kq=kq, src=src: e.tensor_copy(out=VH[hb][:, kq * 4:(kq + 1) * 4, :], in_=src), [],
                             [('ps', b), ('VH', hb)])
                else:
                    self.act(VH[hb][:, kq * 4:(kq + 1) * 4, :], src, AF.Copy, [], [('ps', b), ('VH', hb)])
            for tb in range(NTB):
                sl = slice(tb * TB, (tb + 1) * TB)
                b = self.bank()
                slot, lf = uq(h * 192, 128) if (h * 192) % 512 + 128 <= 512 else (None, None)
                for c in range(3):
                    if slot is not None:
                        self.mm(self.ps[b], lf(c), QN[:, c, sl], c == 0, c == 2, [('wf', slot), ('QN', c, tb)], [('ps', b)])
                    else:
                        j0, o0 = divmod(h * 192, 512)
                        w0 = 512 - o0
                        self.mm(self.ps[b][0:w0, :], suq[j0][1][:, c, o0:512], QN[:, c, sl], c == 0, c == 2,
                                [('wf', suq[j0][0]), ('QN', c, tb)], [('ps', b)])
                if slot is None:
                    for c in range(3):
                        self.mm(self.ps[b][w0:128, :], suq[j0 + 1][1][:, c, 0:128 - w0], QN[:, c, sl], c == 0, c == 2,
                                [('wf', suq[j0 + 1][0]), ('QN', c, tb)], [('ps', b)])
                self.act(QT[:, sl], self.ps[b], AF.Copy, [], [('ps', b), ('QT', tb)])
                br = self.bank()
                sr, lr = uq(h * 192 + 128, 64)
                for c in range(3):
                    self.mm(self.ps[br][0:64, :], lr(c), QN[:, c, sl], c == 0, c == 2, [('wf', sr), ('QN', c, tb)], [('ps', br)])
                if tb < 2:
                    bsw = self.bank()
                    for c in range(3):
                        self.mm(self.ps[bsw][0:64, :], ssw[1][:, c, h * 64:(h + 1) * 64], QN[:, c, sl], c == 0, c == 2,
                                [('wf', ssw[0]), ('QN', c, tb)], [('ps', bsw)])
                    t1, t2 = self.tmp(), self.tmp()
                    a1, a2 = self.TMP[t1][0:64, :], self.TMP[t2][0:64, :]
                    self.dve(lambda e, a1=a1, sl=sl, br=br: e.tensor_tensor(out=a1, in0=self.ps[br][0:64, :], in1=ROC[0:64, sl], op=ALU.mult),
                             [('wa', 0)], [('ps', br), ('tmp', t1)])
                    self.dve(lambda e, a2=a2, sl=sl, bsw=bsw: e.tensor_tensor(out=a2, in0=self.ps[bsw][0:64, :], in1=ROS[0:64, sl], op=ALU.mult),
                             [('wa', 1)], [('ps', bsw), ('tmp', t2)])
                    self.dve(lambda e, a1=a1, a2=a2, sl=sl, hb=hb: e.tensor_tensor(out=QR[hb][0:64, sl], in0=a1, in1=a2, op=ALU.add),
                             [('tmp', t1), ('tmp', t2)], [('QR', hb, tb)])
                else:
                    self.act(QR[hb][0:64, sl], self.ps[br][0:64, :], AF.Copy, [], [('ps', br), ('QR', hb, tb)])
            items = []
            kbA = [(0, 512), (512, 512), (1536, 512)]
            ktA = [0, 1, 2, 3, 4, 5, 6, 7, 12, 13, 14, 15]
            for qb0 in (0, 4):
                for qi in range(4):
                    items.append(dict(qt=qb0 + qi, qi=qi, nq=4, qb0=qb0, kblocks=kbA, krows=68, ptoff=0, ktiles=ktA,
                                      first=(qi == 0), last=(qi == 3), zero=False))
            for qi in range(4):
                kb = [(1024, 256)] if qi < 2 else [(1280, 256)]
                items.append(dict(qt=8 + qi, qi=qi, nq=4, qb0=8, kblocks=kb, krows=64, ptoff=0 if qi < 2 else 2,
                                  ktiles=[8, 9, 10, 11], first=(qi == 0), last=(qi == 3), zero=(qi == 0)))

            def stage_qk(it):
                qt = it['qt']
                qs = slice(qt * 128, (qt + 1) * 128)
                tbq = qt // 4
                it['banks'] = []
                for bi_, (k0, kw) in enumerate(it['kblocks']):
                    b = (qt % 2) * 3 + bi_
                    it['banks'].append(b)
                    self.mm(self.ps[b][:, 0:kw], QT[:, qs], KT[hb][:, k0:k0 + kw], True, False,
                            [('QT', tbq), ('KT', hb)], [('ps', b)])
                    rdm = ['KRm0', 'KRm1', ('QRm', hb)] if it['krows'] == 68 else []
                    self.mm(self.ps[b][:, 0:kw], QR[hb][0:it['krows'], qs], KRb[0:it['krows'], k0:k0 + kw], False, True,
                            [('QR', hb, tbq)] + [('KRb', i) for i in range(4)] + rdm, [('ps', b)])

            def stage_max(it):
                qt = it['qt']
                par = qt % 2
                so = par * 8
                banks = it['banks']
                nb = len(banks)
                for bi, b in enumerate(banks):
                    kw = it['kblocks'][bi][1]
                    self.dve(lambda e, bi=bi, b=b, kw=kw, so=so: e.reduce_max(out=self.SM[:, so + bi:so + bi + 1], in_=self.ps[b][:, 0:kw],
                                                                           axis=mybir.AxisListType.X), [], [('ps', b), ('sm_mx', par)])
                if nb > 1:
                    self.dve(lambda e, so=so, nb=nb: e.reduce_max(out=self.SM[:, so + 3:so + 4], in_=self.SM[:, so:so + nb],
                                                                  axis=mybir.AxisListType.X), [], [('sm_mx', par)])
                    src_c = so + 3
                else:
                    src_c = so
                self.dve(lambda e, so=so, src_c=src_c: e.tensor_scalar_mul(out=self.SM[:, so + 4:so + 5], in0=self.SM[:, src_c:src_c + 1],
                                                                        scalar1=-scale), [], [('sm_mx', par), ('sm_nb', par)])

            def stage_exp(it):
                qt = it['qt']
                par = qt % 2
                so = par * 8
                pp = PP[par]
                ko = 0
                for bi, b in enumerate(it['banks']):
                    kw = it['kblocks'][bi][1]
                    self.act(pp[:, ko:ko + kw], self.ps[b][:, 0:kw], AF.Exp, [('sm_nb', par)], [('ps', b), ('PP', par, bi)],
                             bias=self.SM[:, so + 4:so + 5], scale=scale)
                    ko += kw

            def stage_transpose(it):
                qt, qi = it['qt'], it['qi']
                pp = PP[qt % 2]
                nkt = sum(w for _, w in it['kblocks']) // 128
                po = it['ptoff']
                if it['zero']:
                    z1, z2 = PT[:, 2:4, 0:256], PT[:, 0:2, 256:512]
                    self.dve(lambda e, z1=z1: e.memset(z1, 0.0), [], [('PT', 0, 0), ('PT', 1, 0)])
                    self.dve(lambda e, z2=z2: e.memset(z2, 0.0), [], [('PT', 2, 0), ('PT', 3, 0)])
                for kq in range(0, nkt, 8):
                    b = 6 + kq // 8
                    pb = self.ps[b].bitcast(BF16)
                    n8 = min(8, nkt - kq)
                    for q in range(n8):
                        kt = kq + q
                        self.tr(pb[:, q * 128:(q + 1) * 128], pp[:, kt * 128:(kt + 1) * 128], self.IDB,
                                [('PP', qt % 2, kt // 4), 'idb'], [('ps', b)], inc=(q == n8 - 1))
                    src = pb[:, 0:n8 * 128].rearrange("p (q t) -> p q t", q=n8)
                    dst = PT[:, po + kq:po + kq + n8, qi * 128:(qi + 1) * 128]
                    if (kq // 8) % 2 == 0:
                        self.act(dst, src, AF.Copy, [], [('ps', b), ('PT', qi, kq // 8)])
                    else:
                        self.dve(lambda e, dst=dst, src=src: e.tensor_copy(out=dst, in_=src), [], [('ps', b), ('PT', qi, kq // 8)])

            def stage_pv(it):
                nq, qb0 = it['nq'], it['qb0']
                nqc = nq * 128
                ktiles = it['ktiles']
                bo, bsum = 6, 7
                rdpt = [('PT', q, g) for q in range(nq) for g in range((len(ktiles) + 7) // 8)]
                for i, ktg in enumerate(ktiles):
                    self.mm(self.ps[bo][:, 0:nqc], VH[hb][:, ktg, :], PT[:, i, 0:nqc], i == 0, i == len(ktiles) - 1,
                            [('VH', hb)] + rdpt, [('ps', bo)])
                for i, ktg in enumerate(ktiles):
                    self.mm(self.ps[bsum][:, 0:nqc], self.ONES, PT[:, i, 0:nqc], i == 0, i == len(ktiles) - 1,
                            ['ones'] + rdpt, [('ps', bsum)])
                self.dve(lambda e, bsum=bsum, nqc=nqc: e.reciprocal(out=RB[:, 0:nqc], in_=self.ps[bsum][:, 0:nqc]), [], [('ps', bsum), 'RB'])
                q0 = qb0 * 128
                tbo = q0 // TB
                dst = OT[:, h, q0:q0 + nqc]
                self.dve(lambda e, dst=dst, bo=bo, nqc=nqc: e.tensor_tensor(out=dst, in0=self.ps[bo][:, 0:nqc], in1=RB[:, 0:nqc], op=ALU.mult),
                         ['RB'], [('ps', bo), ('H', h, tbo)])

            stage_qk(items[0])
            stage_max(items[0])
            for i, it in enumerate(items):
                if i + 1 < len(items):
                    stage_qk(items[i + 1])
                stage_exp(it)
                if i + 1 < len(items):
                    stage_max(items[i + 1])
                stage_transpose(it)
                if it['last']:
                    stage_pv(it)
        if self.debug == 91:
            for c in range(KC):
                for tb in range(NTB):
                    sl = slice(tb * TB, (tb + 1) * TB)
                    self.act(self.X[:, c, sl], OT[:, c, sl], AF.Copy, [('H', c, tb)], [('X', c, tb)])
            return
        w_o = self.dr['mla_w_o'][0]
        rd_gate = self.ada_res(l, 5)
        self.wf_rr = 6
        for pair in range(4):
            so = self.wload(w_o[:, pair * 256:(pair + 1) * 256].rearrange("(k p) n -> p k n", p=128))
            for il in range(2):
                c = pair * 2 + il
                for tb in range(NTB):
                    sl = slice(tb * TB, (tb + 1) * TB)
                    b = self.bank()
                    for j in range(KC):
                        self.mm(self.ps[b], so[1][:, j, il * 128:(il + 1) * 128], OT[:, j, sl], j == 0, j == KC - 1,
                                [('wf', so[0]), ('H', j, tb)], [('ps', b)])
                    self.resid_ln(l, 5, b, c, tb, rd_gate)
        self.layernorm_all(CO_LNG + (l * 3 + 1) * 8, CO_LNB + (l * 3 + 1) * 8, post=post)

    def store_x(self, tbs=None):
        xo = [self.view(self.stage_off, (D,), F32), self.view(self.stage_off + 4096, (D,), F32)]
        tiles = range(T // 128) if tbs is None else [tt for tb in tbs for tt in range(4 * tb, 4 * tb + 4)]
        for tt in tiles:
            s = tt % 2
            for half in range(2):
                b = self.bank()
                for q in range(4):
                    c = half * 4 + q
                    self.tr(self.ps[b][:, q * 128:(q + 1) * 128], self.X[:, c, tt * 128:(tt + 1) * 128], self.IDF,
                            [('X', c, tt // 4), 'idf'], [('ps', b)], inc=(q == 3))
                out = xo[s][:, half * 512:(half + 1) * 512]
                if half == 0:
                    self.act(out, self.ps[b], AF.Copy, [], [('ps', b), ('xo', s, 0)])
                else:
                    self.dve(lambda e, out=out, b=b: e.tensor_copy(out=out, in_=self.ps[b]), [], [('ps', b), ('xo', s, 1)])
            dst = self.dr['y'][tt * 128:(tt + 1) * 128, :]
            src = xo[s]
            self.P.op('sp', lambda e, dst=dst, src=src: e.dma_start(out=dst, in_=src),
                      reads=[('xo', s, 0), ('xo', s, 1)], dma='yo%d' % s)

    def build(self):
        st = self.debug or 99
        self.constants()
        nada = [0]

        def ada_next(n=1):
            while n > 0 and nada[0] < 72:
                k = min(n, 72 - nada[0], 2)
                self.ada_units([((nada[0] + j) // 36, (nada[0] + j) % 36) for j in range(k)])
                nada[0] += k
                n -= k
        self.load_x(hook=lambda: ada_next(2))
        hook = lambda: ada_next(2)
        if st < 99:
            self.ffn(0, 0, 0, 0, ada_hook=hook)
            if st >= 7:
                self.P.fence()
                self.mixer0(0)
                self.P.fence()
            if st >= 8:
                self.ffn(0, 1, 6, 2, ada_hook=hook)
                ada_next(72)
                self.ffn(1, 0, 0, 3)
            if st >= 9:
                self.P.fence(('pe', 'act', 'dve', 'sp', 'pool'))
                self.mla(1)
                self.P.fence()
            if st >= 10 and st != 91:
                self.ffn(1, 1, 6, 5)
        else:
            self.ffn(0, 0, 0, 0, ada_hook=hook, post=lambda tb: self.modulate(0, 3, 4, [tb]))
            self.P.fence()
            self.mixer0(0, premod=True, post=lambda tb: self.modulate(0, 6, 7, [tb]))
            self.P.fence()
            self.ffn(0, 1, 6, 2, ada_hook=hook, premod=True, post=lambda tb: self.modulate(1, 0, 1, [tb]))
            ada_next(72)
            self.ffn(1, 0, 0, 3, premod=True, post=lambda tb: self.modulate(1, 3, 4, [tb]))
            self.P.fence(('pe', 'act', 'dve', 'sp', 'pool'))
            self.mla(1, premod=True, post=lambda tb: self.modulate(1, 6, 7, [tb]))
            self.P.fence()
            self.ffn(1, 1, 6, 5, premod=True, post=lambda tb: self.store_x([tb]))
            self.emit()
            return self.nc
        self.store_x()
        self.emit()
        return self.nc

    def emit(self):
        nc = self.nc
        P = self.P
        names = sorted(P.cnt.keys())
        sems = {}
        import contextlib
        with contextlib.ExitStack() as st:
            for n in names:
                sems[n] = st.enter_context(nc.semaphore("s_" + n))
            block = st.enter_context(nc.Block())
            finals = [(n, v) for n, v in P.cnt.items() if n not in ENGS]

            def run(e, key, final=False):
                for waits, fn, incspec in P.ops[key]:
                    for s, v in waits:
                        e.wait_ge(sems[s], v)
                    if fn is None:
                        continue
                    ins = fn(e)
                    if incspec is not None:
                        ins.then_inc(sems[incspec[0]], incspec[1])
                if final:
                    for n, v in finals:
                        e.wait_ge(sems[n], v)

            @block.tensor
            def _(e):
                run(e, 'pe')

            @block.scalar
            def _(e):
                run(e, 'act')

            @block.vector
            def _(e):
                run(e, 'dve')

            @block.gpsimd
            def _(e):
                run(e, 'pool')

            @block.sync
            def _(e):
                run(e, 'sp', final=True)
        print("ops:", {k: len(v) for k, v in P.ops.items()}, "sems:", len(names))


def _pack_vecs(inp, cond2, flag):
    v = np.zeros((128, NV), np.float32)

    def put(col, vec):
        vec = np.asarray(vec, np.float32)
        n = vec.shape[0] // 128
        v[:, col:col + n] = vec.reshape(n, 128).T
    for r in range(2):
        put(CO_COND + r * 8, cond2[r])
    for l in range(2):
        put(CO_BADA + l * 72, inp['b_ada'][l])
        for s in range(3):
            put(CO_LNG + (l * 3 + s) * 8, inp['ln_g'][l, s])
            put(CO_LNB + (l * 3 + s) * 8, inp['ln_b'][l, s])
    cw = np.asarray(inp['conv_w'][0], np.float32)
    for i in range(4):
        v[:, CO_CW + i * 31: CO_CW + (i + 1) * 31] = cw[:, i * 128:(i + 1) * 128].T
    put(CO_CB, inp['conv_b'][0]); put(CO_CNG, inp['conv_norm_g'][0]); put(CO_CNB, inp['conv_norm_b'][0])
    put(CO_PSC, inp['pool_scale'][0])
    put(CO_QG, inp['mla_q_norm_g'][0]); put(CO_KVG, inp['mla_kv_norm_g'][0])
    v[:, CO_FLAG] = flag
    return v


def _rope_tables(real):
    C = np.ones((64, 1024), np.float32)
    S = np.zeros((64, 1024), np.float32)
    if real:
        n = 1024
        row = np.repeat(np.arange(n // 64), 64).astype(np.float32)
        col = np.tile(np.arange(64), n // 64).astype(np.float32)
        inv = (10000.0 ** (-np.arange(16, dtype=np.float32) / 16)).astype(np.float32)
        ang = np.concatenate([row[:, None] * inv, col[:, None] * inv], -1).astype(np.float32)
        cos, sin = np.cos(ang), np.sin(ang)
        for a in range(2):
            for j in range(2):
                for p in range(16):
                    dd = a * 32 + j * 16 + p
                    C[dd] = cos[:, a * 16 + p]
                    S[dd] = (-sin[:, a * 16 + p]) if j == 0 else sin[:, a * 16 + p]
    return C, S


_NC_CACHE = {}


def _prep_inputs(inp):
    inp = {k: np.asarray(v) for k, v in inp.items()}
    xp = inp['x_prompt'].astype(np.float32)
    xs = inp['x_sample'].astype(np.float32)
    ident = np.eye(128, dtype=np.float32)
    onehot = np.zeros((4, 1024), np.float32)
    for j in range(4):
        onehot[j, j * 256:(j + 1) * 256] = 1.0
    perm = np.arange(64).reshape(2, 2, 16)[:, ::-1, :].reshape(64)
    w_uq = inp['mla_w_uq'][0]
    uq_r = w_uq.reshape(384, 8, 192)[:, :, 128:]
    w_uq_sw = np.ascontiguousarray(uq_r[:, :, perm].reshape(384, 512))
    w_dkv_sw = np.ascontiguousarray(inp['mla_w_dkv'][0][:, 256:][:, perm])
    shared = {k: np.ascontiguousarray(inp[k], dtype=np.float32) for k in
              ('w_ada', 'ffn_w1', 'ffn_w3', 'ffn_w2', 'cp_w_in', 'pool_w', 'cp_w_out', 'mla_w_dq', 'mla_w_uq',
               'mla_w_dkv', 'mla_w_ukv', 'mla_w_o')}
    shared.update(ident=ident, onehot=onehot, w_uq_sw=w_uq_sw, w_dkv_sw=w_dkv_sw)
    in_maps = []
    for r in range(8):
        if r < 4:
            x = np.concatenate([xs[r], xp[2 * r], xp[2 * r + 1]], 0)
            cond2 = np.stack([inp['c_ctx'], inp['c'][r]], 0)
            flag = 1.0
            cckv = inp['cache_mla_ckv'][r, 0]
            ckr = inp['cache_mla_krope'][r, 0]
            maskk = np.zeros((4, 1536), np.float32)
            C, S = _rope_tables(True)
        else:
            p0 = 8 + 6 * (r - 4)
            x = xp[p0:p0 + 6].reshape(T, D)
            cond2 = np.stack([inp['c_ctx'], inp['c_ctx']], 0)
            flag = 0.0
            cckv = np.zeros((512, 256), np.float32)
            ckr = np.zeros((512, 64), np.float32)
            maskk = np.full((4, 1536), NEG, np.float32)
            for j in range(4):
                maskk[j, j * 256:(j + 1) * 256] = 0.0
            C, S = _rope_tables(False)
        m = dict(shared)
        m.update(x=np.ascontiguousarray(x), vecs=_pack_vecs(inp, cond2, flag),
                 cache_ckv=np.ascontiguousarray(cckv, dtype=np.float32),
                 cache_kr=np.ascontiguousarray(ckr, dtype=np.float32), ropeC=C, ropeS=S, maskk=maskk)
        in_maps.append(m)
    return in_maps


def _assemble(results):
    y_p = np.zeros((32, 256, D), np.float32)
    y_s = np.zeros((4, 1024, D), np.float32)
    ckv = np.zeros((32, 1, 256, 256), np.float32)
    kr = np.zeros((32, 1, 256, 64), np.float32)
    for r in range(8):
        y = results[r]['y']
        ok = results[r]['ockv']
        okr = results[r]['okr']
        if r < 4:
            y_s[r] = y[:1024]
            for i in range(2):
                sl = slice(1024 + 256 * i, 1024 + 256 * (i + 1))
                y_p[2 * r + i] = y[sl]; ckv[2 * r + i, 0] = ok[sl]; kr[2 * r + i, 0] = okr[sl]
        else:
            p0 = 8 + 6 * (r - 4)
            for i in range(6):
                sl = slice(256 * i, 256 * (i + 1))
                y_p[p0 + i] = y[sl]; ckv[p0 + i, 0] = ok[sl]; kr[p0 + i, 0] = okr[sl]
    return y_p, y_s, ckv, kr


def kernel(**inputs):
    in_maps = _prep_inputs(inputs)
    if 'nc' not in _NC_CACHE:
        _NC_CACHE['nc'] = Builder().build()
    res = run_bass_kernel_spmd(_NC_CACHE['nc'], in_maps, core_ids=list(range(8)))
    return _assemble(res.results)
```

```python
import numpy as np
import concourse.bass as bass
import concourse.mybir as mybir
from concourse.bass_utils import run_bass_kernel_spmd

F32 = mybir.dt.float32
BF16 = mybir.dt.bfloat16
AF = mybir.ActivationFunctionType
ALU = mybir.AluOpType

T = 1536
D = 1024
KC = 8
TB = 512
NTB = 3
DFF = 2816
NJ = 22
ALPHA = float((2 * 2) ** 0.25)
LN_EPS = 1e-5
RMS_EPS = 1e-6
NEG = -30000.0

CO_COND = 0
CO_BADA = CO_COND + 16
CO_LNG = CO_BADA + 144
CO_LNB = CO_LNG + 48
CO_CW = CO_LNB + 48
CO_CB = CO_CW + 124
CO_CNG = CO_CB + 4
CO_CNB = CO_CNG + 4
CO_PSC = CO_CNB + 4
CO_QG = CO_PSC + 4
CO_KVG = CO_QG + 3
CO_FLAG = CO_KVG + 2
NV = CO_FLAG + 1

ENGS = ('pe', 'act', 'dve', 'pool', 'sp')


class Prog:
    def __init__(self):
        self.ops = {e: [] for e in ENGS}
        self.cnt = {}
        self.seen = {e: {} for e in ENGS}
        self.res = {}

    def op(self, eng, fn, reads=(), writes=(), inc=True, dma=None):
        need = {}

        def add(tok):
            if tok is not None:
                s, v = tok
                if need.get(s, 0) < v:
                    need[s] = v
        for r in reads:
            e = self.res.get(r)
            if e is not None:
                add(e[0])
        for w in writes:
            e = self.res.get(w)
            if e is not None:
                add(e[0])
                for s, v in e[1].items():
                    add((s, v))
        if dma is not None and self.cnt.get(dma, 0) > 0:
            add((dma, self.cnt[dma]))
        waits = []
        for s, v in need.items():
            if eng == 'pe' and s == 'pe':
                continue
            if self.seen[eng].get(s, 0) >= v:
                continue
            self.seen[eng][s] = v
            waits.append((s, v))
        if dma is not None:
            self.cnt[dma] = self.cnt.get(dma, 0) + 16
            tok = (dma, self.cnt[dma])
            incspec = (dma, 16)
        else:
            before = self.cnt.get(eng, 0)
            tok = (eng, before + 1)
            if inc:
                self.cnt[eng] = before + 1
                incspec = (eng, 1)
            else:
                incspec = None
        for r in reads:
            e = self.res.setdefault(r, [None, {}])
            if e[1].get(tok[0], 0) < tok[1]:
                e[1][tok[0]] = tok[1]
        for w in writes:
            self.res[w] = [tok, {}]
        self.ops[eng].append((waits, fn, incspec))


    def fence(self, engines=('pe', 'act', 'dve', 'sp')):
        for eng in engines:
            waits = []
            for s, v in self.cnt.items():
                if v == 0 or (eng == 'pe' and s == 'pe'):
                    continue
                if self.seen[eng].get(s, 0) >= v:
                    continue
                self.seen[eng][s] = v
                waits.append((s, v))
            if waits:
                self.ops[eng].append((waits, None, None))


class Builder:
    def __init__(self, debug=None):
        self.debug = debug
        nc = self.nc = bass.Bass("TRN2", target_bir_lowering=False)
        self.P = Prog()
        self.dr = {}

        def din(name, shape):
            self.dr[name] = nc.dram_tensor(name, list(shape), F32, kind="ExternalInput").ap()

        def dout(name, shape):
            self.dr[name] = nc.dram_tensor(name, list(shape), F32, kind="ExternalOutput").ap()
        din('x', [T, D]); din('vecs', [128, NV]); din('ident', [128, 128])
        din('cache_ckv', [512, 256]); din('cache_kr', [512, 64])
        din('ropeC', [64, 1024]); din('ropeS', [64, 1024])
        din('maskk', [4, 1536]); din('onehot', [4, 1024])
        din('w_ada', [2, 1024, 9216])
        din('ffn_w1', [2, 2, 1024, DFF]); din('ffn_w3', [2, 2, 1024, DFF]); din('ffn_w2', [2, 2, DFF, 1024])
        din('cp_w_in', [1, 1024, 1536]); din('pool_w', [1, 4, 128, 128]); din('cp_w_out', [1, 1024, 1024])
        din('mla_w_dq', [1, 1024, 384]); din('mla_w_uq', [1, 384, 1536]); din('w_uq_sw', [384, 512])
        din('mla_w_dkv', [1, 1024, 320]); din('w_dkv_sw', [1024, 64])
        din('mla_w_ukv', [1, 256, 2048]); din('mla_w_o', [1, 1024, 1024])
        dout('y', [T, D]); dout('ockv', [T, 256]); dout('okr', [T, 64])

        self.big = nc.alloc_sbuf_tensor("big", [128, 103 * 1024], BF16)
        self.off = 0
        self.X = self.alloc((KC, T), F32)
        self.H = self.alloc((KC, T), BF16)
        self.g_off = self.off
        self.G = self.alloc((NJ, T), BF16)
        self.g_end = self.off
        self.WF = [self.alloc((2048,), BF16) for _ in range(8)]
        self.WA = [self.alloc((8, 256), BF16) for _ in range(2)]
        self.VEC = self.alloc((NV,), F32)
        self.ADAT = self.alloc((2, 72, 2), F32)
        self.IDF = self.alloc((128,), F32)
        self.IDB = self.alloc((128,), BF16)
        self.ONES = self.alloc((128,), BF16)
        self.SC = self.alloc((8, 2), BF16)
        self.SM = self.alloc((16,), F32)
        self.DRF = self.alloc((128,), F32)
        self.ONESF = self.alloc((128,), F32)
        self.rb_bank = 7
        self.EPS_LN = self.alloc((1,), F32)
        self.EPS_RMS = self.alloc((1,), F32)
        self.ST = [self.alloc((TB,), F32) for _ in range(4)]
        self.TMP = [self.alloc((TB,), F32) for _ in range(4)]
        self.XB = [self.alloc((TB,), BF16) for _ in range(4)]
        print("sbuf used bytes/partition:", self.off)
        assert self.off <= 206 * 1024
        self.ps = [nc.alloc_psum_tensor("ps%d" % b, [128, 512], F32)[:, :] for b in range(8)]
        self.stage_off = self.g_end - 8192
        self.bank_rr = 0
        self.nbank_rr = 7
        self.live_banks = set()
        self.wf_rr = 0
        self.wa_rr = 0
        self.tmp_rr = 0
        self.xb_rr = 0
        self.out_rr = 0

    def view(self, off, shape, dtype):
        n = int(np.prod(shape))
        esz = 4 if dtype == F32 else 2
        assert off % 4 == 0
        ap = self.big[:, off // 2: (off + n * esz) // 2]
        if dtype == F32:
            ap = ap.bitcast(F32)
        if len(shape) == 2:
            ap = ap.rearrange("p (a b) -> p a b", a=shape[0])
        elif len(shape) == 3:
            ap = ap.rearrange("p (a b c) -> p a b c", a=shape[0], b=shape[1])
        return ap

    def alloc(self, shape, dtype):
        n = int(np.prod(shape))
        esz = 4 if dtype == F32 else 2
        off = (self.off + 31) // 32 * 32
        ap = self.view(off, shape, dtype)
        self.off = off + n * esz
        return ap

    def bank(self):
        while True:
            b = self.bank_rr
            self.bank_rr = (b + 1) % self.nbank_rr
            if b not in self.live_banks:
                return b

    def tmp(self):
        i = self.tmp_rr
        self.tmp_rr = (i + 1) % 4
        return i

    def act(self, out, in_, func, reads, writes, **kw):
        self.P.op('act', lambda e: e.activation(out=out, in_=in_, func=func, **kw), reads, writes)

    def dve(self, fn, reads, writes):
        self.P.op('dve', fn, reads, writes)

    def mm(self, out, lhsT, rhs, start, stop, reads, writes, inc=None):
        self.P.op('pe', lambda e: e.matmul(out, lhsT=lhsT, rhs=rhs, start=start, stop=stop),
                  reads, writes, inc=(stop if inc is None else inc))

    def tr(self, out, in_, ident, reads, writes, inc=True):
        self.P.op('pe', lambda e: e.transpose(out=out, in_=in_, identity=ident), reads, writes, inc=inc)

    def wload(self, src_ap, rows=8, cols=256, slot=None):
        if slot is None:
            i = self.wf_rr
            self.wf_rr = (i + 1) % 8
        else:
            i = slot
        dst = self.WF[i][:, 0:rows * cols].rearrange("p (a b) -> p a b", a=rows)
        self.P.op('pool', lambda e: e.dma_start(out=dst, in_=src_ap), reads=(), writes=[('wf', i)], dma='wf%d' % i)
        return (i, dst)

    def vcol(self, col, n=1):
        return self.VEC[:, col:col + n]

    def constants(self):
        P = self.P
        P.op('sp', lambda e: e.dma_start(out=self.VEC, in_=self.dr['vecs']), writes=['vec'], dma='c0')
        P.op('sp', lambda e: e.dma_start(out=self.IDF, in_=self.dr['ident']), writes=['idf'], dma='c1')
        self.dve(lambda e: e.tensor_copy(out=self.IDB, in_=self.IDF), ['idf'], ['idb'])
        self.dve(lambda e: e.memset(self.ONES, 1.0), [], ['ones'])
        self.dve(lambda e: e.memset(self.ONESF, 1.0), [], ['onesf'])
        self.dve(lambda e: e.memset(self.EPS_LN, LN_EPS), [], ['eps'])
        self.dve(lambda e: e.memset(self.EPS_RMS, RMS_EPS), [], ['eps2'])
        cond = self.VEC[:, CO_COND:CO_COND + 16].rearrange("p (r c) -> p c r", r=2)
        self.act(self.SC, cond, AF.Silu, ['vec'], ['sc'])

    def load_x(self, hook=None):
        xin = [self.view(self.stage_off - 8192 + i * 4096, (D,), F32) for i in range(4)]
        for tt in range(T // 128):
            if hook is not None and tt in (1, 4, 7, 10):
                hook()
            s = tt % 4
            src = self.dr['x'][tt * 128:(tt + 1) * 128, :]
            dst = xin[s]
            self.P.op('sp', lambda e, dst=dst, src=src: e.dma_start(out=dst, in_=src), writes=[('xin', s)], dma='xin%d' % s)
            for half in range(2):
                b = self.bank()
                for q in range(4):
                    c = half * 4 + q
                    self.tr(self.ps[b][:, q * 128:(q + 1) * 128], xin[s][:, c * 128:(c + 1) * 128], self.IDF,
                            [('xin', s), 'idf'], [('ps', b)], inc=(q == 3))
                src_ps = self.ps[b][:, :].rearrange("p (q t) -> p q t", q=4)
                out = self.X[:, half * 4:half * 4 + 4, tt * 128:(tt + 1) * 128]
                wr = [('X', half * 4 + q, tt // 4) for q in range(4)]
                if half == 0:
                    self.act(out, src_ps, AF.Copy, [('ps', b)], [('ps', b)] + wr)
                else:
                    self.dve(lambda e, out=out, src_ps=src_ps: e.tensor_copy(out=out, in_=src_ps), [], [('ps', b)] + wr)

    def ada_units(self, units):
        b = 7
        for (l, u) in units:
            i = self.wa_rr
            self.wa_rr = (i + 1) % 2
            src = self.dr['w_ada'][l, :, u * 256:(u + 1) * 256].rearrange("(k p) n -> p k n", p=128)
            dst = self.WA[i]
            self.P.op('pool', lambda e, dst=dst, src=src: e.dma_start(out=dst, in_=src), writes=[('wa', i)], dma='wa%d' % i)
            for fl in range(2):
                fc = u * 2 + fl
                for k in range(KC):
                    self.mm(self.ps[b][:, fc * 2:fc * 2 + 2], self.WA[i][:, k, fl * 128:(fl + 1) * 128], self.SC[:, k, :],
                            k == 0, k == KC - 1, [('wa', i), 'sc'], [('ps', b)])
        for (l, u) in units:
            m = (u * 2) // 8
            for r in range(2):
                out = self.ADAT[:, l, u * 2:u * 2 + 2, r]
                in0 = self.ps[b][:, u * 4:u * 4 + 4].rearrange("p (f r) -> p f r", r=2)[:, :, r]
                in1 = self.VEC[:, CO_BADA + l * 72 + u * 2: CO_BADA + l * 72 + u * 2 + 2]
                self.dve(lambda e, out=out, in0=in0, in1=in1: e.tensor_tensor(out=out, in0=in0, in1=in1, op=ALU.add),
                         ['vec'], [('ps', b), ('ada', l, u)])
            out = self.ADAT[:, l, u * 2:u * 2 + 2, :]
            if m in (1, 4, 7):
                self.dve(lambda e, out=out: e.tensor_scalar_add(out=out, in0=out, scalar1=1.0), [], [('ada', l, u)])
            elif m in (2, 8):
                self.dve(lambda e, out=out: e.tensor_scalar_mul(out=out, in0=out, scalar1=0.5), [], [('ada', l, u)])

    def ada_res(self, l, m):
        return [('ada', l, m * 4 + j) for j in range(4)]

    def mod(self, l, m, c, r):
        return self.ADAT[:, l, m * 8 + c, r:r + 1]

    @staticmethod
    def cond_of(tb):
        return 1 if tb < 2 else 0

    def modulate(self, l, m_shift, m_scale, tbs=None):
        rd_ada = self.ada_res(l, m_shift) + self.ada_res(l, m_scale)
        for tb in (range(NTB) if tbs is None else tbs):
            r = self.cond_of(tb)
            for c in range(KC):
                out = self.H[:, c, tb * TB:(tb + 1) * TB]
                in_ = self.X[:, c, tb * TB:(tb + 1) * TB]
                sc = self.mod(l, m_scale, c, r)
                sh = self.mod(l, m_shift, c, r)
                if c % 2 == 0:
                    self.act(out, in_, AF.Identity, [('X', c, tb)] + rd_ada, [('H', c, tb)], scale=sc, bias=sh)
                else:
                    self.dve(lambda e, out=out, in_=in_, sc=sc, sh=sh: e.tensor_scalar(
                        out=out, in0=in_, scalar1=sc, scalar2=sh, op0=ALU.mult, op1=ALU.add),
                        [('X', c, tb)] + rd_ada, [('H', c, tb)])

    def _ln_args(self, tb, src, skey, dst, dkey, func):
        sl = slice(tb * TB, (tb + 1) * TB)
        if src is None:
            src = lambda c: self.X[:, c, sl]
            skey = lambda c: ('X', c, tb)
        if dst is None:
            dst, dkey = src, skey
        if func is None:
            func = AF.Identity
        return src, skey, dst, dkey, func

    def ln_stats(self, tb, nch=KC, src=None, skey=None):
        src, skey, _, _, _ = self._ln_args(tb, src, skey, None, None, None)
        bs, bq = self.bank(), self.bank()
        for c in range(nch):
            i = (self.xb_rr // 2 * 2) % 4
            self.xb_rr = (i + 2) % 4
            xq = self.XB[i + 1]
            self.act(xq, src(c), AF.Square, [skey(c)], [('xb', i + 1)])
            self.mm(self.ps[bs], self.ONESF, src(c), c == 0, c == nch - 1, ['onesf', skey(c)], [('ps', bs)], inc=True)
            self.mm(self.ps[bq], self.ONES, xq, c == 0, c == nch - 1, ['ones', ('xb', i + 1)], [('ps', bq)], inc=True)
        return bs, bq

    def ln_apply(self, tb, banks, g_col, b_col, nch=KC, src=None, skey=None, dst=None, dkey=None, func=None):
        src, skey, dst, dkey, func = self._ln_args(tb, src, skey, dst, dkey, func)
        bs, bq = banks
        mean, msq, var, rstd = self.ST
        n = float(nch * 128)
        self.dve(lambda e: e.tensor_scalar_mul(out=mean, in0=self.ps[bs], scalar1=1.0 / n), [], [('ps', bs), ('st', 0)])
        self.dve(lambda e: e.tensor_tensor(out=msq, in0=mean, in1=mean, op=ALU.mult), [('st', 0)], [('st', 1)])
        self.dve(lambda e: e.scalar_tensor_tensor(out=var, in0=self.ps[bq], scalar=1.0 / n, in1=msq,
                                                 op0=ALU.mult, op1=ALU.subtract), [('st', 1)], [('ps', bq), ('st', 2)])
        self.act(var, var, AF.Ln, ['eps'], [('st', 2)], bias=self.EPS_LN, scale=1.0)
        self.act(rstd, var, AF.Exp, [('st', 2)], [('st', 3)], scale=-0.5)
        for c in range(nch):
            xs = src(c)
            t = self.tmp()
            tt_ = self.TMP[t]
            self.dve(lambda e, xs=xs, tt_=tt_: e.tensor_tensor(out=tt_, in0=xs, in1=mean, op=ALU.subtract),
                     [skey(c), ('st', 0)], [('tmp', t)])
            self.dve(lambda e, tt_=tt_: e.tensor_tensor(out=tt_, in0=tt_, in1=rstd, op=ALU.mult),
                     [('st', 3)], [('tmp', t)])
            self.act(dst(c), tt_, func, [('tmp', t), 'vec'], [dkey(c)],
                     scale=self.vcol(g_col + c), bias=self.vcol(b_col + c))

    def layernorm_all(self, g_col, b_col, post=None, nch=KC, mk=None):
        kw = [(mk(tb) if mk is not None else {}) for tb in range(NTB)]
        banks = {}

        def stats(tb):
            banks[tb] = self.ln_stats(tb, nch=nch, src=kw[tb].get('src'), skey=kw[tb].get('skey'))
            self.live_banks.update(banks[tb])

        def apply(tb):
            self.ln_apply(tb, banks[tb], g_col, b_col, nch=nch, **kw[tb])
            self.live_banks.difference_update(banks[tb])
            if post is not None:
                post(tb)
        stats(0); stats(1); apply(0); stats(2); apply(1); apply(2)

    def ffn(self, l, f, m0, ln_idx, ada_hook=None, parts=3, premod=False, post=None):
        if not premod:
            self.modulate(l, m0, m0 + 1)
        w1 = self.dr['ffn_w1'][l, f]
        w3 = self.dr['ffn_w3'][l, f]
        w2 = self.dr['ffn_w2'][l, f]
        loads = []
        for ng in range(NJ // 2):
            loads.append((w1[:, ng * 256:(ng + 1) * 256].rearrange("(k p) n -> p k n", p=128), 8, 256))
            loads.append((w3[:, ng * 256:(ng + 1) * 256].rearrange("(k p) n -> p k n", p=128), 8, 256))
        for c in range(KC):
            loads.append((w2[0:11 * 128, c * 128:(c + 1) * 128].rearrange("(j p) n -> p j n", p=128), 11, 128))
            loads.append((w2[11 * 128:22 * 128, c * 128:(c + 1) * 128].rearrange("(j p) n -> p j n", p=128), 11, 128))
        slots = []

        def issue(upto):
            while len(slots) <= min(upto, len(loads) - 1):
                a, rws, cls = loads[len(slots)]
                slots.append(self.wload(a, rows=rws, cols=cls))
        for ng in range(NJ // 2):
            issue(min(2 * (ng + 2) + 1, 21 if parts < 2 else 99))
            s1, s3 = slots[2 * ng], slots[2 * ng + 1]
            for jl in range(2):
                j = ng * 2 + jl
                for tb in range(NTB):
                    sl = slice(tb * TB, (tb + 1) * TB)
                    ba, bb = self.bank(), self.bank()
                    for (bk, sw) in ((ba, s1), (bb, s3)):
                        for k in range(KC):
                            self.mm(self.ps[bk], sw[1][:, k, jl * 128:(jl + 1) * 128], self.H[:, k, sl],
                                    k == 0, k == KC - 1, [('wf', sw[0]), ('H', k, tb)], [('ps', bk)])
                    t = self.tmp()
                    s_ = self.TMP[t]
                    self.act(s_, self.ps[ba], AF.Silu, [], [('ps', ba), ('tmp', t)])
                    out = self.G[:, j, sl]
                    self.dve(lambda e, out=out, s_=s_, bb=bb: e.tensor_tensor(out=out, in0=self.ps[bb], in1=s_, op=ALU.mult),
                             [('tmp', t)], [('ps', bb), ('G', j, tb)])
            if ada_hook is not None:
                ada_hook()
        if parts < 2:
            return
        rd_gate = self.ada_res(l, m0 + 2)
        for c in range(KC):
            issue(22 + 2 * (c + 2) + 1)
            sa, sb = slots[22 + 2 * c], slots[22 + 2 * c + 1]
            for tb in range(NTB):
                sl = slice(tb * TB, (tb + 1) * TB)
                r = self.cond_of(tb)
                b = self.bank()
                for j in range(NJ):
                    sw = sa if j < 11 else sb
                    self.mm(self.ps[b], sw[1][:, j % 11, :], self.G[:, j, sl],
                            j == 0, j == NJ - 1, [('wf', sw[0]), ('G', j, tb)], [('ps', b)])
                t = self.tmp()
                y_ = self.TMP[t]
                self.act(y_, self.ps[b], AF.Identity, rd_gate, [('ps', b), ('tmp', t)], scale=self.mod(l, m0 + 2, c, r))
                xs = self.X[:, c, sl]
                self.dve(lambda e, xs=xs, y_=y_: e.scalar_tensor_tensor(out=xs, in0=xs, scalar=ALPHA, in1=y_,
                                                                       op0=ALU.mult, op1=ALU.add),
                         [('tmp', t)], [('X', c, tb)])
            if ada_hook is not None:
                ada_hook()
        if parts < 3:
            return
        self.layernorm_all(CO_LNG + ln_idx * 8, CO_LNB + ln_idx * 8, post=post)

    def resid_ln(self, l, m_gate, ps_b, c, tb, rd_gate):
        sl = slice(tb * TB, (tb + 1) * TB)
        r = self.cond_of(tb)
        t = self.tmp()
        y_ = self.TMP[t]
        self.act(y_, self.ps[ps_b], AF.Identity, rd_gate, [('ps', ps_b), ('tmp', t)], scale=self.mod(l, m_gate, c, r))
        xs = self.X[:, c, sl]
        self.dve(lambda e, xs=xs, y_=y_: e.scalar_tensor_tensor(out=xs, in0=xs, scalar=ALPHA, in1=y_,
                                                               op0=ALU.mult, op1=ALU.add),
                 [('tmp', t)], [('X', c, tb)])

    def mixer0(self, l=0, premod=False, post=None):
        if not premod:
            self.modulate(l, 3, 4)
        go = self.g_off
        CB = self.view(go, (4, 6, 286), BF16)
        DT = self.view(go + 13824, (4, T), BF16)
        CO = self.view(go + 26112, (4, T), F32)
        DG = [self.view(go + 50688 + i * 7936, (31, 128), BF16) for i in range(2)]
        PA = self.view(go + 26112, (2, 6, 271), F32)
        PBf = self.view(go + 26112 + 13024, (2, 6, 271), F32)
        HP = self.view(go + 26112 + 26048, (6, 256), F32)
        AB = self.H
        flag = self.vcol(CO_FLAG)
        w_in = self.dr['cp_w_in'][0]

        def win(c0):
            return self.wload(w_in[:, c0:c0 + 256].rearrange("(k p) n -> p k n", p=128))
        self.dve(lambda e: e.memset(CB, 0.0), [], ['CB'])
        for pair in range(2):
            su = win(pair * 256)
            sg = win(512 + pair * 256)
            for il in range(2):
                i = pair * 2 + il
                for tb in range(NTB):
                    sl = slice(tb * TB, (tb + 1) * TB)
                    bu, bg = self.bank(), self.bank()
                    for (bk, sw) in ((bu, su), (bg, sg)):
                        for k in range(KC):
                            self.mm(self.ps[bk], sw[1][:, k, il * 128:(il + 1) * 128], self.H[:, k, sl],
                                    k == 0, k == KC - 1, [('wf', sw[0]), ('H', k, tb)], [('ps', bk)])
                    t = self.tmp()
                    s_ = self.TMP[t]
                    self.act(s_, self.ps[bg], AF.Sigmoid, [], [('ps', bg), ('tmp', t)])
                    out = CB[:, i, 2 * tb:2 * tb + 2, 15:271]
                    in0 = self.ps[bu].rearrange("p (s u) -> p s u", s=2)
                    in1 = s_.rearrange("p (s u) -> p s u", s=2)
                    self.dve(lambda e, out=out, in0=in0, in1=in1: e.tensor_tensor(out=out, in0=in0, in1=in1, op=ALU.mult),
                             [('tmp', t), 'CB'], [('ps', bu), ('CBc', i, tb)])
        for i in range(4):
            cbk = [('CBc', i, tb) for tb in range(NTB)]
            o1, i1 = CB[:, i, 1:4, 0:15], CB[:, i, 0:3, 256:271]
            o2, i2 = CB[:, i, 0:3, 271:286], CB[:, i, 1:4, 15:30]
            self.dve(lambda e, o1=o1, i1=i1: e.tensor_scalar_mul(out=o1, in0=i1, scalar1=flag), cbk + ['vec'], [('CBh', i, 0)])
            self.dve(lambda e, o2=o2, i2=i2: e.tensor_scalar_mul(out=o2, in0=i2, scalar1=flag), cbk + ['vec'], [('CBh', i, 1)])
        spw = self.wload(self.dr['pool_w'][0].rearrange("g p n -> p g n"), rows=4, cols=128)
        PD = [PA[:, 0], PBf[:, 0]]
        ICE = PA[:, 1, 0:4, 0:96].rearrange("p g (s e) -> p g s e", s=6)
        STEPS = [(1, 0, 1, 271, 0, 1), (0, 1, 2, 270, 1, 3), (1, 0, 4, 268, 2, 6), (0, 1, 8, 264, 4, 12)]

        def halo_zero(buf, key):
            self.dve(lambda e: e.memset(buf[:, :, 0:8], 0.0), [], [key])
            self.dve(lambda e: e.memset(buf[:, :, 264:271], 0.0), [], [key])

        def halo_flag(buf, rd, key):
            self.dve(lambda e: e.tensor_scalar_mul(out=buf[:, 1:4, 0:8], in0=buf[:, 0:3, 256:264], scalar1=flag), rd + ['vec'], [key])
            self.dve(lambda e: e.tensor_scalar_mul(out=buf[:, 0:3, 264:271], in0=buf[:, 1:4, 8:15], scalar1=flag), rd + ['vec'], [key])

        def run_steps(nsteps, first_rd, on_step=None):
            rd = first_rd
            for si in range(nsteps):
                di, si_, lo, hi, a0_, a1_ = STEPS[si]
                n = hi - lo
                o, x0, x1 = PD[di][:, :, lo:hi], PD[si_][:, :, a0_:a0_ + n], PD[si_][:, :, a1_:a1_ + n]
                key = ('PS', di)
                self.dve(lambda e, o=o, x0=x0, x1=x1: e.tensor_tensor(out=o, in0=x0, in1=x1, op=ALU.add), rd, [key])
                rd = [key]
                if on_step is not None:
                    on_step(si, PD[di], key)
            return rd
        self.dve(lambda e: e.memset(PD[0], 0.0), [], [('PS', 0)])
        self.dve(lambda e: e.memset(PD[0][:, :, 8:264], 1.0), [], [('PS', 0)])
        halo_flag(PD[0], [], ('PS', 0))

        def grab(si, buf, key):
            self.dve(lambda e: e.reciprocal(out=ICE[:, si, :, 0:8], in_=buf[:, :, 8:16]), [key], [('ICE', si)])
            self.dve(lambda e: e.reciprocal(out=ICE[:, si, :, 8:16], in_=buf[:, :, 256:264]), [key], [('ICE', si)])
        run_steps(4, [('PS', 0)], on_step=grab)
        for pair in range(2):
            sh = win(1024 + pair * 256)
            for il in range(2):
                gi = pair * 2 + il
                halo_zero(PD[0], ('PS', 0))
                for tb in range(NTB):
                    sl = slice(tb * TB, (tb + 1) * TB)
                    b = self.bank()
                    for k in range(KC):
                        self.mm(self.ps[b], sh[1][:, k, il * 128:(il + 1) * 128], self.H[:, k, sl],
                                k == 0, k == KC - 1, [('wf', sh[0]), ('H', k, tb)], [('ps', b)])
                    src = self.ps[b].rearrange("p (s u) -> p s u", s=2)
                    self.act(PD[0][:, 2 * tb:2 * tb + 2, 8:264], src, AF.Copy, [], [('ps', b), ('PS', 0)])
                    self.act(HP[:, 2 * tb:2 * tb + 2, :], src, AF.Copy, [], [('ps', b), ('HP', tb)])
                halo_flag(PD[0], [], ('PS', 0))
                rd = run_steps(gi + 1, [('PS', 0)])
                fin = PD[STEPS[gi][0]]
                fkey = ('PS', STEPS[gi][0])
                w_ = float(2 ** (gi + 1))
                sd = fin[:, :, 8:264]
                dt = DT[:, gi, :].rearrange("p (s u) -> p s u", s=6)
                hpk = [('HP', tb) for tb in range(NTB)]
                self.dve(lambda e, dt=dt, sd=sd, w_=w_: e.scalar_tensor_tensor(out=dt, in0=sd, scalar=1.0 / w_, in1=HP,
                                                                            op0=ALU.mult, op1=ALU.subtract),
                         [fkey] + hpk, [('DT', gi)])
                for (c0, e0) in ((0, 0), (248, 8)):
                    se = fin[:, :, 8 + c0:16 + c0]
                    ie = ICE[:, gi, :, e0:e0 + 8]
                    de = dt[:, :, c0:c0 + 8]
                    he = HP[:, :, c0:c0 + 8]
                    self.dve(lambda e, se=se, ie=ie: e.tensor_tensor(out=se, in0=se, in1=ie, op=ALU.mult), [('ICE', gi)], [fkey])
                    self.dve(lambda e, de=de, se=se, he=he: e.tensor_tensor(out=de, in0=se, in1=he, op=ALU.subtract), hpk, [fkey, ('DT', gi)])
                for tb in range(NTB):
                    sl = slice(tb * TB, (tb + 1) * TB)
                    b = self.bank()
                    self.mm(self.ps[b], spw[1][:, gi, :], DT[:, gi, sl], True, True, [('wf', spw[0]), ('DT', gi)], [('ps', b)])
                    self.act(DT[:, gi, sl], self.ps[b], AF.Identity, ['vec'], [('ps', b), ('DTB', gi, tb)],
                             scale=self.vcol(CO_PSC + gi))
        for i in range(4):
            dg = DG[i % 2]
            for k in range(31):
                o = dg[:, k, :]
                wc = self.vcol(CO_CW + i * 31 + k)
                self.dve(lambda e, o=o, wc=wc: e.tensor_scalar_mul(out=o, in0=self.IDB, scalar1=wc),
                         ['idb', 'vec'], [('DG', i % 2)])
            for tb in range(NTB):
                sl = slice(tb * TB, (tb + 1) * TB)
                b = self.bank()
                for k in range(31):
                    self.mm(self.ps[b], dg[:, k, :], CB[:, i, 2 * tb:2 * tb + 2, k:k + 256], k == 0, k == 30,
                            [('DG', i % 2), ('CBh', i, 0), ('CBh', i, 1)] + [('CBc', i, t2) for t2 in range(NTB)], [('ps', b)])
                self.act(CO[:, i, sl], self.ps[b], AF.Identity, ['vec'], [('ps', b), ('CO', i, tb)],
                         bias=self.vcol(CO_CB + i))
        def conv_ln_args(tb):
            sl = slice(tb * TB, (tb + 1) * TB)
            return dict(src=lambda c, sl=sl: CO[:, c, sl], skey=lambda c, tb=tb: ('CO', c, tb),
                        dst=lambda c, sl=sl: AB[:, c, sl], dkey=lambda c, tb=tb: ('H', c, tb), func=AF.Silu)
        self.layernorm_all(CO_CNG, CO_CNB, nch=4, mk=conv_ln_args)
        w_out = self.dr['cp_w_out'][0]
        rd_gate = self.ada_res(l, 5)
        for pair in range(4):
            so = self.wload(w_out[:, pair * 256:(pair + 1) * 256].rearrange("(k p) n -> p k n", p=128))
            for il in range(2):
                c = pair * 2 + il
                for tb in range(NTB):
                    sl = slice(tb * TB, (tb + 1) * TB)
                    b = self.bank()
                    for j in range(KC):
                        rhs = AB[:, j, sl] if j < 4 else DT[:, j - 4, sl]
                        rk = ('H', j, tb) if j < 4 else ('DTB', j - 4, tb)
                        self.mm(self.ps[b], so[1][:, j, il * 128:(il + 1) * 128], rhs, j == 0, j == KC - 1,
                                [('wf', so[0]), rk], [('ps', b)])
                    self.resid_ln(l, 5, b, c, tb, rd_gate)
        self.layernorm_all(CO_LNG + (l * 3 + 1) * 8, CO_LNB + (l * 3 + 1) * 8, post=post)

    def rms_block(self, tb, nch, wslices, g_col, dst_f32=None, dst_bf=None, key=None, alt=0):
        sl = slice(tb * TB, (tb + 1) * TB)
        QD = self.mla_bufs['QD'][alt]
        qk = 'QD%d' % alt
        bq = self.bank()
        for c in range(nch):
            b = self.bank()
            slot, lf = wslices[c]
            for k in range(KC):
                self.mm(self.ps[b], lf(k), self.H[:, k, sl], k == 0, k == KC - 1, [('wf', slot), ('H', k, tb)], [('ps', b)])
            self.act(QD[:, c, :], self.ps[b], AF.Copy, [], [('ps', b), (qk, c)])
            i = self.xb_rr
            self.xb_rr = (i + 1) % 4
            self.act(self.XB[i], QD[:, c, :], AF.Square, [(qk, c)], [('xb', i)])
            self.mm(self.ps[bq], self.ONES, self.XB[i], c == 0, c == nch - 1, ['ones', ('xb', i)], [('ps', bq)])
        var, rstd = (self.ST[2], self.ST[3]) if alt == 0 else (self.ST[0], self.ST[1])
        k2, k3 = (('st', 2), ('st', 3)) if alt == 0 else (('st', 0), ('st', 1))
        self.act(var, self.ps[bq], AF.Sqrt, ['eps2'], [('ps', bq), k2], bias=self.EPS_RMS, scale=1.0 / (nch * 128))
        self.dve(lambda e: e.reciprocal(out=rstd, in_=var), [k2], [k3])
        for c in range(nch):
            qd = QD[:, c, :]
            g = self.vcol(g_col + c)
            if dst_f32 is not None:
                o = dst_f32[:, c, sl]
                self.dve(lambda e, o=o, qd=qd, g=g: e.scalar_tensor_tensor(out=o, in0=qd, scalar=g, in1=rstd,
                                                                          op0=ALU.mult, op1=ALU.mult),
                         [(qk, c), k3, 'vec'], [(key + 'f', c, tb)])
                ob = dst_bf[:, c, sl]
                self.act(ob, o, AF.Copy, [(key + 'f', c, tb)], [(key, c, tb)])
            else:
                ob = dst_bf[:, c, sl]
                self.dve(lambda e, ob=ob, qd=qd, g=g: e.scalar_tensor_tensor(out=ob, in0=qd, scalar=g, in1=rstd,
                                                                            op0=ALU.mult, op1=ALU.mult),
                         [(qk, c), k3, 'vec'], [(key, c, tb)])

    def mla(self, l=1, premod=False, post=None):
        if not premod:
            self.modulate(l, 3, 4)
        self.wf_rr = 0
        self.nbank_rr = 8
        go = self.g_off
        QN = self.view(go, (3, T), BF16)
        CKVb = self.view(go + 9216, (2, 2048), BF16)
        KRb = self.view(go + 17408, (2048,), BF16)
        KT = [self.view(go + 21504 + i * 4096, (2048,), BF16) for i in range(2)]
        VH = [self.view(go + 29696 + i * 4096, (16, 128), BF16) for i in range(2)]
        QT = self.view(go + 37888, (T,), BF16)
        QR = [self.view(go + 40960 + i * 3072, (T,), BF16) for i in range(2)]
        PP = [self.view(go + 47104 + i * 3072, (T,), BF16) for i in range(2)]
        PT = self.view(go + 53248, (12, 512), BF16)
        RB = self.view(go + 65536, (512,), F32)
        CKVf = self.view(go + 21504, (2, T), F32)
        KRraw = self.view(go + 33792, (T,), F32)
        QDall = self.view(go + 39936, (3, 3, TB), F32)
        OST = [self.view(go + 58368 + i * 1280, (320,), F32) for i in range(2)]
        CST = [self.view(go + 60928 + i * 1280, (320,), F32) for i in range(4)]
        ROC = self.WA[0].rearrange("p a b -> p (a b)").bitcast(F32)
        ROS = self.WA[1].rearrange("p a b -> p (a b)").bitcast(F32)
        OT = self.H
        P = self.P
        P.op('sp', lambda e: e.dma_start(out=ROC[0:64, :], in_=self.dr['ropeC']), writes=[('wa', 0)], dma='c0')
        P.op('sp', lambda e: e.dma_start(out=ROS[0:64, :], in_=self.dr['ropeS']), writes=[('wa', 1)], dma='c1')
        P.op('pool', lambda e: e.dma_start(out=KRb[64:68, 0:1024], in_=self.dr['maskk'][:, 0:1024]), writes=['KRm0'], dma='m0')
        P.op('pool', lambda e: e.dma_start(out=KRb[64:68, 1536:2048], in_=self.dr['maskk'][:, 1024:1536]), writes=['KRm1'], dma='m1')
        for ct in range(4):
            P.op('sp', lambda e, ct=ct: e.dma_start(out=CST[ct][:, 0:256], in_=self.dr['cache_ckv'][ct * 128:(ct + 1) * 128, :]),
                 writes=[('cst', ct, 0)], dma='cs%d' % ct)
            P.op('sp', lambda e, ct=ct: e.dma_start(out=CST[ct][:, 256:320], in_=self.dr['cache_kr'][ct * 128:(ct + 1) * 128, :]),
                 writes=[('cst', ct, 1)], dma='ck%d' % ct)
        for ct in range(4):
            b = self.bank()
            for c in range(2):
                self.tr(self.ps[b][:, c * 128:(c + 1) * 128], CST[ct][:, c * 128:(c + 1) * 128], self.IDF,
                        [('cst', ct, 0), 'idf'], [('ps', b)], inc=False)
            self.tr(self.ps[b][0:64, 256:384], CST[ct][:, 256:320], self.IDF, [('cst', ct, 1), 'idf'], [('ps', b)])
            ks = slice(1536 + ct * 128, 1536 + (ct + 1) * 128)
            self.act(CKVb[:, :, ks], self.ps[b][:, 0:256].rearrange("p (c t) -> p c t", c=2), AF.Copy, [],
                     [('ps', b), ('CKVb', 3)])
            self.dve(lambda e, ks=ks, b=b: e.tensor_copy(out=KRb[0:64, ks], in_=self.ps[b][0:64, 256:384]), [],
                     [('ps', b), ('KRb', 3)])
        wdq = self.dr['mla_w_dq'][0]
        wdkv = self.dr['mla_w_dkv'][0]
        sq0 = self.wload(wdq[:, 0:256].rearrange("(k p) n -> p k n", p=128))
        sq1 = self.wload(wdq[:, 256:384].rearrange("(k p) n -> p k n", p=128), rows=8, cols=128)
        sk0 = self.wload(wdkv[:, 0:256].rearrange("(k p) n -> p k n", p=128))
        sk1 = self.wload(wdkv[:, 256:320].rearrange("(k p) n -> p k n", p=128), rows=8, cols=64)
        sk2 = self.wload(self.dr['w_dkv_sw'].rearrange("(k p) n -> p k n", p=128), rows=8, cols=64)
        qsl = [(sq0[0], lambda k: sq0[1][:, k, 0:128]), (sq0[0], lambda k: sq0[1][:, k, 128:256]), (sq1[0], lambda k: sq1[1][:, k, :])]
        ksl = [(sk0[0], lambda k: sk0[1][:, k, 0:128]), (sk0[0], lambda k: sk0[1][:, k, 128:256])]
        pp_ = [6]

        def pbank():
            pp_[0] = 13 - pp_[0]
            return pp_[0]
        for tb in range(NTB):
            sl = slice(tb * TB, (tb + 1) * TB)
            for (nch, wsl, raw, rk, sbank) in ((3, qsl, lambda c: QDall[:, tb, c, :], 'QDr', tb),
                                               (2, ksl, lambda c: CKVf[:, c, sl], 'CKVr', 3 + tb)):
                for c in range(nch):
                    b = pbank()
                    slot, lf = wsl[c]
                    for k in range(KC):
                        self.mm(self.ps[b], lf(k), self.H[:, k, sl], k == 0, k == KC - 1, [('wf', slot), ('H', k, tb)], [('ps', b)])
                    self.act(raw(c), self.ps[b], AF.Copy, [], [('ps', b), (rk, c, tb)])
                    i = self.xb_rr
                    self.xb_rr = (i + 1) % 4
                    self.act(self.XB[i], raw(c), AF.Square, [(rk, c, tb)], [('xb', i)])
                    self.mm(self.ps[sbank], self.ONES, self.XB[i], c == 0, c == nch - 1, ['ones', ('xb', i)], [('ps', sbank)], inc=True)
            br = pbank()
            for k in range(KC):
                self.mm(self.ps[br][0:64, :], sk1[1][:, k, :], self.H[:, k, sl], k == 0, k == KC - 1,
                        [('wf', sk1[0]), ('H', k, tb)], [('ps', br)])
            self.act(KRraw[0:64, sl], self.ps[br][0:64, :], AF.Copy, [], [('ps', br), ('KRraw', tb)])
            if tb < 2:
                bs = pbank()
                for k in range(KC):
                    self.mm(self.ps[bs][0:64, :], sk2[1][:, k, :], self.H[:, k, sl], k == 0, k == KC - 1,
                            [('wf', sk2[0]), ('H', k, tb)], [('ps', bs)])
                t1, t2 = self.tmp(), self.tmp()
                a1, a2 = self.TMP[t1][0:64, :], self.TMP[t2][0:64, :]
                self.dve(lambda e, a1=a1, sl=sl: e.tensor_tensor(out=a1, in0=KRraw[0:64, sl], in1=ROC[0:64, sl], op=ALU.mult),
                         [('KRraw', tb), ('wa', 0)], [('tmp', t1)])
                self.dve(lambda e, a2=a2, sl=sl, bs=bs: e.tensor_tensor(out=a2, in0=self.ps[bs][0:64, :], in1=ROS[0:64, sl], op=ALU.mult),
                         [('wa', 1)], [('ps', bs), ('tmp', t2)])
                self.dve(lambda e, a1=a1, a2=a2, sl=sl: e.tensor_tensor(out=KRb[0:64, sl], in0=a1, in1=a2, op=ALU.add),
                         [('tmp', t1), ('tmp', t2)], [('KRb', tb)])
            else:
                self.dve(lambda e, sl=sl: e.tensor_copy(out=KRb[0:64, sl], in_=KRraw[0:64, sl]), [('KRraw', tb)], [('KRb', tb)])
        for tb in range(NTB):
            sl = slice(tb * TB, (tb + 1) * TB)
            for (nch, g_col, raw, rk, sbank, vi) in ((3, CO_QG, lambda c: QDall[:, tb, c, :], 'QDr', tb, 0),
                                                     (2, CO_KVG, lambda c: CKVf[:, c, sl], 'CKVr', 3 + tb, 2)):
                var, rstd = self.ST[vi], self.ST[vi + 1]
                kv_, kr_ = ('st', vi), ('st', vi + 1)
                self.act(var, self.ps[sbank], AF.Identity, ['eps2'], [('ps', sbank), kv_], bias=self.EPS_RMS, scale=1.0 / (nch * 128))
                self.act(var, var, AF.Ln, [], [kv_])
                self.act(rstd, var, AF.Exp, [kv_], [kr_], scale=-0.5)
                for c in range(nch):
                    g = self.vcol(g_col + c)
                    x_ = raw(c)
                    if nch == 3:
                        ob = QN[:, c, sl]
                        self.dve(lambda e, ob=ob, x_=x_, g=g, rstd=rstd: e.scalar_tensor_tensor(out=ob, in0=x_, scalar=g, in1=rstd,
                                                                                              op0=ALU.mult, op1=ALU.mult),
                                 [(rk, c, tb), kr_, 'vec'], [('QN', c, tb)])
                    else:
                        self.dve(lambda e, x_=x_, g=g, rstd=rstd: e.scalar_tensor_tensor(out=x_, in0=x_, scalar=g, in1=rstd,
                                                                                        op0=ALU.mult, op1=ALU.mult),
                                 [kr_, 'vec'], [(rk, c, tb), ('CKVf', c, tb)])
                        self.act(CKVb[:, c, sl], x_, AF.Copy, [('CKVf', c, tb)], [('CKV', c, tb)])
        for tt in range(T // 128):
            s = tt % 2
            ts_ = slice(tt * 128, (tt + 1) * 128)
            tb = tt // 4
            b = self.bank()
            for c in range(2):
                self.tr(self.ps[b][:, c * 128:(c + 1) * 128], CKVf[:, c, ts_], self.IDF, [('CKVf', c, tb), 'idf'], [('ps', b)], inc=False)
            self.tr(self.ps[b][:, 256:320], KRraw[0:64, ts_], self.IDF[0:64, 0:64], [('KRraw', tb), 'idf'], [('ps', b)])
            self.act(OST[s], self.ps[b][:, 0:320], AF.Copy, [], [('ps', b), ('ost', s)])
            P.op('sp', lambda e, s=s, ts_=ts_: e.dma_start(out=self.dr['ockv'][ts_, :], in_=OST[s][:, 0:256]), reads=[('ost', s)], dma='oa%d' % s)
            P.op('sp', lambda e, s=s, ts_=ts_: e.dma_start(out=self.dr['okr'][ts_, :], in_=OST[s][:, 256:320]), reads=[('ost', s)], dma='ob%d' % s)
        wuq = self.dr['mla_w_uq'][0]
        wukv = self.dr['mla_w_ukv'][0]
        suq = [self.wload(wuq[:, j * 512:(j + 1) * 512].rearrange("(k p) n -> p k n", p=128), rows=3, cols=512, slot=j) for j in range(3)]
        ssw = self.wload(self.dr['w_uq_sw'].rearrange("(k p) n -> p k n", p=128), rows=3, cols=512, slot=3)
        P.fence(('pool',))
        for i in range(2):
            P.op('pool', lambda e, i=i: e.dma_start(out=QR[i][64:68, 0:1024], in_=self.dr['onehot']), writes=[('QRm', i)], dma='m%d' % (2 + i))

        def uq(col0, width):
            j, o = divmod(col0, 512)
            assert o + width <= 512
            return suq[j][0], (lambda k: suq[j][1][:, k, o:o + width])
        scale = float((128 + 64) ** -0.5)
        groups = [(0, 8, [(0, 512), (512, 512), (1536, 512)], 68),
                  (8, 2, [(1024, 256)], 64), (10, 2, [(1280, 256)], 64)]
        for h in range(8):
            hb = h % 2
            swv = self.wload(wukv[:, h * 256:(h + 1) * 256].rearrange("(k p) n -> p k n", p=128), rows=2, cols=256, slot=4 + hb)
            for kb in range(4):
                b = self.bank()
                ks = slice(kb * 512, (kb + 1) * 512)
                for c in range(2):
                    self.mm(self.ps[b], swv[1][:, c, 0:128], CKVb[:, c, ks], c == 0, c == 1,
                            [('wf', swv[0])] + [('CKVb', i) for i in range(4)] + [('CKV', c, i) for i in range(3)], [('ps', b)])
                if kb % 2 == 0:
                    self.act(KT[hb][:, ks], self.ps[b], AF.Copy, [], [('ps', b), ('KT', hb)])
                else:
                    self.dve(lambda e, hb=hb, ks=ks, b=b: e.tensor_copy(out=KT[hb][:, ks], in_=self.ps[b]), [], [('ps', b), ('KT', hb)])
            for kq in range(4):
                b = self.bank()
                for q in range(4):
                    kt = kq * 4 + q
                    for c in range(2):
                        self.mm(self.ps[b][:, q * 128:(q + 1) * 128], CKVb[:, c, kt * 128:(kt + 1) * 128], swv[1][:, c, 128:256],
                                c == 0, c == 1, [('wf', swv[0])], [('ps', b)])
                src = self.ps[b].rearrange("p (q d) -> p q d", q=4)
                if kq % 2 == 0:
                    self.dve(lambda e, hb=hb, kq=kq, src=src: e.tensor_copy(out=VH[hb][:, kq * 4:(kq + 1) * 4, :], in_=src), [],
                             [('ps', b), ('VH', hb)])
                else:
                    self.act(VH[hb][:, kq * 4:(kq + 1) * 4, :], src, AF.Copy, [], [('ps', b), ('VH', hb)])
            for tb in range(NTB):
                sl = slice(tb * TB, (tb + 1) * TB)
                b = self.bank()
                slot, lf = uq(h * 192, 128) if (h * 192) % 512 + 128 <= 512 else (None, None)
                for c in range(3):
                    if slot is not None:
                        self.mm(self.ps[b], lf(c), QN[:, c, sl], c == 0, c == 2, [('wf', slot), ('QN', c, tb)], [('ps', b)])
                    else:
                        j0, o0 = divmod(h * 192, 512)
                        w0 = 512 - o0
                        self.mm(self.ps[b][0:w0, :], suq[j0][1][:, c, o0:512], QN[:, c, sl], c == 0, c == 2,
                                [('wf', suq[j0][0]), ('QN', c, tb)], [('ps', b)])
                if slot is None:
                    for c in range(3):
                        self.mm(self.ps[b][w0:128, :], suq[j0 + 1][1][:, c, 0:128 - w0], QN[:, c, sl], c == 0, c == 2,
                                [('wf', suq[j0 + 1][0]), ('QN', c, tb)], [('ps', b)])
                self.act(QT[:, sl], self.ps[b], AF.Copy, [], [('ps', b), ('QT', tb)])
                br = self.bank()
                sr, lr = uq(h * 192 + 128, 64)
                for c in range(3):
                    self.mm(self.ps[br][0:64, :], lr(c), QN[:, c, sl], c == 0, c == 2, [('wf', sr), ('QN', c, tb)], [('ps', br)])
                if tb < 2:
                    bsw = self.bank()
                    for c in range(3):
                        self.mm(self.ps[bsw][0:64, :], ssw[1][:, c, h * 64:(h + 1) * 64], QN[:, c, sl], c == 0, c == 2,
                                [('wf', ssw[0]), ('QN', c, tb)], [('ps', bsw)])
                    t1, t2 = self.tmp(), self.tmp()
                    a1, a2 = self.TMP[t1][0:64, :], self.TMP[t2][0:64, :]
                    self.dve(lambda e, a1=a1, sl=sl, br=br: e.tensor_tensor(out=a1, in0=self.ps[br][0:64, :], in1=ROC[0:64, sl], op=ALU.mult),
                             [('wa', 0)], [('ps', br), ('tmp', t1)])
                    self.dve(lambda e, a2=a2, sl=sl, bsw=bsw: e.tensor_tensor(out=a2, in0=self.ps[bsw][0:64, :], in1=ROS[0:64, sl], op=ALU.mult),
                             [('wa', 1)], [('ps', bsw), ('tmp', t2)])
                    self.dve(lambda e, a1=a1, a2=a2, sl=sl, hb=hb: e.tensor_tensor(out=QR[hb][0:64, sl], in0=a1, in1=a2, op=ALU.add),
                             [('tmp', t1), ('tmp', t2)], [('QR', hb, tb)])
                else:
                    self.act(QR[hb][0:64, sl], self.ps[br][0:64, :], AF.Copy, [], [('ps', br), ('QR', hb, tb)])
            items = []
            kbA = [(0, 512), (512, 512), (1536, 512)]
            ktA = [0, 1, 2, 3, 4, 5, 6, 7, 12, 13, 14, 15]
            for qb0 in (0, 4):
                for qi in range(4):
                    items.append(dict(qt=qb0 + qi, qi=qi, nq=4, qb0=qb0, kblocks=kbA, krows=68, ptoff=0, ktiles=ktA,
                                      first=(qi == 0), last=(qi == 3), zero=False))
            for qi in range(4):
                kb = [(1024, 256)] if qi < 2 else [(1280, 256)]
                items.append(dict(qt=8 + qi, qi=qi, nq=4, qb0=8, kblocks=kb, krows=64, ptoff=0 if qi < 2 else 2,
                                  ktiles=[8, 9, 10, 11], first=(qi == 0), last=(qi == 3), zero=(qi == 0)))

            def stage_qk(it):
                qt = it['qt']
                qs = slice(qt * 128, (qt + 1) * 128)
                tbq = qt // 4
                it['banks'] = []
                for bi_, (k0, kw) in enumerate(it['kblocks']):
                    b = (qt % 2) * 3 + bi_
                    it['banks'].append(b)
                    self.mm(self.ps[b][:, 0:kw], QT[:, qs], KT[hb][:, k0:k0 + kw], True, False,
                            [('QT', tbq), ('KT', hb)], [('ps', b)])
                    rdm = ['KRm0', 'KRm1', ('QRm', hb)] if it['krows'] == 68 else []
                    self.mm(self.ps[b][:, 0:kw], QR[hb][0:it['krows'], qs], KRb[0:it['krows'], k0:k0 + kw], False, True,
                            [('QR', hb, tbq)] + [('KRb', i) for i in range(4)] + rdm, [('ps', b)])

            def stage_max(it):
                qt = it['qt']
                par = qt % 2
                so = par * 8
                banks = it['banks']
                nb = len(banks)
                for bi, b in enumerate(banks):
                    kw = it['kblocks'][bi][1]
                    self.dve(lambda e, bi=bi, b=b, kw=kw, so=so: e.reduce_max(out=self.SM[:, so + bi:so + bi + 1], in_=self.ps[b][:, 0:kw],
                                                                           axis=mybir.AxisListType.X), [], [('ps', b), ('sm_mx', par)])
                if nb > 1:
                    self.dve(lambda e, so=so, nb=nb: e.reduce_max(out=self.SM[:, so + 3:so + 4], in_=self.SM[:, so:so + nb],
                                                                  axis=mybir.AxisListType.X), [], [('sm_mx', par)])
                    src_c = so + 3
                else:
                    src_c = so
                self.dve(lambda e, so=so, src_c=src_c: e.tensor_scalar_mul(out=self.SM[:, so + 4:so + 5], in0=self.SM[:, src_c:src_c + 1],
                                                                        scalar1=-scale), [], [('sm_mx', par), ('sm_nb', par)])

            def stage_exp(it):
                qt = it['qt']
                par = qt % 2
                so = par * 8
                pp = PP[par]
                ko = 0
                for bi, b in enumerate(it['banks']):
                    kw = it['kblocks'][bi][1]
                    self.act(pp[:, ko:ko + kw], self.ps[b][:, 0:kw], AF.Exp, [('sm_nb', par)], [('ps', b), ('PP', par, bi)],
                             bias=self.SM[:, so + 4:so + 5], scale=scale)
                    ko += kw

            def stage_transpose(it):
                qt, qi = it['qt'], it['qi']
                pp = PP[qt % 2]
                nkt = sum(w for _, w in it['kblocks']) // 128
                po = it['ptoff']
                if it['zero']:
                    z1, z2 = PT[:, 2:4, 0:256], PT[:, 0:2, 256:512]
                    self.dve(lambda e, z1=z1: e.memset(z1, 0.0), [], [('PT', 0, 0), ('PT', 1, 0)])
                    self.dve(lambda e, z2=z2: e.memset(z2, 0.0), [], [('PT', 2, 0), ('PT', 3, 0)])
                for kq in range(0, nkt, 8):
                    b = 6 + kq // 8
                    pb = self.ps[b].bitcast(BF16)
                    n8 = min(8, nkt - kq)
                    for q in range(n8):
                        kt = kq + q
                        self.tr(pb[:, q * 128:(q + 1) * 128], pp[:, kt * 128:(kt + 1) * 128], self.IDB,
                                [('PP', qt % 2, kt // 4), 'idb'], [('ps', b)], inc=(q == n8 - 1))
                    src = pb[:, 0:n8 * 128].rearrange("p (q t) -> p q t", q=n8)
                    dst = PT[:, po + kq:po + kq + n8, qi * 128:(qi + 1) * 128]
                    if (kq // 8) % 2 == 0:
                        self.act(dst, src, AF.Copy, [], [('ps', b), ('PT', qi, kq // 8)])
                    else:
                        self.dve(lambda e, dst=dst, src=src: e.tensor_copy(out=dst, in_=src), [], [('ps', b), ('PT', qi, kq // 8)])

            def stage_pv(it):
                nq, qb0 = it['nq'], it['qb0']
                nqc = nq * 128
                ktiles = it['ktiles']
                bo, bsum = 6, 7
                rdpt = [('PT', q, g) for q in range(nq) for g in range((len(ktiles) + 7) // 8)]
                for i, ktg in enumerate(ktiles):
                    self.mm(self.ps[bo][:, 0:nqc], VH[hb][:, ktg, :], PT[:, i, 0:nqc], i == 0, i == len(ktiles) - 1,
                            [('VH', hb)] + rdpt, [('ps', bo)])
                for i, ktg in enumerate(ktiles):
                    self.mm(self.ps[bsum][:, 0:nqc], self.ONES, PT[:, i, 0:nqc], i == 0, i == len(ktiles) - 1,
                            ['ones'] + rdpt, [('ps', bsum)])
                self.dve(lambda e, bsum=bsum, nqc=nqc: e.reciprocal(out=RB[:, 0:nqc], in_=self.ps[bsum][:, 0:nqc]), [], [('ps', bsum), 'RB'])
                q0 = qb0 * 128
                tbo = q0 // TB
                dst = OT[:, h, q0:q0 + nqc]
                self.dve(lambda e, dst=dst, bo=bo, nqc=nqc: e.tensor_tensor(out=dst, in0=self.ps[bo][:, 0:nqc], in1=RB[:, 0:nqc], op=ALU.mult),
                         ['RB'], [('ps', bo), ('H', h, tbo)])

            stage_qk(items[0])
            stage_max(items[0])
            for i, it in enumerate(items):
                if i + 1 < len(items):
                    stage_qk(items[i + 1])
                stage_exp(it)
                if i + 1 < len(items):
                    stage_max(items[i + 1])
                stage_transpose(it)
                if it['last']:
                    stage_pv(it)
        if self.debug == 91:
            for c in range(KC):
                for tb in range(NTB):
                    sl = slice(tb * TB, (tb + 1) * TB)
                    self.act(self.X[:, c, sl], OT[:, c, sl], AF.Copy, [('H', c, tb)], [('X', c, tb)])
            return
        w_o = self.dr['mla_w_o'][0]
        rd_gate = self.ada_res(l, 5)
        self.wf_rr = 6
        for pair in range(4):
            so = self.wload(w_o[:, pair * 256:(pair + 1) * 256].rearrange("(k p) n -> p k n", p=128))
            for il in range(2):
                c = pair * 2 + il
                for tb in range(NTB):
                    sl = slice(tb * TB, (tb + 1) * TB)
                    b = self.bank()
                    for j in range(KC):
                        self.mm(self.ps[b], so[1][:, j, il * 128:(il + 1) * 128], OT[:, j, sl], j == 0, j == KC - 1,
                                [('wf', so[0]), ('H', j, tb)], [('ps', b)])
                    self.resid_ln(l, 5, b, c, tb, rd_gate)
        self.layernorm_all(CO_LNG + (l * 3 + 1) * 8, CO_LNB + (l * 3 + 1) * 8, post=post)

    def store_x(self, tbs=None):
        xo = [self.view(self.stage_off, (D,), F32), self.view(self.stage_off + 4096, (D,), F32)]
        tiles = range(T // 128) if tbs is None else [tt for tb in tbs for tt in range(4 * tb, 4 * tb + 4)]
        for tt in tiles:
            s = tt % 2
            for half in range(2):
                b = self.bank()
                for q in range(4):
                    c = half * 4 + q
                    self.tr(self.ps[b][:, q * 128:(q + 1) * 128], self.X[:, c, tt * 128:(tt + 1) * 128], self.IDF,
                            [('X', c, tt // 4), 'idf'], [('ps', b)], inc=(q == 3))
                out = xo[s][:, half * 512:(half + 1) * 512]
                if half == 0:
                    self.act(out, self.ps[b], AF.Copy, [], [('ps', b), ('xo', s, 0)])
                else:
                    self.dve(lambda e, out=out, b=b: e.tensor_copy(out=out, in_=self.ps[b]), [], [('ps', b), ('xo', s, 1)])
            dst = self.dr['y'][tt * 128:(tt + 1) * 128, :]
            src = xo[s]
            self.P.op('sp', lambda e, dst=dst, src=src: e.dma_start(out=dst, in_=src),
                      reads=[('xo', s, 0), ('xo', s, 1)], dma='yo%d' % s)

    def build(self):
        st = self.debug or 99
        self.constants()
        nada = [0]

        def ada_next(n=1):
            while n > 0 and nada[0] < 72:
                k = min(n, 72 - nada[0], 2)
                self.ada_units([((nada[0] + j) // 36, (nada[0] + j) % 36) for j in range(k)])
                nada[0] += k
                n -= k
        self.load_x(hook=lambda: ada_next(2))
        hook = lambda: ada_next(2)
        if st < 99:
            self.ffn(0, 0, 0, 0, ada_hook=hook)
            if st >= 7:
                self.P.fence()
                self.mixer0(0)
                self.P.fence()
            if st >= 8:
                self.ffn(0, 1, 6, 2, ada_hook=hook)
                ada_next(72)
                self.ffn(1, 0, 0, 3)
            if st >= 9:
                self.P.fence(('pe', 'act', 'dve', 'sp', 'pool'))
                self.mla(1)
                self.P.fence()
            if st >= 10 and st != 91:
                self.ffn(1, 1, 6, 5)
        else:
            self.ffn(0, 0, 0, 0, ada_hook=hook, post=lambda tb: self.modulate(0, 3, 4, [tb]))
            self.P.fence()
            self.mixer0(0, premod=True, post=lambda tb: self.modulate(0, 6, 7, [tb]))
            self.P.fence()
            self.ffn(0, 1, 6, 2, ada_hook=hook, premod=True, post=lambda tb: self.modulate(1, 0, 1, [tb]))
            ada_next(72)
            self.ffn(1, 0, 0, 3, premod=True, post=lambda tb: self.modulate(1, 3, 4, [tb]))
            self.P.fence(('pe', 'act', 'dve', 'sp', 'pool'))
            self.mla(1, premod=True, post=lambda tb: self.modulate(1, 6, 7, [tb]))
            self.P.fence()
            self.ffn(1, 1, 6, 5, premod=True, post=lambda tb: self.store_x([tb]))
            self.emit()
            return self.nc
        self.store_x()
        self.emit()
        return self.nc

    def emit(self):
        nc = self.nc
        P = self.P
        names = sorted(P.cnt.keys())
        sems = {}
        import contextlib
        with contextlib.ExitStack() as st:
            for n in names:
                sems[n] = st.enter_context(nc.semaphore("s_" + n))
            block = st.enter_context(nc.Block())
            finals = [(n, v) for n, v in P.cnt.items() if n not in ENGS]

            def run(e, key, final=False):
                for waits, fn, incspec in P.ops[key]:
                    for s, v in waits:
                        e.wait_ge(sems[s], v)
                    if fn is None:
                        continue
                    ins = fn(e)
                    if incspec is not None:
                        ins.then_inc(sems[incspec[0]], incspec[1])
                if final:
                    for n, v in finals:
                        e.wait_ge(sems[n], v)

            @block.tensor
            def _(e):
                run(e, 'pe')

            @block.scalar
            def _(e):
                run(e, 'act')

            @block.vector
            def _(e):
                run(e, 'dve')

            @block.gpsimd
            def _(e):
                run(e, 'pool')

            @block.sync
            def _(e):
                run(e, 'sp', final=True)
        print("ops:", {k: len(v) for k, v in P.ops.items()}, "sems:", len(names))


def _pack_vecs(inp, cond2, flag):
    v = np.zeros((128, NV), np.float32)

    def put(col, vec):
        vec = np.asarray(vec, np.float32)
        n = vec.shape[0] // 128
        v[:, col:col + n] = vec.reshape(n, 128).T
    for r in range(2):
        put(CO_COND + r * 8, cond2[r])
    for l in range(2):
        put(CO_BADA + l * 72, inp['b_ada'][l])
        for s in range(3):
            put(CO_LNG + (l * 3 + s) * 8, inp['ln_g'][l, s])
            put(CO_LNB + (l * 3 + s) * 8, inp['ln_b'][l, s])
    cw = np.asarray(inp['conv_w'][0], np.float32)
    for i in range(4):
        v[:, CO_CW + i * 31: CO_CW + (i + 1) * 31] = cw[:, i * 128:(i + 1) * 128].T
    put(CO_CB, inp['conv_b'][0]); put(CO_CNG, inp['conv_norm_g'][0]); put(CO_CNB, inp['conv_norm_b'][0])
    put(CO_PSC, inp['pool_scale'][0])
    put(CO_QG, inp['mla_q_norm_g'][0]); put(CO_KVG, inp['mla_kv_norm_g'][0])
    v[:, CO_FLAG] = flag
    return v


def _rope_tables(real):
    C = np.ones((64, 1024), np.float32)
    S = np.zeros((64, 1024), np.float32)
    if real:
        n = 1024
        row = np.repeat(np.arange(n // 64), 64).astype(np.float32)
        col = np.tile(np.arange(64), n // 64).astype(np.float32)
        inv = (10000.0 ** (-np.arange(16, dtype=np.float32) / 16)).astype(np.float32)
        ang = np.concatenate([row[:, None] * inv, col[:, None] * inv], -1).astype(np.float32)
        cos, sin = np.cos(ang), np.sin(ang)
        for a in range(2):
            for j in range(2):
                for p in range(16):
                    dd = a * 32 + j * 16 + p
                    C[dd] = cos[:, a * 16 + p]
                    S[dd] = (-sin[:, a * 16 + p]) if j == 0 else sin[:, a * 16 + p]
    return C, S


_NC_CACHE = {}


def _prep_inputs(inp):
    inp = {k: np.asarray(v) for k, v in inp.items()}
    xp = inp['x_prompt'].astype(np.float32)
    xs = inp['x_sample'].astype(np.float32)
    ident = np.eye(128, dtype=np.float32)
    onehot = np.zeros((4, 1024), np.float32)
    for j in range(4):
        onehot[j, j * 256:(j + 1) * 256] = 1.0
    perm = np.arange(64).reshape(2, 2, 16)[:, ::-1, :].reshape(64)
    w_uq = inp['mla_w_uq'][0]
    uq_r = w_uq.reshape(384, 8, 192)[:, :, 128:]
    w_uq_sw = np.ascontiguousarray(uq_r[:, :, perm].reshape(384, 512))
    w_dkv_sw = np.ascontiguousarray(inp['mla_w_dkv'][0][:, 256:][:, perm])
    shared = {k: np.ascontiguousarray(inp[k], dtype=np.float32) for k in
              ('w_ada', 'ffn_w1', 'ffn_w3', 'ffn_w2', 'cp_w_in', 'pool_w', 'cp_w_out', 'mla_w_dq', 'mla_w_uq',
               'mla_w_dkv', 'mla_w_ukv', 'mla_w_o')}
    shared.update(ident=ident, onehot=onehot, w_uq_sw=w_uq_sw, w_dkv_sw=w_dkv_sw)
    in_maps = []
    for r in range(8):
        if r < 4:
            x = np.concatenate([xs[r], xp[2 * r], xp[2 * r + 1]], 0)
            cond2 = np.stack([inp['c_ctx'], inp['c'][r]], 0)
            flag = 1.0
            cckv = inp['cache_mla_ckv'][r, 0]
            ckr = inp['cache_mla_krope'][r, 0]
            maskk = np.zeros((4, 1536), np.float32)
            C, S = _rope_tables(True)
        else:
            p0 = 8 + 6 * (r - 4)
            x = xp[p0:p0 + 6].reshape(T, D)
            cond2 = np.stack([inp['c_ctx'], inp['c_ctx']], 0)
            flag = 0.0
            cckv = np.zeros((512, 256), np.float32)
            ckr = np.zeros((512, 64), np.float32)
            maskk = np.full((4, 1536), NEG, np.float32)
            for j in range(4):
                maskk[j, j * 256:(j + 1) * 256] = 0.0
            C, S = _rope_tables(False)
        m = dict(shared)
        m.update(x=np.ascontiguousarray(x), vecs=_pack_vecs(inp, cond2, flag),
                 cache_ckv=np.ascontiguousarray(cckv, dtype=np.float32),
                 cache_kr=np.ascontiguousarray(ckr, dtype=np.float32), ropeC=C, ropeS=S, maskk=maskk)
        in_maps.append(m)
    return in_maps


def _assemble(results):
    y_p = np.zeros((32, 256, D), np.float32)
    y_s = np.zeros((4, 1024, D), np.float32)
    ckv = np.zeros((32, 1, 256, 256), np.float32)
    kr = np.zeros((32, 1, 256, 64), np.float32)
    for r in range(8):
        y = results[r]['y']
        ok = results[r]['ockv']
        okr = results[r]['okr']
        if r < 4:
            y_s[r] = y[:1024]
            for i in range(2):
                sl = slice(1024 + 256 * i, 1024 + 256 * (i + 1))
                y_p[2 * r + i] = y[sl]; ckv[2 * r + i, 0] = ok[sl]; kr[2 * r + i, 0] = okr[sl]
        else:
            p0 = 8 + 6 * (r - 4)
            for i in range(6):
                sl = slice(256 * i, 256 * (i + 1))
                y_p[p0 + i] = y[sl]; ckv[p0 + i, 0] = ok[sl]; kr[p0 + i, 0] = okr[sl]
    return y_p, y_s, ckv, kr


def kernel(**inputs):
    in_maps = _prep_inputs(inputs)
    if 'nc' not in _NC_CACHE:
        _NC_CACHE['nc'] = Builder().build()
    res = run_bass_kernel_spmd(_NC_CACHE['nc'], in_maps, core_ids=list(range(8)))
    return _assemble(res.results)
```

```python
import numpy as np
import concourse.bass as bass
import concourse.mybir as mybir
from concourse.bass_utils import run_bass_kernel_spmd

F32 = mybir.dt.float32
BF16 = mybir.dt.bfloat16
AF = mybir.ActivationFunctionType
ALU = mybir.AluOpType

T = 1536
D = 1024
KC = 8
TB = 512
NTB = 3
DFF = 2816
NJ = 22
ALPHA = float((2 * 2) ** 0.25)
LN_EPS = 1e-5
RMS_EPS = 1e-6
NEG = -30000.0

CO_COND = 0
CO_BADA = CO_COND + 16
CO_LNG = CO_BADA + 144
CO_LNB = CO_LNG + 48
CO_CW = CO_LNB + 48
CO_CB = CO_CW + 124
CO_CNG = CO_CB + 4
CO_CNB = CO_CNG + 4
CO_PSC = CO_CNB + 4
CO_QG = CO_PSC + 4
CO_KVG = CO_QG + 3
CO_FLAG = CO_KVG + 2
NV = CO_FLAG + 1

ENGS = ('pe', 'act', 'dve', 'pool', 'sp')


class Prog:
    def __init__(self):
        self.ops = {e: [] for e in ENGS}
        self.cnt = {}
        self.seen = {e: {} for e in ENGS}
        self.res = {}

    def op(self, eng, fn, reads=(), writes=(), inc=True, dma=None):
        need = {}

        def add(tok):
            if tok is not None:
                s, v = tok
                if need.get(s, 0) < v:
                    need[s] = v
        for r in reads:
            e = self.res.get(r)
            if e is not None:
                add(e[0])
        for w in writes:
            e = self.res.get(w)
            if e is not None:
                add(e[0])
                for s, v in e[1].items():
                    add((s, v))
        if dma is not None and self.cnt.get(dma, 0) > 0:
            add((dma, self.cnt[dma]))
        waits = []
        for s, v in need.items():
            if eng == 'pe' and s == 'pe':
                continue
            if self.seen[eng].get(s, 0) >= v:
                continue
            self.seen[eng][s] = v
            waits.append((s, v))
        if dma is not None:
            self.cnt[dma] = self.cnt.get(dma, 0) + 16
            tok = (dma, self.cnt[dma])
            incspec = (dma, 16)
        else:
            before = self.cnt.get(eng, 0)
            tok = (eng, before + 1)
            if inc:
                self.cnt[eng] = before + 1
                incspec = (eng, 1)
            else:
                incspec = None
        for r in reads:
            e = self.res.setdefault(r, [None, {}])
            if e[1].get(tok[0], 0) < tok[1]:
                e[1][tok[0]] = tok[1]
        for w in writes:
            self.res[w] = [tok, {}]
        self.ops[eng].append((waits, fn, incspec))


    def fence(self, engines=('pe', 'act', 'dve', 'sp')):
        for eng in engines:
            waits = []
            for s, v in self.cnt.items():
                if v == 0 or (eng == 'pe' and s == 'pe'):
                    continue
                if self.seen[eng].get(s, 0) >= v:
                    continue
                self.seen[eng][s] = v
                waits.append((s, v))
            if waits:
                self.ops[eng].append((waits, None, None))


class Builder:
    def __init__(self, debug=None):
        self.debug = debug
        nc = self.nc = bass.Bass("TRN2", target_bir_lowering=False)
        self.P = Prog()
        self.dr = {}

        def din(name, shape):
            self.dr[name] = nc.dram_tensor(name, list(shape), F32, kind="ExternalInput").ap()

        def dout(name, shape):
            self.dr[name] = nc.dram_tensor(name, list(shape), F32, kind="ExternalOutput").ap()
        din('x', [T, D]); din('vecs', [128, NV]); din('ident', [128, 128])
        din('cache_ckv', [512, 256]); din('cache_kr', [512, 64])
        din('ropeC', [64, 1024]); din('ropeS', [64, 1024])
        din('maskk', [4, 1536]); din('onehot', [4, 1024])
        din('w_ada', [2, 1024, 9216])
        din('ffn_w1', [2, 2, 1024, DFF]); din('ffn_w3', [2, 2, 1024, DFF]); din('ffn_w2', [2, 2, DFF, 1024])
        din('cp_w_in', [1, 1024, 1536]); din('pool_w', [1, 4, 128, 128]); din('cp_w_out', [1, 1024, 1024])
        din('mla_w_dq', [1, 1024, 384]); din('mla_w_uq', [1, 384, 1536]); din('w_uq_sw', [384, 512])
        din('mla_w_dkv', [1, 1024, 320]); din('w_dkv_sw', [1024, 64])
        din('mla_w_ukv', [1, 256, 2048]); din('mla_w_o', [1, 1024, 1024])
        dout('y', [T, D]); dout('ockv', [T, 256]); dout('okr', [T, 64])

        self.big = nc.alloc_sbuf_tensor("big", [128, 103 * 1024], BF16)
        self.off = 0
        self.X = self.alloc((KC, T), F32)
        self.H = self.alloc((KC, T), BF16)
        self.g_off = self.off
        self.G = self.alloc((NJ, T), BF16)
        self.g_end = self.off
        self.WF = [self.alloc((2048,), BF16) for _ in range(8)]
        self.WA = [self.alloc((8, 256), BF16) for _ in range(2)]
        self.VEC = self.alloc((NV,), F32)
        self.ADAT = self.alloc((2, 72, 2), F32)
        self.IDF = self.alloc((128,), F32)
        self.IDB = self.alloc((128,), BF16)
        self.ONES = self.alloc((128,), BF16)
        self.SC = self.alloc((8, 2), BF16)
        self.SM = self.alloc((16,), F32)
        self.DRF = self.alloc((128,), F32)
        self.ONESF = self.alloc((128,), F32)
        self.rb_bank = 7
        self.EPS_LN = self.alloc((1,), F32)
        self.EPS_RMS = self.alloc((1,), F32)
        self.ST = [self.alloc((TB,), F32) for _ in range(4)]
        self.TMP = [self.alloc((TB,), F32) for _ in range(4)]
        self.XB = [self.alloc((TB,), BF16) for _ in range(4)]
        print("sbuf used bytes/partition:", self.off)
        assert self.off <= 206 * 1024
        self.ps = [nc.alloc_psum_tensor("ps%d" % b, [128, 512], F32)[:, :] for b in range(8)]
        self.stage_off = self.g_end - 8192
        self.bank_rr = 0
        self.nbank_rr = 7
        self.live_banks = set()
        self.wf_rr = 0
        self.wa_rr = 0
        self.tmp_rr = 0
        self.xb_rr = 0
        self.out_rr = 0

    def view(self, off, shape, dtype):
        n = int(np.prod(shape))
        esz = 4 if dtype == F32 else 2
        assert off % 4 == 0
        ap = self.big[:, off // 2: (off + n * esz) // 2]
        if dtype == F32:
            ap = ap.bitcast(F32)
        if len(shape) == 2:
            ap = ap.rearrange("p (a b) -> p a b", a=shape[0])
        elif len(shape) == 3:
            ap = ap.rearrange("p (a b c) -> p a b c", a=shape[0], b=shape[1])
        return ap

    def alloc(self, shape, dtype):
        n = int(np.prod(shape))
        esz = 4 if dtype == F32 else 2
        off = (self.off + 31) // 32 * 32
        ap = self.view(off, shape, dtype)
        self.off = off + n * esz
        return ap

    def bank(self):
        while True:
            b = self.bank_rr
            self.bank_rr = (b + 1) % self.nbank_rr
            if b not in self.live_banks:
                return b

    def tmp(self):
        i = self.tmp_rr
        self.tmp_rr = (i + 1) % 4
        return i

    def act(self, out, in_, func, reads, writes, **kw):
        self.P.op('act', lambda e: e.activation(out=out, in_=in_, func=func, **kw), reads, writes)

    def dve(self, fn, reads, writes):
        self.P.op('dve', fn, reads, writes)

    def mm(self, out, lhsT, rhs, start, stop, reads, writes, inc=None):
        self.P.op('pe', lambda e: e.matmul(out, lhsT=lhsT, rhs=rhs, start=start, stop=stop),
                  reads, writes, inc=(stop if inc is None else inc))

    def tr(self, out, in_, ident, reads, writes, inc=True):
        self.P.op('pe', lambda e: e.transpose(out=out, in_=in_, identity=ident), reads, writes, inc=inc)

    def wload(self, src_ap, rows=8, cols=256, slot=None):
        if slot is None:
            i = self.wf_rr
            self.wf_rr = (i + 1) % 8
        else:
            i = slot
        dst = self.WF[i][:, 0:rows * cols].rearrange("p (a b) -> p a b", a=rows)
        self.P.op('pool', lambda e: e.dma_start(out=dst, in_=src_ap), reads=(), writes=[('wf', i)], dma='wf%d' % i)
        return (i, dst)

    def vcol(self, col, n=1):
        return self.VEC[:, col:col + n]

    def constants(self):
        P = self.P
        P.op('sp', lambda e: e.dma_start(out=self.VEC, in_=self.dr['vecs']), writes=['vec'], dma='c0')
        P.op('sp', lambda e: e.dma_start(out=self.IDF, in_=self.dr['ident']), writes=['idf'], dma='c1')
        self.dve(lambda e: e.tensor_copy(out=self.IDB, in_=self.IDF), ['idf'], ['idb'])
        self.dve(lambda e: e.memset(self.ONES, 1.0), [], ['ones'])
        self.dve(lambda e: e.memset(self.ONESF, 1.0), [], ['onesf'])
        self.dve(lambda e: e.memset(self.EPS_LN, LN_EPS), [], ['eps'])
        self.dve(lambda e: e.memset(self.EPS_RMS, RMS_EPS), [], ['eps2'])
        cond = self.VEC[:, CO_COND:CO_COND + 16].rearrange("p (r c) -> p c r", r=2)
        self.act(self.SC, cond, AF.Silu, ['vec'], ['sc'])

    def load_x(self, hook=None):
        xin = [self.view(self.stage_off - 8192 + i * 4096, (D,), F32) for i in range(4)]
        for tt in range(T // 128):
            if hook is not None and tt in (1, 4, 7, 10):
                hook()
            s = tt % 4
            src = self.dr['x'][tt * 128:(tt + 1) * 128, :]
            dst = xin[s]
            self.P.op('sp', lambda e, dst=dst, src=src: e.dma_start(out=dst, in_=src), writes=[('xin', s)], dma='xin%d' % s)
            for half in range(2):
                b = self.bank()
                for q in range(4):
                    c = half * 4 + q
                    self.tr(self.ps[b][:, q * 128:(q + 1) * 128], xin[s][:, c * 128:(c + 1) * 128], self.IDF,
                            [('xin', s), 'idf'], [('ps', b)], inc=(q == 3))
                src_ps = self.ps[b][:, :].rearrange("p (q t) -> p q t", q=4)
                out = self.X[:, half * 4:half * 4 + 4, tt * 128:(tt + 1) * 128]
                wr = [('X', half * 4 + q, tt // 4) for q in range(4)]
                if half == 0:
                    self.act(out, src_ps, AF.Copy, [('ps', b)], [('ps', b)] + wr)
                else:
                    self.dve(lambda e, out=out, src_ps=src_ps: e.tensor_copy(out=out, in_=src_ps), [], [('ps', b)] + wr)

    def ada_units(self, units):
        b = 7
        for (l, u) in units:
            i = self.wa_rr
            self.wa_rr = (i + 1) % 2
            src = self.dr['w_ada'][l, :, u * 256:(u + 1) * 256].rearrange("(k p) n -> p k n", p=128)
            dst = self.WA[i]
            self.P.op('pool', lambda e, dst=dst, src=src: e.dma_start(out=dst, in_=src), writes=[('wa', i)], dma='wa%d' % i)
            for fl in range(2):
                fc = u * 2 + fl
                for k in range(KC):
                    self.mm(self.ps[b][:, fc * 2:fc * 2 + 2], self.WA[i][:, k, fl * 128:(fl + 1) * 128], self.SC[:, k, :],
                            k == 0, k == KC - 1, [('wa', i), 'sc'], [('ps', b)])
        for (l, u) in units:
            m = (u * 2) // 8
            for r in range(2):
                out = self.ADAT[:, l, u * 2:u * 2 + 2, r]
                in0 = self.ps[b][:, u * 4:u * 4 + 4].rearrange("p (f r) -> p f r", r=2)[:, :, r]
                in1 = self.VEC[:, CO_BADA + l * 72 + u * 2: CO_BADA + l * 72 + u * 2 + 2]
                self.dve(lambda e, out=out, in0=in0, in1=in1: e.tensor_tensor(out=out, in0=in0, in1=in1, op=ALU.add),
                         ['vec'], [('ps', b), ('ada', l, u)])
            out = self.ADAT[:, l, u * 2:u * 2 + 2, :]
            if m in (1, 4, 7):
                self.dve(lambda e, out=out: e.tensor_scalar_add(out=out, in0=out, scalar1=1.0), [], [('ada', l, u)])
            elif m in (2, 8):
                self.dve(lambda e, out=out: e.tensor_scalar_mul(out=out, in0=out, scalar1=0.5), [], [('ada', l, u)])

    def ada_res(self, l, m):
        return [('ada', l, m * 4 + j) for j in range(4)]

    def mod(self, l, m, c, r):
        return self.ADAT[:, l, m * 8 + c, r:r + 1]

    @staticmethod
    def cond_of(tb):
        return 1 if tb < 2 else 0

    def modulate(self, l, m_shift, m_scale, tbs=None):
        rd_ada = self.ada_res(l, m_shift) + self.ada_res(l, m_scale)
        for tb in (range(NTB) if tbs is None else tbs):
            r = self.cond_of(tb)
            for c in range(KC):
                out = self.H[:, c, tb * TB:(tb + 1) * TB]
                in_ = self.X[:, c, tb * TB:(tb + 1) * TB]
                sc = self.mod(l, m_scale, c, r)
                sh = self.mod(l, m_shift, c, r)
                if c % 2 == 0:
                    self.act(out, in_, AF.Identity, [('X', c, tb)] + rd_ada, [('H', c, tb)], scale=sc, bias=sh)
                else:
                    self.dve(lambda e, out=out, in_=in_, sc=sc, sh=sh: e.tensor_scalar(
                        out=out, in0=in_, scalar1=sc, scalar2=sh, op0=ALU.mult, op1=ALU.add),
                        [('X', c, tb)] + rd_ada, [('H', c, tb)])

    def _ln_args(self, tb, src, skey, dst, dkey, func):
        sl = slice(tb * TB, (tb + 1) * TB)
        if src is None:
            src = lambda c: self.X[:, c, sl]
            skey = lambda c: ('X', c, tb)
        if dst is None:
            dst, dkey = src, skey
        if func is None:
            func = AF.Identity
        return src, skey, dst, dkey, func

    def ln_stats(self, tb, nch=KC, src=None, skey=None):
        src, skey, _, _, _ = self._ln_args(tb, src, skey, None, None, None)
        bs, bq = self.bank(), self.bank()
        for c in range(nch):
            i = (self.xb_rr // 2 * 2) % 4
            self.xb_rr = (i + 2) % 4
            xq = self.XB[i + 1]
            self.act(xq, src(c), AF.Square, [skey(c)], [('xb', i + 1)])
            self.mm(self.ps[bs], self.ONESF, src(c), c == 0, c == nch - 1, ['onesf', skey(c)], [('ps', bs)], inc=True)
            self.mm(self.ps[bq], self.ONES, xq, c == 0, c == nch - 1, ['ones', ('xb', i + 1)], [('ps', bq)], inc=True)
        return bs, bq

    def ln_apply(self, tb, banks, g_col, b_col, nch=KC, src=None, skey=None, dst=None, dkey=None, func=None):
        src, skey, dst, dkey, func = self._ln_args(tb, src, skey, dst, dkey, func)
        bs, bq = banks
        mean, msq, var, rstd = self.ST
        n = float(nch * 128)
        self.dve(lambda e: e.tensor_scalar_mul(out=mean, in0=self.ps[bs], scalar1=1.0 / n), [], [('ps', bs), ('st', 0)])
        self.dve(lambda e: e.tensor_tensor(out=msq, in0=mean, in1=mean, op=ALU.mult), [('st', 0)], [('st', 1)])
        self.dve(lambda e: e.scalar_tensor_tensor(out=var, in0=self.ps[bq], scalar=1.0 / n, in1=msq,
                                                 op0=ALU.mult, op1=ALU.subtract), [('st', 1)], [('ps', bq), ('st', 2)])
        self.act(var, var, AF.Ln, ['eps'], [('st', 2)], bias=self.EPS_LN, scale=1.0)
        self.act(rstd, var, AF.Exp, [('st', 2)], [('st', 3)], scale=-0.5)
        for c in range(nch):
            xs = src(c)
            t = self.tmp()
            tt_ = self.TMP[t]
            self.dve(lambda e, xs=xs, tt_=tt_: e.tensor_tensor(out=tt_, in0=xs, in1=mean, op=ALU.subtract),
                     [skey(c), ('st', 0)], [('tmp', t)])
            self.dve(lambda e, tt_=tt_: e.tensor_tensor(out=tt_, in0=tt_, in1=rstd, op=ALU.mult),
                     [('st', 3)], [('tmp', t)])
            self.act(dst(c), tt_, func, [('tmp', t), 'vec'], [dkey(c)],
                     scale=self.vcol(g_col + c), bias=self.vcol(b_col + c))

    def layernorm_all(self, g_col, b_col, post=None, nch=KC, mk=None):
        kw = [(mk(tb) if mk is not None else {}) for tb in range(NTB)]
        banks = {}

        def stats(tb):
            banks[tb] = self.ln_stats(tb, nch=nch, src=kw[tb].get('src'), skey=kw[tb].get('skey'))
            self.live_banks.update(banks[tb])

        def apply(tb):
            self.ln_apply(tb, banks[tb], g_col, b_col, nch=nch, **kw[tb])
            self.live_banks.difference_update(banks[tb])
            if post is not None:
                post(tb)
        stats(0); stats(1); apply(0); stats(2); apply(1); apply(2)

    def ffn(self, l, f, m0, ln_idx, ada_hook=None, parts=3, premod=False, post=None):
        if not premod:
            self.modulate(l, m0, m0 + 1)
        w1 = self.dr['ffn_w1'][l, f]
        w3 = self.dr['ffn_w3'][l, f]
        w2 = self.dr['ffn_w2'][l, f]
        loads = []
        for ng in range(NJ // 2):
            loads.append((w1[:, ng * 256:(ng + 1) * 256].rearrange("(k p) n -> p k n", p=128), 8, 256))
            loads.append((w3[:, ng * 256:(ng + 1) * 256].rearrange("(k p) n -> p k n", p=128), 8, 256))
        for c in range(KC):
            loads.append((w2[0:11 * 128, c * 128:(c + 1) * 128].rearrange("(j p) n -> p j n", p=128), 11, 128))
            loads.append((w2[11 * 128:22 * 128, c * 128:(c + 1) * 128].rearrange("(j p) n -> p j n", p=128), 11, 128))
        slots = []

        def issue(upto):
            while len(slots) <= min(upto, len(loads) - 1):
                a, rws, cls = loads[len(slots)]
                slots.append(self.wload(a, rows=rws, cols=cls))
        for ng in range(NJ // 2):
            issue(min(2 * (ng + 2) + 1, 21 if parts < 2 else 99))
            s1, s3 = slots[2 * ng], slots[2 * ng + 1]
            for jl in range(2):
                j = ng * 2 + jl
                for tb in range(NTB):
                    sl = slice(tb * TB, (tb + 1) * TB)
                    ba, bb = self.bank(), self.bank()
                    for (bk, sw) in ((ba, s1), (bb, s3)):
                        for k in range(KC):
                            self.mm(self.ps[bk], sw[1][:, k, jl * 128:(jl + 1) * 128], self.H[:, k, sl],
                                    k == 0, k == KC - 1, [('wf', sw[0]), ('H', k, tb)], [('ps', bk)])
                    t = self.tmp()
                    s_ = self.TMP[t]
                    self.act(s_, self.ps[ba], AF.Silu, [], [('ps', ba), ('tmp', t)])
                    out = self.G[:, j, sl]
                    self.dve(lambda e, out=out, s_=s_, bb=bb: e.tensor_tensor(out=out, in0=self.ps[bb], in1=s_, op=ALU.mult),
                             [('tmp', t)], [('ps', bb), ('G', j, tb)])
            if ada_hook is not None:
                ada_hook()
        if parts < 2:
            return
        rd_gate = self.ada_res(l, m0 + 2)
        for c in range(KC):
            issue(22 + 2 * (c + 2) + 1)
            sa, sb = slots[22 + 2 * c], slots[22 + 2 * c + 1]
            for tb in range(NTB):
                sl = slice(tb * TB, (tb + 1) * TB)
                r = self.cond_of(tb)
                b = self.bank()
                for j in range(NJ):
                    sw = sa if j < 11 else sb
                    self.mm(self.ps[b], sw[1][:, j % 11, :], self.G[:, j, sl],
                            j == 0, j == NJ - 1, [('wf', sw[0]), ('G', j, tb)], [('ps', b)])
                t = self.tmp()
                y_ = self.TMP[t]
                self.act(y_, self.ps[b], AF.Identity, rd_gate, [('ps', b), ('tmp', t)], scale=self.mod(l, m0 + 2, c, r))
                xs = self.X[:, c, sl]
                self.dve(lambda e, xs=xs, y_=y_: e.scalar_tensor_tensor(out=xs, in0=xs, scalar=ALPHA, in1=y_,
                                                                       op0=ALU.mult, op1=ALU.add),
                         [('tmp', t)], [('X', c, tb)])
            if ada_hook is not None:
                ada_hook()
        if parts < 3:
            return
        self.layernorm_all(CO_LNG + ln_idx * 8, CO_LNB + ln_idx * 8, post=post)

    def resid_ln(self, l, m_gate, ps_b, c, tb, rd_gate):
        sl = slice(tb * TB, (tb + 1) * TB)
        r = self.cond_of(tb)
        t = self.tmp()
        y_ = self.TMP[t]
        self.act(y_, self.ps[ps_b], AF.Identity, rd_gate, [('ps', ps_b), ('tmp', t)], scale=self.mod(l, m_gate, c, r))
        xs = self.X[:, c, sl]
        self.dve(lambda e, xs=xs, y_=y_: e.scalar_tensor_tensor(out=xs, in0=xs, scalar=ALPHA, in1=y_,
                                                               op0=ALU.mult, op1=ALU.add),
                 [('tmp', t)], [('X', c, tb)])

    def mixer0(self, l=0, premod=False, post=None):
        if not premod:
            self.modulate(l, 3, 4)
        go = self.g_off
        CB = self.view(go, (4, 6, 286), BF16)
        DT = self.view(go + 13824, (4, T), BF16)
        CO = self.view(go + 26112, (4, T), F32)
        DG = [self.view(go + 50688 + i * 7936, (31, 128), BF16) for i in range(2)]
        PA = self.view(go + 26112, (2, 6, 271), F32)
        PBf = self.view(go + 26112 + 13024, (2, 6, 271), F32)
        HPs = [self.view(go + 26112 + 26048 + i * 6144, (6, 256), F32) for i in range(2)]
        AB = self.H
        flag = self.vcol(CO_FLAG)
        w_in = self.dr['cp_w_in'][0]

        def win(c0):
            return self.wload(w_in[:, c0:c0 + 256].rearrange("(k p) n -> p k n", p=128))
        self.dve(lambda e: e.memset(CB, 0.0), [], ['CB'])
        for pair in range(2):
            su = win(pair * 256)
            sg = win(512 + pair * 256)
            for il in range(2):
                i = pair * 2 + il
                for tb in range(NTB):
                    sl = slice(tb * TB, (tb + 1) * TB)
                    bu, bg = self.bank(), self.bank()
                    for (bk, sw) in ((bu, su), (bg, sg)):
                        for k in range(KC):
                            self.mm(self.ps[bk], sw[1][:, k, il * 128:(il + 1) * 128], self.H[:, k, sl],
                                    k == 0, k == KC - 1, [('wf', sw[0]), ('H', k, tb)], [('ps', bk)])
                    t = self.tmp()
                    s_ = self.TMP[t]
                    self.act(s_, self.ps[bg], AF.Sigmoid, [], [('ps', bg), ('tmp', t)])
                    out = CB[:, i, 2 * tb:2 * tb + 2, 15:271]
                    in0 = self.ps[bu].rearrange("p (s u) -> p s u", s=2)
                    in1 = s_.rearrange("p (s u) -> p s u", s=2)
                    self.dve(lambda e, out=out, in0=in0, in1=in1: e.tensor_tensor(out=out, in0=in0, in1=in1, op=ALU.mult),
                             [('tmp', t), 'CB'], [('ps', bu), ('CBc', i, tb)])
        for i in range(4):
            cbk = [('CBc', i, tb) for tb in range(NTB)]
            o1, i1 = CB[:, i, 1:4, 0:15], CB[:, i, 0:3, 256:271]
            o2, i2 = CB[:, i, 0:3, 271:286], CB[:, i, 1:4, 15:30]
            self.dve(lambda e, o1=o1, i1=i1: e.tensor_scalar_mul(out=o1, in0=i1, scalar1=flag), cbk + ['vec'], [('CBh', i, 0)])
            self.dve(lambda e, o2=o2, i2=i2: e.tensor_scalar_mul(out=o2, in0=i2, scalar1=flag), cbk + ['vec'], [('CBh', i, 1)])
        spw = self.wload(self.dr['pool_w'][0].rearrange("g p n -> p g n"), rows=4, cols=128)
        PDs = [[PA[:, 0], PBf[:, 0]], [PA[:, 1], PBf[:, 1]]]
        ICE = self.ST[0][:, 0:384].rearrange("p (g s e) -> p g s e", g=4, s=6)
        STEPS = [(1, 0, 1, 271, 0, 1), (0, 1, 2, 270, 1, 3), (1, 0, 4, 268, 2, 6), (0, 1, 8, 264, 4, 12)]

        def halo_zero(buf, key):
            self.dve(lambda e: e.memset(buf[:, :, 0:8], 0.0), [], [key])
            self.dve(lambda e: e.memset(buf[:, :, 264:271], 0.0), [], [key])

        def halo_flag(buf, rd, key):
            self.dve(lambda e: e.tensor_scalar_mul(out=buf[:, 1:4, 0:8], in0=buf[:, 0:3, 256:264], scalar1=flag), rd + ['vec'], [key])
            self.dve(lambda e: e.tensor_scalar_mul(out=buf[:, 0:3, 264:271], in0=buf[:, 1:4, 8:15], scalar1=flag), rd + ['vec'], [key])

        def run_steps(pr, nsteps, first_rd, on_step=None):
            PD = PDs[pr]
            rd = first_rd
            for si in range(nsteps):
                di, si_, lo, hi, a0_, a1_ = STEPS[si]
                n = hi - lo
                o, x0, x1 = PD[di][:, :, lo:hi], PD[si_][:, :, a0_:a0_ + n], PD[si_][:, :, a1_:a1_ + n]
                key = ('PS', pr, di)
                self.dve(lambda e, o=o, x0=x0, x1=x1: e.tensor_tensor(out=o, in0=x0, in1=x1, op=ALU.add), rd, [key])
                rd = [key]
                if on_step is not None:
                    on_step(si, PD[di], key)
            return rd
        PD = PDs[1]
        self.dve(lambda e: e.memset(PD[0], 0.0), [], [('PS', 1, 0)])
        self.dve(lambda e: e.memset(PD[0][:, :, 8:264], 1.0), [], [('PS', 1, 0)])
        halo_flag(PD[0], [], ('PS', 1, 0))

        def grab(si, buf, key):
            self.dve(lambda e: e.reciprocal(out=ICE[:, si, :, 0:8], in_=buf[:, :, 8:16]), [key], [('ICE', si)])
            self.dve(lambda e: e.reciprocal(out=ICE[:, si, :, 8:16], in_=buf[:, :, 256:264]), [key], [('ICE', si)])
        run_steps(1, 4, [('PS', 1, 0)], on_step=grab)
        for pair in range(2):
            sh = win(1024 + pair * 256)
            for il in range(2):
                gi = pair * 2 + il
                pr = gi % 2
                PD = PDs[pr]
                HP = HPs[pr]
                halo_zero(PD[0], ('PS', pr, 0))
                for tb in range(NTB):
                    sl = slice(tb * TB, (tb + 1) * TB)
                    b = self.bank()
                    for k in range(KC):
                        self.mm(self.ps[b], sh[1][:, k, il * 128:(il + 1) * 128], self.H[:, k, sl],
                                k == 0, k == KC - 1, [('wf', sh[0]), ('H', k, tb)], [('ps', b)])
                    src = self.ps[b].rearrange("p (s u) -> p s u", s=2)
                    self.act(PD[0][:, 2 * tb:2 * tb + 2, 8:264], src, AF.Copy, [], [('ps', b), ('PS', pr, 0)])
                    self.act(HP[:, 2 * tb:2 * tb + 2, :], src, AF.Copy, [], [('ps', b), ('HP', pr, tb)])
                halo_flag(PD[0], [], ('PS', pr, 0))
                rd = run_steps(pr, gi + 1, [('PS', pr, 0)])
                fin = PD[STEPS[gi][0]]
                fkey = ('PS', pr, STEPS[gi][0])
                w_ = float(2 ** (gi + 1))
                sd = fin[:, :, 8:264]
                dt = DT[:, gi, :].rearrange("p (s u) -> p s u", s=6)
                hpk = [('HP', pr, tb) for tb in range(NTB)]
                self.dve(lambda e, dt=dt, sd=sd, w_=w_, HP=HP: e.scalar_tensor_tensor(out=dt, in0=sd, scalar=1.0 / w_, in1=HP,
                                                                            op0=ALU.mult, op1=ALU.subtract),
                         [fkey] + hpk, [('DT', gi)])
                for (c0, e0) in ((0, 0), (248, 8)):
                    se = fin[:, :, 8 + c0:16 + c0]
                    ie = ICE[:, gi, :, e0:e0 + 8]
                    de = dt[:, :, c0:c0 + 8]
                    he = HP[:, :, c0:c0 + 8]
                    self.dve(lambda e, se=se, ie=ie: e.tensor_tensor(out=se, in0=se, in1=ie, op=ALU.mult), [('ICE', gi)], [fkey])
                    self.dve(lambda e, de=de, se=se, he=he: e.tensor_tensor(out=de, in0=se, in1=he, op=ALU.subtract), hpk, [fkey, ('DT', gi)])
                for tb in range(NTB):
                    sl = slice(tb * TB, (tb + 1) * TB)
                    b = self.bank()
                    self.mm(self.ps[b], spw[1][:, gi, :], DT[:, gi, sl], True, True, [('wf', spw[0]), ('DT', gi)], [('ps', b)])
                    self.act(DT[:, gi, sl], self.ps[b], AF.Identity, ['vec'], [('ps', b), ('DTB', gi, tb)],
                             scale=self.vcol(CO_PSC + gi))
        for i in range(4):
            dg = DG[i % 2]
            for k in range(31):
                o = dg[:, k, :]
                wc = self.vcol(CO_CW + i * 31 + k)
                self.dve(lambda e, o=o, wc=wc: e.tensor_scalar_mul(out=o, in0=self.IDB, scalar1=wc),
                         ['idb', 'vec'], [('DG', i % 2)])
            for tb in range(NTB):
                sl = slice(tb * TB, (tb + 1) * TB)
                b = self.bank()
                for k in range(31):
                    self.mm(self.ps[b], dg[:, k, :], CB[:, i, 2 * tb:2 * tb + 2, k:k + 256], k == 0, k == 30,
                            [('DG', i % 2), ('CBh', i, 0), ('CBh', i, 1)] + [('CBc', i, t2) for t2 in range(NTB)], [('ps', b)])
                self.act(CO[:, i, sl], self.ps[b], AF.Identity, ['vec'], [('ps', b), ('CO', i, tb)],
                         bias=self.vcol(CO_CB + i))
        def conv_ln_args(tb):
            sl = slice(tb * TB, (tb + 1) * TB)
            return dict(src=lambda c, sl=sl: CO[:, c, sl], skey=lambda c, tb=tb: ('CO', c, tb),
                        dst=lambda c, sl=sl: AB[:, c, sl], dkey=lambda c, tb=tb: ('H', c, tb), func=AF.Silu)
        self.layernorm_all(CO_CNG, CO_CNB, nch=4, mk=conv_ln_args)
        w_out = self.dr['cp_w_out'][0]
        rd_gate = self.ada_res(l, 5)
        for pair in range(4):
            so = self.wload(w_out[:, pair * 256:(pair + 1) * 256].rearrange("(k p) n -> p k n", p=128))
            for il in range(2):
                c = pair * 2 + il
                for tb in range(NTB):
                    sl = slice(tb * TB, (tb + 1) * TB)
                    b = self.bank()
                    for j in range(KC):
                        rhs = AB[:, j, sl] if j < 4 else DT[:, j - 4, sl]
                        rk = ('H', j, tb) if j < 4 else ('DTB', j - 4, tb)
                        self.mm(self.ps[b], so[1][:, j, il * 128:(il + 1) * 128], rhs, j == 0, j == KC - 1,
                                [('wf', so[0]), rk], [('ps', b)])
                    self.resid_ln(l, 5, b, c, tb, rd_gate)
        self.layernorm_all(CO_LNG + (l * 3 + 1) * 8, CO_LNB + (l * 3 + 1) * 8, post=post)

    def rms_block(self, tb, nch, wslices, g_col, dst_f32=None, dst_bf=None, key=None, alt=0):
        sl = slice(tb * TB, (tb + 1) * TB)
        QD = self.mla_bufs['QD'][alt]
        qk = 'QD%d' % alt
        bq = self.bank()
        for c in range(nch):
            b = self.bank()
            slot, lf = wslices[c]
            for k in range(KC):
                self.mm(self.ps[b], lf(k), self.H[:, k, sl], k == 0, k == KC - 1, [('wf', slot), ('H', k, tb)], [('ps', b)])
            self.act(QD[:, c, :], self.ps[b], AF.Copy, [], [('ps', b), (qk, c)])
            i = self.xb_rr
            self.xb_rr = (i + 1) % 4
            self.act(self.XB[i], QD[:, c, :], AF.Square, [(qk, c)], [('xb', i)])
            self.mm(self.ps[bq], self.ONES, self.XB[i], c == 0, c == nch - 1, ['ones', ('xb', i)], [('ps', bq)])
        var, rstd = (self.ST[2], self.ST[3]) if alt == 0 else (self.ST[0], self.ST[1])
        k2, k3 = (('st', 2), ('st', 3)) if alt == 0 else (('st', 0), ('st', 1))
        self.act(var, self.ps[bq], AF.Sqrt, ['eps2'], [('ps', bq), k2], bias=self.EPS_RMS, scale=1.0 / (nch * 128))
        self.dve(lambda e: e.reciprocal(out=rstd, in_=var), [k2], [k3])
        for c in range(nch):
            qd = QD[:, c, :]
            g = self.vcol(g_col + c)
            if dst_f32 is not None:
                o = dst_f32[:, c, sl]
                self.dve(lambda e, o=o, qd=qd, g=g: e.scalar_tensor_tensor(out=o, in0=qd, scalar=g, in1=rstd,
                                                                          op0=ALU.mult, op1=ALU.mult),
                         [(qk, c), k3, 'vec'], [(key + 'f', c, tb)])
                ob = dst_bf[:, c, sl]
                self.act(ob, o, AF.Copy, [(key + 'f', c, tb)], [(key, c, tb)])
            else:
                ob = dst_bf[:, c, sl]
                self.dve(lambda e, ob=ob, qd=qd, g=g: e.scalar_tensor_tensor(out=ob, in0=qd, scalar=g, in1=rstd,
                                                                            op0=ALU.mult, op1=ALU.mult),
                         [(qk, c), k3, 'vec'], [(key, c, tb)])

    def mla(self, l=1, premod=False, post=None):
        if not premod:
            self.modulate(l, 3, 4)
        self.wf_rr = 0
        self.nbank_rr = 8
        go = self.g_off
        QN = self.view(go, (3, T), BF16)
        CKVb = self.view(go + 9216, (2, 2048), BF16)
        KRb = self.view(go + 17408, (2048,), BF16)
        KT = [self.view(go + 21504 + i * 4096, (2048,), BF16) for i in range(2)]
        VH = [self.view(go + 29696 + i * 4096, (16, 128), BF16) for i in range(2)]
        QT = self.view(go + 37888, (T,), BF16)
        QR = [self.view(go + 40960 + i * 3072, (T,), BF16) for i in range(2)]
        PP = [self.view(go + 47104 + i * 3072, (T,), BF16) for i in range(2)]
        PT = self.view(go + 53248, (12, 512), BF16)
        RB = self.view(go + 65536, (512,), F32)
        CKVf = self.view(go + 21504, (2, T), F32)
        KRraw = self.view(go + 33792, (T,), F32)
        QDall = self.view(go + 39936, (3, 3, TB), F32)
        OST = [self.view(go + 58368 + i * 1280, (320,), F32) for i in range(2)]
        CST = [self.view(go + 60928 + i * 1280, (320,), F32) for i in range(4)]
        ROC = self.WA[0].rearrange("p a b -> p (a b)").bitcast(F32)
        ROS = self.WA[1].rearrange("p a b -> p (a b)").bitcast(F32)
        OT = self.H
        P = self.P
        P.op('sp', lambda e: e.dma_start(out=ROC[0:64, :], in_=self.dr['ropeC']), writes=[('wa', 0)], dma='c0')
        P.op('sp', lambda e: e.dma_start(out=ROS[0:64, :], in_=self.dr['ropeS']), writes=[('wa', 1)], dma='c1')
        P.op('pool', lambda e: e.dma_start(out=KRb[64:68, 0:1024], in_=self.dr['maskk'][:, 0:1024]), writes=['KRm0'], dma='m0')
        P.op('pool', lambda e: e.dma_start(out=KRb[64:68, 1536:2048], in_=self.dr['maskk'][:, 1024:1536]), writes=['KRm1'], dma='m1')
        for ct in range(4):
            P.op('sp', lambda e, ct=ct: e.dma_start(out=CST[ct][:, 0:256], in_=self.dr['cache_ckv'][ct * 128:(ct + 1) * 128, :]),
                 writes=[('cst', ct, 0)], dma='cs%d' % ct)
            P.op('sp', lambda e, ct=ct: e.dma_start(out=CST[ct][:, 256:320], in_=self.dr['cache_kr'][ct * 128:(ct + 1) * 128, :]),
                 writes=[('cst', ct, 1)], dma='ck%d' % ct)
        for ct in range(4):
            b = self.bank()
            for c in range(2):
                self.tr(self.ps[b][:, c * 128:(c + 1) * 128], CST[ct][:, c * 128:(c + 1) * 128], self.IDF,
                        [('cst', ct, 0), 'idf'], [('ps', b)], inc=False)
            self.tr(self.ps[b][0:64, 256:384], CST[ct][:, 256:320], self.IDF, [('cst', ct, 1), 'idf'], [('ps', b)])
            ks = slice(1536 + ct * 128, 1536 + (ct + 1) * 128)
            self.act(CKVb[:, :, ks], self.ps[b][:, 0:256].rearrange("p (c t) -> p c t", c=2), AF.Copy, [],
                     [('ps', b), ('CKVb', 3)])
            self.dve(lambda e, ks=ks, b=b: e.tensor_copy(out=KRb[0:64, ks], in_=self.ps[b][0:64, 256:384]), [],
                     [('ps', b), ('KRb', 3)])
        wdq = self.dr['mla_w_dq'][0]
        wdkv = self.dr['mla_w_dkv'][0]
        sq0 = self.wload(wdq[:, 0:256].rearrange("(k p) n -> p k n", p=128))
        sq1 = self.wload(wdq[:, 256:384].rearrange("(k p) n -> p k n", p=128), rows=8, cols=128)
        sk0 = self.wload(wdkv[:, 0:256].rearrange("(k p) n -> p k n", p=128))
        sk1 = self.wload(wdkv[:, 256:320].rearrange("(k p) n -> p k n", p=128), rows=8, cols=64)
        sk2 = self.wload(self.dr['w_dkv_sw'].rearrange("(k p) n -> p k n", p=128), rows=8, cols=64)
        qsl = [(sq0[0], lambda k: sq0[1][:, k, 0:128]), (sq0[0], lambda k: sq0[1][:, k, 128:256]), (sq1[0], lambda k: sq1[1][:, k, :])]
        ksl = [(sk0[0], lambda k: sk0[1][:, k, 0:128]), (sk0[0], lambda k: sk0[1][:, k, 128:256])]
        pp_ = [6]

        def pbank():
            pp_[0] = 13 - pp_[0]
            return pp_[0]
        for tb in range(NTB):
            sl = slice(tb * TB, (tb + 1) * TB)
            for (nch, wsl, raw, rk, sbank) in ((3, qsl, lambda c: QDall[:, tb, c, :], 'QDr', tb),
                                               (2, ksl, lambda c: CKVf[:, c, sl], 'CKVr', 3 + tb)):
                for c in range(nch):
                    b = pbank()
                    slot, lf = wsl[c]
                    for k in range(KC):
                        self.mm(self.ps[b], lf(k), self.H[:, k, sl], k == 0, k == KC - 1, [('wf', slot), ('H', k, tb)], [('ps', b)])
                    self.act(raw(c), self.ps[b], AF.Copy, [], [('ps', b), (rk, c, tb)])
                    i = self.xb_rr
                    self.xb_rr = (i + 1) % 4
                    self.act(self.XB[i], raw(c), AF.Square, [(rk, c, tb)], [('xb', i)])
                    self.mm(self.ps[sbank], self.ONES, self.XB[i], c == 0, c == nch - 1, ['ones', ('xb', i)], [('ps', sbank)], inc=True)
            br = pbank()
            for k in range(KC):
                self.mm(self.ps[br][0:64, :], sk1[1][:, k, :], self.H[:, k, sl], k == 0, k == KC - 1,
                        [('wf', sk1[0]), ('H', k, tb)], [('ps', br)])
            self.act(KRraw[0:64, sl], self.ps[br][0:64, :], AF.Copy, [], [('ps', br), ('KRraw', tb)])
            if tb < 2:
                bs = pbank()
                for k in range(KC):
                    self.mm(self.ps[bs][0:64, :], sk2[1][:, k, :], self.H[:, k, sl], k == 0, k == KC - 1,
                            [('wf', sk2[0]), ('H', k, tb)], [('ps', bs)])
                t1, t2 = self.tmp(), self.tmp()
                a1, a2 = self.TMP[t1][0:64, :], self.TMP[t2][0:64, :]
                self.dve(lambda e, a1=a1, sl=sl: e.tensor_tensor(out=a1, in0=KRraw[0:64, sl], in1=ROC[0:64, sl], op=ALU.mult),
                         [('KRraw', tb), ('wa', 0)], [('tmp', t1)])
                self.dve(lambda e, a2=a2, sl=sl, bs=bs: e.tensor_tensor(out=a2, in0=self.ps[bs][0:64, :], in1=ROS[0:64, sl], op=ALU.mult),
                         [('wa', 1)], [('ps', bs), ('tmp', t2)])
                self.dve(lambda e, a1=a1, a2=a2, sl=sl: e.tensor_tensor(out=KRb[0:64, sl], in0=a1, in1=a2, op=ALU.add),
                         [('tmp', t1), ('tmp', t2)], [('KRb', tb)])
            else:
                self.dve(lambda e, sl=sl: e.tensor_copy(out=KRb[0:64, sl], in_=KRraw[0:64, sl]), [('KRraw', tb)], [('KRb', tb)])
        for tb in range(NTB):
            sl = slice(tb * TB, (tb + 1) * TB)
            for (nch, g_col, raw, rk, sbank, vi) in ((3, CO_QG, lambda c: QDall[:, tb, c, :], 'QDr', tb, 0),
                                                     (2, CO_KVG, lambda c: CKVf[:, c, sl], 'CKVr', 3 + tb, 2)):
                var, rstd = self.ST[vi], self.ST[vi + 1]
                kv_, kr_ = ('st', vi), ('st', vi + 1)
                self.act(var, self.ps[sbank], AF.Identity, ['eps2'], [('ps', sbank), kv_], bias=self.EPS_RMS, scale=1.0 / (nch * 128))
                self.act(var, var, AF.Ln, [], [kv_])
                self.act(rstd, var, AF.Exp, [kv_], [kr_], scale=-0.5)
                for c in range(nch):
                    g = self.vcol(g_col + c)
                    x_ = raw(c)
                    if nch == 3:
                        ob = QN[:, c, sl]
                        self.dve(lambda e, ob=ob, x_=x_, g=g, rstd=rstd: e.scalar_tensor_tensor(out=ob, in0=x_, scalar=g, in1=rstd,
                                                                                              op0=ALU.mult, op1=ALU.mult),
                                 [(rk, c, tb), kr_, 'vec'], [('QN', c, tb)])
                    else:
                        self.dve(lambda e, x_=x_, g=g, rstd=rstd: e.scalar_tensor_tensor(out=x_, in0=x_, scalar=g, in1=rstd,
                                                                                        op0=ALU.mult, op1=ALU.mult),
                                 [kr_, 'vec'], [(rk, c, tb), ('CKVf', c, tb)])
                        self.act(CKVb[:, c, sl], x_, AF.Copy, [('CKVf', c, tb)], [('CKV', c, tb)])
        for tt in range(T // 128):
            s = tt % 2
            ts_ = slice(tt * 128, (tt + 1) * 128)
            tb = tt // 4
            b = self.bank()
            for c in range(2):
                self.tr(self.ps[b][:, c * 128:(c + 1) * 128], CKVf[:, c, ts_], self.IDF, [('CKVf', c, tb), 'idf'], [('ps', b)], inc=False)
            self.tr(self.ps[b][:, 256:320], KRraw[0:64, ts_], self.IDF[0:64, 0:64], [('KRraw', tb), 'idf'], [('ps', b)])
            self.act(OST[s], self.ps[b][:, 0:320], AF.Copy, [], [('ps', b), ('ost', s)])
            P.op('sp', lambda e, s=s, ts_=ts_: e.dma_start(out=self.dr['ockv'][ts_, :], in_=OST[s][:, 0:256]), reads=[('ost', s)], dma='oa%d' % s)
            P.op('sp', lambda e, s=s, ts_=ts_: e.dma_start(out=self.dr['okr'][ts_, :], in_=OST[s][:, 256:320]), reads=[('ost', s)], dma='ob%d' % s)
        wuq = self.dr['mla_w_uq'][0]
        wukv = self.dr['mla_w_ukv'][0]
        suq = [self.wload(wuq[:, j * 512:(j + 1) * 512].rearrange("(k p) n -> p k n", p=128), rows=3, cols=512, slot=j) for j in range(3)]
        ssw = self.wload(self.dr['w_uq_sw'].rearrange("(k p) n -> p k n", p=128), rows=3, cols=512, slot=3)
        P.fence(('pool',))
        for i in range(2):
            P.op('pool', lambda e, i=i: e.dma_start(out=QR[i][64:68, 0:1024], in_=self.dr['onehot']), writes=[('QRm', i)], dma='m%d' % (2 + i))

        def uq(col0, width):
            j, o = divmod(col0, 512)
            assert o + width <= 512
            return suq[j][0], (lambda k: suq[j][1][:, k, o:o + width])
        scale = float((128 + 64) ** -0.5)
        groups = [(0, 8, [(0, 512), (512, 512), (1536, 512)], 68),
                  (8, 2, [(1024, 256)], 64), (10, 2, [(1280, 256)], 64)]
        for h in range(8):
            hb = h % 2
            swv = self.wload(wukv[:, h * 256:(h + 1) * 256].rearrange("(k p) n -> p k n", p=128), rows=2, cols=256, slot=4 + hb)
            for kb in range(4):
                b = self.bank()
                ks = slice(kb * 512, (kb + 1) * 512)
                for c in range(2):
                    self.mm(self.ps[b], swv[1][:, c, 0:128], CKVb[:, c, ks], c == 0, c == 1,
                            [('wf', swv[0])] + [('CKVb', i) for i in range(4)] + [('CKV', c, i) for i in range(3)], [('ps', b)])
                if kb % 2 == 0:
                    self.act(KT[hb][:, ks], self.ps[b], AF.Copy, [], [('ps', b), ('KT', hb)])
                else:
                    self.dve(lambda e, hb=hb, ks=ks, b=b: e.tensor_copy(out=KT[hb][:, ks], in_=self.ps[b]), [], [('ps', b), ('KT', hb)])
            for kq in range(4):
                b = self.bank()
                for q in range(4):
                    kt = kq * 4 + q
                    for c in range(2):
                        self.mm(self.ps[b][:, q * 128:(q + 1) * 128], CKVb[:, c, kt * 128:(kt + 1) * 128], swv[1][:, c, 128:256],
                                c == 0, c == 1, [('wf', swv[0])], [('ps', b)])
                src = self.ps[b].rearrange("p (q d) -> p q d", q=4)
                if kq % 2 == 0:
                    self.dve(lambda e, hb=hb, kq=kq, src=src: e.tensor_copy(out=VH[hb][:, kq * 4:(kq + 1) * 4, :], in_=src), [],
                             [('ps', b), ('VH', hb)])
                else:
                    self.act(VH[hb][:, kq * 4:(kq + 1) * 4, :], src, AF.Copy, [], [('ps', b), ('VH', hb)])
            for tb in range(NTB):
                sl = slice(tb * TB, (tb + 1) * TB)
                b = self.bank()
                slot, lf = uq(h * 192, 128) if (h * 192) % 512 + 128 <= 512 else (None, None)
                for c in range(3):
                    if slot is not None:
                        self.mm(self.ps[b], lf(c), QN[:, c, sl], c == 0, c == 2, [('wf', slot), ('QN', c, tb)], [('ps', b)])
                    else:
                        j0, o0 = divmod(h * 192, 512)
                        w0 = 512 - o0
                        self.mm(self.ps[b][0:w0, :], suq[j0][1][:, c, o0:512], QN[:, c, sl], c == 0, c == 2,
                                [('wf', suq[j0][0]), ('QN', c, tb)], [('ps', b)])
                if slot is None:
                    for c in range(3):
                        self.mm(self.ps[b][w0:128, :], suq[j0 + 1][1][:, c, 0:128 - w0], QN[:, c, sl], c == 0, c == 2,
                                [('wf', suq[j0 + 1][0]), ('QN', c, tb)], [('ps', b)])
                self.act(QT[:, sl], self.ps[b], AF.Copy, [], [('ps', b), ('QT', tb)])
                br = self.bank()
                sr, lr = uq(h * 192 + 128, 64)
                for c in range(3):
                    self.mm(self.ps[br][0:64, :], lr(c), QN[:, c, sl], c == 0, c == 2, [('wf', sr), ('QN', c, tb)], [('ps', br)])
                if tb < 2:
                    bsw = self.bank()
                    for c in range(3):
                        self.mm(self.ps[bsw][0:64, :], ssw[1][:, c, h * 64:(h + 1) * 64], QN[:, c, sl], c == 0, c == 2,
                                [('wf', ssw[0]), ('QN', c, tb)], [('ps', bsw)])
                    t1, t2 = self.tmp(), self.tmp()
                    a1, a2 = self.TMP[t1][0:64, :], self.TMP[t2][0:64, :]
                    self.dve(lambda e, a1=a1, sl=sl, br=br: e.tensor_tensor(out=a1, in0=self.ps[br][0:64, :], in1=ROC[0:64, sl], op=ALU.mult),
                             [('wa', 0)], [('ps', br), ('tmp', t1)])
                    self.dve(lambda e, a2=a2, sl=sl, bsw=bsw: e.tensor_tensor(out=a2, in0=self.ps[bsw][0:64, :], in1=ROS[0:64, sl], op=ALU.mult),
                             [('wa', 1)], [('ps', bsw), ('tmp', t2)])
                    self.dve(lambda e, a1=a1, a2=a2, sl=sl, hb=hb: e.tensor_tensor(out=QR[hb][0:64, sl], in0=a1, in1=a2, op=ALU.add),
                             [('tmp', t1), ('tmp', t2)], [('QR', hb, tb)])
                else:
                    self.act(QR[hb][0:64, sl], self.ps[br][0:64, :], AF.Copy, [], [('ps', br), ('QR', hb, tb)])
            items = []
            kbA = [(0, 512), (512, 512), (1536, 512)]
            ktA = [0, 1, 2, 3, 4, 5, 6, 7, 12, 13, 14, 15]
            for qb0 in (0, 4):
                for qi in range(4):
                    items.append(dict(qt=qb0 + qi, qi=qi, nq=4, qb0=qb0, kblocks=kbA, krows=68, ptoff=0, ktiles=ktA,
                                      first=(qi == 0), last=(qi == 3), zero=False))
            for qi in range(4):
                kb = [(1024, 256)] if qi < 2 else [(1280, 256)]
                items.append(dict(qt=8 + qi, qi=qi, nq=4, qb0=8, kblocks=kb, krows=64, ptoff=0 if qi < 2 else 2,
                                  ktiles=[8, 9, 10, 11], first=(qi == 0), last=(qi == 3), zero=(qi == 0)))

            def stage_qk(it):
                qt = it['qt']
                qs = slice(qt * 128, (qt + 1) * 128)
                tbq = qt // 4
                it['banks'] = []
                for bi_, (k0, kw) in enumerate(it['kblocks']):
                    b = (qt % 2) * 3 + bi_
                    it['banks'].append(b)
                    self.mm(self.ps[b][:, 0:kw], QT[:, qs], KT[hb][:, k0:k0 + kw], True, False,
                            [('QT', tbq), ('KT', hb)], [('ps', b)])
                    rdm = ['KRm0', 'KRm1', ('QRm', hb)] if it['krows'] == 68 else []
                    self.mm(self.ps[b][:, 0:kw], QR[hb][0:it['krows'], qs], KRb[0:it['krows'], k0:k0 + kw], False, True,
                            [('QR', hb, tbq)] + [('KRb', i) for i in range(4)] + rdm, [('ps', b)])

            def stage_max(it):
                qt = it['qt']
                par = qt % 2
                so = par * 8
                banks = it['banks']
                nb = len(banks)
                for bi, b in enumerate(banks):
                    kw = it['kblocks'][bi][1]
                    self.dve(lambda e, bi=bi, b=b, kw=kw, so=so: e.reduce_max(out=self.SM[:, so + bi:so + bi + 1], in_=self.ps[b][:, 0:kw],
                                                                           axis=mybir.AxisListType.X), [], [('ps', b), ('sm_mx', par)])
                if nb > 1:
                    self.dve(lambda e, so=so, nb=nb: e.reduce_max(out=self.SM[:, so + 3:so + 4], in_=self.SM[:, so:so + nb],
                                                                  axis=mybir.AxisListType.X), [], [('sm_mx', par)])
                    src_c = so + 3
                else:
                    src_c = so
                self.dve(lambda e, so=so, src_c=src_c: e.tensor_scalar_mul(out=self.SM[:, so + 4:so + 5], in0=self.SM[:, src_c:src_c + 1],
                                                                        scalar1=-scale), [], [('sm_mx', par), ('sm_nb', par)])

            def stage_exp(it):
                qt = it['qt']
                par = qt % 2
                so = par * 8
                pp = PP[par]
                ko = 0
                for bi, b in enumerate(it['banks']):
                    kw = it['kblocks'][bi][1]
                    self.act(pp[:, ko:ko + kw], self.ps[b][:, 0:kw], AF.Exp, [('sm_nb', par)], [('ps', b), ('PP', par, bi)],
                             bias=self.SM[:, so + 4:so + 5], scale=scale)
                    ko += kw

            def stage_transpose(it):
                qt, qi = it['qt'], it['qi']
                pp = PP[qt % 2]
                nkt = sum(w for _, w in it['kblocks']) // 128
                po = it['ptoff']
                if it['zero']:
                    z1, z2 = PT[:, 2:4, 0:256], PT[:, 0:2, 256:512]
                    self.dve(lambda e, z1=z1: e.memset(z1, 0.0), [], [('PT', 0, 0), ('PT', 1, 0)])
                    self.dve(lambda e, z2=z2: e.memset(z2, 0.0), [], [('PT', 2, 0), ('PT', 3, 0)])
                for kq in range(0, nkt, 8):
                    b = 6 + kq // 8
                    pb = self.ps[b].bitcast(BF16)
                    n8 = min(8, nkt - kq)
                    for q in range(n8):
                        kt = kq + q
                        self.tr(pb[:, q * 128:(q + 1) * 128], pp[:, kt * 128:(kt + 1) * 128], self.IDB,
                                [('PP', qt % 2, kt // 4), 'idb'], [('ps', b)], inc=(q == n8 - 1))
                    src = pb[:, 0:n8 * 128].rearrange("p (q t) -> p q t", q=n8)
                    dst = PT[:, po + kq:po + kq + n8, qi * 128:(qi + 1) * 128]
                    if (kq // 8) % 2 == 0:
                        self.act(dst, src, AF.Copy, [], [('ps', b), ('PT', qi, kq // 8)])
                    else:
                        self.dve(lambda e, dst=dst, src=src: e.tensor_copy(out=dst, in_=src), [], [('ps', b), ('PT', qi, kq // 8)])

            def stage_pv(it):
                nq, qb0 = it['nq'], it['qb0']
                nqc = nq * 128
                ktiles = it['ktiles']
                bo, bsum = 6, 7
                rdpt = [('PT', q, g) for q in range(nq) for g in range((len(ktiles) + 7) // 8)]
                for i, ktg in enumerate(ktiles):
                    self.mm(self.ps[bo][:, 0:nqc], VH[hb][:, ktg, :], PT[:, i, 0:nqc], i == 0, i == len(ktiles) - 1,
                            [('VH', hb)] + rdpt, [('ps', bo)])
                for i, ktg in enumerate(ktiles):
                    self.mm(self.ps[bsum][:, 0:nqc], self.ONES, PT[:, i, 0:nqc], i == 0, i == len(ktiles) - 1,
                            ['ones'] + rdpt, [('ps', bsum)])
                self.dve(lambda e, bsum=bsum, nqc=nqc: e.reciprocal(out=RB[:, 0:nqc], in_=self.ps[bsum][:, 0:nqc]), [], [('ps', bsum), 'RB'])
                q0 = qb0 * 128
                tbo = q0 // TB
                dst = OT[:, h, q0:q0 + nqc]
                self.dve(lambda e, dst=dst, bo=bo, nqc=nqc: e.tensor_tensor(out=dst, in0=self.ps[bo][:, 0:nqc], in1=RB[:, 0:nqc], op=ALU.mult),
                         ['RB'], [('ps', bo), ('H', h, tbo)])

            stage_qk(items[0])
            stage_max(items[0])
            for i, it in enumerate(items):
                if i + 1 < len(items):
                    stage_qk(items[i + 1])
                stage_exp(it)
                if i + 1 < len(items):
                    stage_max(items[i + 1])
                stage_transpose(it)
                if it['last']:
                    stage_pv(it)
        if self.debug == 91:
            for c in range(KC):
                for tb in range(NTB):
                    sl = slice(tb * TB, (tb + 1) * TB)
                    self.act(self.X[:, c, sl], OT[:, c, sl], AF.Copy, [('H', c, tb)], [('X', c, tb)])
            return
        w_o = self.dr['mla_w_o'][0]
        rd_gate = self.ada_res(l, 5)
        self.wf_rr = 6
        for pair in range(4):
            so = self.wload(w_o[:, pair * 256:(pair + 1) * 256].rearrange("(k p) n -> p k n", p=128))
            for il in range(2):
                c = pair * 2 + il
                for tb in range(NTB):
                    sl = slice(tb * TB, (tb + 1) * TB)
                    b = self.bank()
                    for j in range(KC):
                        self.mm(self.ps[b], so[1][:, j, il * 128:(il + 1) * 128], OT[:, j, sl], j == 0, j == KC - 1,
                                [('wf', so[0]), ('H', j, tb)], [('ps', b)])
                    self.resid_ln(l, 5, b, c, tb, rd_gate)
        self.layernorm_all(CO_LNG + (l * 3 + 1) * 8, CO_LNB + (l * 3 + 1) * 8, post=post)

    def store_x(self, tbs=None):
        xo = [self.view(self.stage_off, (D,), F32), self.view(self.stage_off + 4096, (D,), F32)]
        tiles = range(T // 128) if tbs is None else [tt for tb in tbs for tt in range(4 * tb, 4 * tb + 4)]
        for tt in tiles:
            s = tt % 2
            for half in range(2):
                b = self.bank()
                for q in range(4):
                    c = half * 4 + q
                    self.tr(self.ps[b][:, q * 128:(q + 1) * 128], self.X[:, c, tt * 128:(tt + 1) * 128], self.IDF,
                            [('X', c, tt // 4), 'idf'], [('ps', b)], inc=(q == 3))
                out = xo[s][:, half * 512:(half + 1) * 512]
                if half == 0:
                    self.act(out, self.ps[b], AF.Copy, [], [('ps', b), ('xo', s, 0)])
                else:
                    self.dve(lambda e, out=out, b=b: e.tensor_copy(out=out, in_=self.ps[b]), [], [('ps', b), ('xo', s, 1)])
            dst = self.dr['y'][tt * 128:(tt + 1) * 128, :]
            src = xo[s]
            self.P.op('sp', lambda e, dst=dst, src=src: e.dma_start(out=dst, in_=src),
                      reads=[('xo', s, 0), ('xo', s, 1)], dma='yo%d' % s)

    def build(self):
        st = self.debug or 99
        self.constants()
        nada = [0]

        def ada_next(n=1):
            while n > 0 and nada[0] < 72:
                k = min(n, 72 - nada[0], 2)
                self.ada_units([((nada[0] + j) // 36, (nada[0] + j) % 36) for j in range(k)])
                nada[0] += k
                n -= k
        self.load_x(hook=lambda: ada_next(2))
        hook = lambda: ada_next(2)
        if st < 99:
            self.ffn(0, 0, 0, 0, ada_hook=hook)
            if st >= 7:
                self.P.fence()
                self.mixer0(0)
                self.P.fence()
            if st >= 8:
                self.ffn(0, 1, 6, 2, ada_hook=hook)
                ada_next(72)
                self.ffn(1, 0, 0, 3)
            if st >= 9:
                self.P.fence(('pe', 'act', 'dve', 'sp', 'pool'))
                self.mla(1)
                self.P.fence()
            if st >= 10 and st != 91:
                self.ffn(1, 1, 6, 5)
        else:
            self.ffn(0, 0, 0, 0, ada_hook=hook, post=lambda tb: self.modulate(0, 3, 4, [tb]))
            self.P.fence()
            self.mixer0(0, premod=True, post=lambda tb: self.modulate(0, 6, 7, [tb]))
            self.P.fence()
            self.ffn(0, 1, 6, 2, ada_hook=hook, premod=True, post=lambda tb: self.modulate(1, 0, 1, [tb]))
            ada_next(72)
            self.ffn(1, 0, 0, 3, premod=True, post=lambda tb: self.modulate(1, 3, 4, [tb]))
            self.P.fence(('pe', 'act', 'dve', 'sp', 'pool'))
            self.mla(1, premod=True, post=lambda tb: self.modulate(1, 6, 7, [tb]))
            self.P.fence()
            self.ffn(1, 1, 6, 5, premod=True, post=lambda tb: self.store_x([tb]))
            self.emit()
            return self.nc
        self.store_x()
        self.emit()
        return self.nc

    def emit(self):
        nc = self.nc
        P = self.P
        names = sorted(P.cnt.keys())
        sems = {}
        import contextlib
        with contextlib.ExitStack() as st:
            for n in names:
                sems[n] = st.enter_context(nc.semaphore("s_" + n))
            block = st.enter_context(nc.Block())
            finals = [(n, v) for n, v in P.cnt.items() if n not in ENGS]

            def run(e, key, final=False):
                for waits, fn, incspec in P.ops[key]:
                    for s, v in waits:
                        e.wait_ge(sems[s], v)
                    if fn is None:
                        continue
                    ins = fn(e)
                    if incspec is not None:
                        ins.then_inc(sems[incspec[0]], incspec[1])
                if final:
                    for n, v in finals:
                        e.wait_ge(sems[n], v)

            @block.tensor
            def _(e):
                run(e, 'pe')

            @block.scalar
            def _(e):
                run(e, 'act')

            @block.vector
            def _(e):
                run(e, 'dve')

            @block.gpsimd
            def _(e):
                run(e, 'pool')

            @block.sync
            def _(e):
                run(e, 'sp', final=True)
        print("ops:", {k: len(v) for k, v in P.ops.items()}, "sems:", len(names))


def _pack_vecs(inp, cond2, flag):
    v = np.zeros((128, NV), np.float32)

    def put(col, vec):
        vec = np.asarray(vec, np.float32)
        n = vec.shape[0] // 128
        v[:, col:col + n] = vec.reshape(n, 128).T
    for r in range(2):
        put(CO_COND + r * 8, cond2[r])
    for l in range(2):
        put(CO_BADA + l * 72, inp['b_ada'][l])
        for s in range(3):
            put(CO_LNG + (l * 3 + s) * 8, inp['ln_g'][l, s])
            put(CO_LNB + (l * 3 + s) * 8, inp['ln_b'][l, s])
    cw = np.asarray(inp['conv_w'][0], np.float32)
    for i in range(4):
        v[:, CO_CW + i * 31: CO_CW + (i + 1) * 31] = cw[:, i * 128:(i + 1) * 128].T
    put(CO_CB, inp['conv_b'][0]); put(CO_CNG, inp['conv_norm_g'][0]); put(CO_CNB, inp['conv_norm_b'][0])
    put(CO_PSC, inp['pool_scale'][0])
    put(CO_QG, inp['mla_q_norm_g'][0]); put(CO_KVG, inp['mla_kv_norm_g'][0])
    v[:, CO_FLAG] = flag
    return v


def _rope_tables(real):
    C = np.ones((64, 1024), np.float32)
    S = np.zeros((64, 1024), np.float32)
    if real:
        n = 1024
        row = np.repeat(np.arange(n // 64), 64).astype(np.float32)
        col = np.tile(np.arange(64), n // 64).astype(np.float32)
        inv = (10000.0 ** (-np.arange(16, dtype=np.float32) / 16)).astype(np.float32)
        ang = np.concatenate([row[:, None] * inv, col[:, None] * inv], -1).astype(np.float32)
        cos, sin = np.cos(ang), np.sin(ang)
        for a in range(2):
            for j in range(2):
                for p in range(16):
                    dd = a * 32 + j * 16 + p
                    C[dd] = cos[:, a * 16 + p]
                    S[dd] = (-sin[:, a * 16 + p]) if j == 0 else sin[:, a * 16 + p]
    return C, S


_NC_CACHE = {}


def _prep_inputs(inp):
    inp = {k: np.asarray(v) for k, v in inp.items()}
    xp = inp['x_prompt'].astype(np.float32)
    xs = inp['x_sample'].astype(np.float32)
    ident = np.eye(128, dtype=np.float32)
    onehot = np.zeros((4, 1024), np.float32)
    for j in range(4):
        onehot[j, j * 256:(j + 1) * 256] = 1.0
    perm = np.arange(64).reshape(2, 2, 16)[:, ::-1, :].reshape(64)
    w_uq = inp['mla_w_uq'][0]
    uq_r = w_uq.reshape(384, 8, 192)[:, :, 128:]
    w_uq_sw = np.ascontiguousarray(uq_r[:, :, perm].reshape(384, 512))
    w_dkv_sw = np.ascontiguousarray(inp['mla_w_dkv'][0][:, 256:][:, perm])
    shared = {k: np.ascontiguousarray(inp[k], dtype=np.float32) for k in
              ('w_ada', 'ffn_w1', 'ffn_w3', 'ffn_w2', 'cp_w_in', 'pool_w', 'cp_w_out', 'mla_w_dq', 'mla_w_uq',
               'mla_w_dkv', 'mla_w_ukv', 'mla_w_o')}
    shared.update(ident=ident, onehot=onehot, w_uq_sw=w_uq_sw, w_dkv_sw=w_dkv_sw)
    in_maps = []
    for r in range(8):
        if r < 4:
            x = np.concatenate([xs[r], xp[2 * r], xp[2 * r + 1]], 0)
            cond2 = np.stack([inp['c_ctx'], inp['c'][r]], 0)
            flag = 1.0
            cckv = inp['cache_mla_ckv'][r, 0]
            ckr = inp['cache_mla_krope'][r, 0]
            maskk = np.zeros((4, 1536), np.float32)
            C, S = _rope_tables(True)
        else:
            p0 = 8 + 6 * (r - 4)
            x = xp[p0:p0 + 6].reshape(T, D)
            cond2 = np.stack([inp['c_ctx'], inp['c_ctx']], 0)
            flag = 0.0
            cckv = np.zeros((512, 256), np.float32)
            ckr = np.zeros((512, 64), np.float32)
            maskk = np.full((4, 1536), NEG, np.float32)
            for j in range(4):
                maskk[j, j * 256:(j + 1) * 256] = 0.0
            C, S = _rope_tables(False)
        m = dict(shared)
        m.update(x=np.ascontiguousarray(x), vecs=_pack_vecs(inp, cond2, flag),
                 cache_ckv=np.ascontiguousarray(cckv, dtype=np.float32),
                 cache_kr=np.ascontiguousarray(ckr, dtype=np.float32), ropeC=C, ropeS=S, maskk=maskk)
        in_maps.append(m)
    return in_maps


def _assemble(results):
    y_p = np.zeros((32, 256, D), np.float32)
    y_s = np.zeros((4, 1024, D), np.float32)
    ckv = np.zeros((32, 1, 256, 256), np.float32)
    kr = np.zeros((32, 1, 256, 64), np.float32)
    for r in range(8):
        y = results[r]['y']
        ok = results[r]['ockv']
        okr = results[r]['okr']
        if r < 4:
            y_s[r] = y[:1024]
            for i in range(2):
                sl = slice(1024 + 256 * i, 1024 + 256 * (i + 1))
                y_p[2 * r + i] = y[sl]; ckv[2 * r + i, 0] = ok[sl]; kr[2 * r + i, 0] = okr[sl]
        else:
            p0 = 8 + 6 * (r - 4)
            for i in range(6):
                sl = slice(256 * i, 256 * (i + 1))
                y_p[p0 + i] = y[sl]; ckv[p0 + i, 0] = ok[sl]; kr[p0 + i, 0] = okr[sl]
    return y_p, y_s, ckv, kr


def kernel(**inputs):
    in_maps = _prep_inputs(inputs)
    if 'nc' not in _NC_CACHE:
        _NC_CACHE['nc'] = Builder().build()
    res = run_bass_kernel_spmd(_NC_CACHE['nc'], in_maps, core_ids=list(range(8)))
    return _assemble(res.results)
```

```python
import numpy as np
import concourse.bass as bass
import concourse.mybir as mybir
from concourse.bass_utils import run_bass_kernel_spmd

F32 = mybir.dt.float32
BF16 = mybir.dt.bfloat16
AF = mybir.ActivationFunctionType
ALU = mybir.AluOpType

T = 1536
D = 1024
KC = 8
TB = 512
NTB = 3
DFF = 2816
NJ = 22
ALPHA = float((2 * 2) ** 0.25)
LN_EPS = 1e-5
RMS_EPS = 1e-6
NEG = -30000.0

CO_COND = 0
CO_BADA = CO_COND + 16
CO_LNG = CO_BADA + 144
CO_LNB = CO_LNG + 48
CO_CW = CO_LNB + 48
CO_CB = CO_CW + 124
CO_CNG = CO_CB + 4
CO_CNB = CO_CNG + 4
CO_PSC = CO_CNB + 4
CO_QG = CO_PSC + 4
CO_KVG = CO_QG + 3
CO_FLAG = CO_KVG + 2
NV = CO_FLAG + 1

ENGS = ('pe', 'act', 'dve', 'pool', 'sp')


class Prog:
    def __init__(self):
        self.ops = {e: [] for e in ENGS}
        self.cnt = {}
        self.seen = {e: {} for e in ENGS}
        self.res = {}

    def op(self, eng, fn, reads=(), writes=(), inc=True, dma=None):
        need = {}

        def add(tok):
            if tok is not None:
                s, v = tok
                if need.get(s, 0) < v:
                    need[s] = v
        for r in reads:
            e = self.res.get(r)
            if e is not None:
                add(e[0])
        for w in writes:
            e = self.res.get(w)
            if e is not None:
                add(e[0])
                for s, v in e[1].items():
                    add((s, v))
        if dma is not None and self.cnt.get(dma, 0) > 0:
            add((dma, self.cnt[dma]))
        waits = []
        for s, v in need.items():
            if eng == 'pe' and s == 'pe':
                continue
            if self.seen[eng].get(s, 0) >= v:
                continue
            self.seen[eng][s] = v
            waits.append((s, v))
        if dma is not None:
            self.cnt[dma] = self.cnt.get(dma, 0) + 16
            tok = (dma, self.cnt[dma])
            incspec = (dma, 16)
        else:
            before = self.cnt.get(eng, 0)
            tok = (eng, before + 1)
            if inc:
                self.cnt[eng] = before + 1
                incspec = (eng, 1)
            else:
                incspec = None
        for r in reads:
            e = self.res.setdefault(r, [None, {}])
            if e[1].get(tok[0], 0) < tok[1]:
                e[1][tok[0]] = tok[1]
        for w in writes:
            self.res[w] = [tok, {}]
        self.ops[eng].append((waits, fn, incspec))


    def fence(self, engines=('pe', 'act', 'dve', 'sp')):
        for eng in engines:
            waits = []
            for s, v in self.cnt.items():
                if v == 0 or (eng == 'pe' and s == 'pe'):
                    continue
                if self.seen[eng].get(s, 0) >= v:
                    continue
                self.seen[eng][s] = v
                waits.append((s, v))
            if waits:
                self.ops[eng].append((waits, None, None))


class Builder:
    def __init__(self, debug=None):
        self.debug = debug
        nc = self.nc = bass.Bass("TRN2", target_bir_lowering=False)
        self.P = Prog()
        self.dr = {}

        def din(name, shape):
            self.dr[name] = nc.dram_tensor(name, list(shape), F32, kind="ExternalInput").ap()

        def dout(name, shape):
            self.dr[name] = nc.dram_tensor(name, list(shape), F32, kind="ExternalOutput").ap()
        din('x', [T, D]); din('vecs', [128, NV]); din('ident', [128, 128])
        din('cache_ckv', [512, 256]); din('cache_kr', [512, 64])
        din('ropeC', [64, 1024]); din('ropeS', [64, 1024])
        din('maskk', [4, 1536]); din('onehot', [4, 1024])
        din('w_ada', [2, 1024, 9216])
        din('ffn_w1', [2, 2, 1024, DFF]); din('ffn_w3', [2, 2, 1024, DFF]); din('ffn_w2', [2, 2, DFF, 1024])
        din('cp_w_in', [1, 1024, 1536]); din('pool_w', [1, 4, 128, 128]); din('cp_w_out', [1, 1024, 1024])
        din('mla_w_dq', [1, 1024, 384]); din('mla_w_uq', [1, 384, 1536]); din('w_uq_sw', [384, 512])
        din('mla_w_dkv', [1, 1024, 320]); din('w_dkv_sw', [1024, 64])
        din('mla_w_ukv', [1, 256, 2048]); din('mla_w_o', [1, 1024, 1024])
        dout('y', [T, D]); dout('ockv', [T, 256]); dout('okr', [T, 64])

        self.big = nc.alloc_sbuf_tensor("big", [128, 103 * 1024], BF16)
        self.off = 0
        self.X = self.alloc((KC, T), F32)
        self.H = self.alloc((KC, T), BF16)
        self.g_off = self.off
        self.G = self.alloc((NJ, T), BF16)
        self.g_end = self.off
        self.WF = [self.alloc((2048,), BF16) for _ in range(8)]
        self.WA = [self.alloc((8, 256), BF16) for _ in range(2)]
        self.VEC = self.alloc((NV,), F32)
        self.ADAT = self.alloc((2, 72, 2), F32)
        self.IDF = self.alloc((128,), F32)
        self.IDB = self.alloc((128,), BF16)
        self.ONES = self.alloc((128,), BF16)
        self.SC = self.alloc((8, 2), BF16)
        self.SM = self.alloc((16,), F32)
        self.DRF = self.alloc((128,), F32)
        self.ONESF = self.alloc((128,), F32)
        self.rb_bank = 7
        self.EPS_LN = self.alloc((1,), F32)
        self.EPS_RMS = self.alloc((1,), F32)
        self.ST = [self.alloc((TB,), F32) for _ in range(4)]
        self.TMP = [self.alloc((TB,), F32) for _ in range(4)]
        self.XB = [self.alloc((TB,), BF16) for _ in range(4)]
        print("sbuf used bytes/partition:", self.off)
        assert self.off <= 206 * 1024
        self.ps = [nc.alloc_psum_tensor("ps%d" % b, [128, 512], F32)[:, :] for b in range(8)]
        self.stage_off = self.g_end - 8192
        self.bank_rr = 0
        self.nbank_rr = 7
        self.live_banks = set()
        self.wf_rr = 0
        self.wa_rr = 0
        self.tmp_rr = 0
        self.xb_rr = 0
        self.out_rr = 0

    def view(self, off, shape, dtype):
        n = int(np.prod(shape))
        esz = 4 if dtype == F32 else 2
        assert off % 4 == 0
        ap = self.big[:, off // 2: (off + n * esz) // 2]
        if dtype == F32:
            ap = ap.bitcast(F32)
        if len(shape) == 2:
            ap = ap.rearrange("p (a b) -> p a b", a=shape[0])
        elif len(shape) == 3:
            ap = ap.rearrange("p (a b c) -> p a b c", a=shape[0], b=shape[1])
        return ap

    def alloc(self, shape, dtype):
        n = int(np.prod(shape))
        esz = 4 if dtype == F32 else 2
        off = (self.off + 31) // 32 * 32
        ap = self.view(off, shape, dtype)
        self.off = off + n * esz
        return ap

    def bank(self):
        while True:
            b = self.bank_rr
            self.bank_rr = (b + 1) % self.nbank_rr
            if b not in self.live_banks:
                return b

    def tmp(self):
        i = self.tmp_rr
        self.tmp_rr = (i + 1) % 4
        return i

    def act(self, out, in_, func, reads, writes, **kw):
        self.P.op('act', lambda e: e.activation(out=out, in_=in_, func=func, **kw), reads, writes)

    def dve(self, fn, reads, writes):
        self.P.op('dve', fn, reads, writes)

    def mm(self, out, lhsT, rhs, start, stop, reads, writes, inc=None):
        self.P.op('pe', lambda e: e.matmul(out, lhsT=lhsT, rhs=rhs, start=start, stop=stop),
                  reads, writes, inc=(stop if inc is None else inc))

    def tr(self, out, in_, ident, reads, writes, inc=True):
        self.P.op('pe', lambda e: e.transpose(out=out, in_=in_, identity=ident), reads, writes, inc=inc)

    def wload(self, src_ap, rows=8, cols=256, slot=None):
        if slot is None:
            i = self.wf_rr
            self.wf_rr = (i + 1) % 8
        else:
            i = slot
        dst = self.WF[i][:, 0:rows * cols].rearrange("p (a b) -> p a b", a=rows)
        self.P.op('pool', lambda e: e.dma_start(out=dst, in_=src_ap), reads=(), writes=[('wf', i)], dma='wf%d' % i)
        return (i, dst)

    def vcol(self, col, n=1):
        return self.VEC[:, col:col + n]

    def constants(self):
        P = self.P
        P.op('sp', lambda e: e.dma_start(out=self.VEC, in_=self.dr['vecs']), writes=['vec'], dma='c0')
        P.op('sp', lambda e: e.dma_start(out=self.IDF, in_=self.dr['ident']), writes=['idf'], dma='c1')
        self.dve(lambda e: e.tensor_copy(out=self.IDB, in_=self.IDF), ['idf'], ['idb'])
        self.dve(lambda e: e.memset(self.ONES, 1.0), [], ['ones'])
        self.dve(lambda e: e.memset(self.ONESF, 1.0), [], ['onesf'])
        self.dve(lambda e: e.memset(self.EPS_LN, LN_EPS), [], ['eps'])
        self.dve(lambda e: e.memset(self.EPS_RMS, RMS_EPS), [], ['eps2'])
        cond = self.VEC[:, CO_COND:CO_COND + 16].rearrange("p (r c) -> p c r", r=2)
        self.act(self.SC, cond, AF.Silu, ['vec'], ['sc'])

    def load_x(self, hook=None):
        xin = [self.view(self.stage_off - 8192 + i * 4096, (D,), F32) for i in range(4)]
        for tt in range(T // 128):
            if hook is not None and tt in (1, 4, 7, 10):
                hook()
            s = tt % 4
            src = self.dr['x'][tt * 128:(tt + 1) * 128, :]
            dst = xin[s]
            self.P.op('sp', lambda e, dst=dst, src=src: e.dma_start(out=dst, in_=src), writes=[('xin', s)], dma='xin%d' % s)
            for half in range(2):
                b = self.bank()
                for q in range(4):
                    c = half * 4 + q
                    self.tr(self.ps[b][:, q * 128:(q + 1) * 128], xin[s][:, c * 128:(c + 1) * 128], self.IDF,
                            [('xin', s), 'idf'], [('ps', b)], inc=(q == 3))
                src_ps = self.ps[b][:, :].rearrange("p (q t) -> p q t", q=4)
                out = self.X[:, half * 4:half * 4 + 4, tt * 128:(tt + 1) * 128]
                wr = [('X', half * 4 + q, tt // 4) for q in range(4)]
                if half == 0:
                    self.act(out, src_ps, AF.Copy, [('ps', b)], [('ps', b)] + wr)
                else:
                    self.dve(lambda e, out=out, src_ps=src_ps: e.tensor_copy(out=out, in_=src_ps), [], [('ps', b)] + wr)

    def ada_units(self, units):
        b = 7
        for (l, u) in units:
            i = self.wa_rr
            self.wa_rr = (i + 1) % 2
            src = self.dr['w_ada'][l, :, u * 256:(u + 1) * 256].rearrange("(k p) n -> p k n", p=128)
            dst = self.WA[i]
            self.P.op('pool', lambda e, dst=dst, src=src: e.dma_start(out=dst, in_=src), writes=[('wa', i)], dma='wa%d' % i)
            for fl in range(2):
                fc = u * 2 + fl
                for k in range(KC):
                    self.mm(self.ps[b][:, fc * 2:fc * 2 + 2], self.WA[i][:, k, fl * 128:(fl + 1) * 128], self.SC[:, k, :],
                            k == 0, k == KC - 1, [('wa', i), 'sc'], [('ps', b)])
        for (l, u) in units:
            m = (u * 2) // 8
            for r in range(2):
                out = self.ADAT[:, l, u * 2:u * 2 + 2, r]
                in0 = self.ps[b][:, u * 4:u * 4 + 4].rearrange("p (f r) -> p f r", r=2)[:, :, r]
                in1 = self.VEC[:, CO_BADA + l * 72 + u * 2: CO_BADA + l * 72 + u * 2 + 2]
                self.dve(lambda e, out=out, in0=in0, in1=in1: e.tensor_tensor(out=out, in0=in0, in1=in1, op=ALU.add),
                         ['vec'], [('ps', b), ('ada', l, u)])
            out = self.ADAT[:, l, u * 2:u * 2 + 2, :]
            if m in (1, 4, 7):
                self.dve(lambda e, out=out: e.tensor_scalar_add(out=out, in0=out, scalar1=1.0), [], [('ada', l, u)])
            elif m in (2, 8):
                self.dve(lambda e, out=out: e.tensor_scalar_mul(out=out, in0=out, scalar1=0.5), [], [('ada', l, u)])

    def ada_res(self, l, m):
        return [('ada', l, m * 4 + j) for j in range(4)]

    def mod(self, l, m, c, r):
        return self.ADAT[:, l, m * 8 + c, r:r + 1]

    @staticmethod
    def cond_of(tb):
        return 1 if tb < 2 else 0

    def modulate(self, l, m_shift, m_scale, tbs=None):
        rd_ada = self.ada_res(l, m_shift) + self.ada_res(l, m_scale)
        for tb in (range(NTB) if tbs is None else tbs):
            r = self.cond_of(tb)
            for c in range(KC):
                out = self.H[:, c, tb * TB:(tb + 1) * TB]
                in_ = self.X[:, c, tb * TB:(tb + 1) * TB]
                sc = self.mod(l, m_scale, c, r)
                sh = self.mod(l, m_shift, c, r)
                if c % 2 == 0:
                    self.act(out, in_, AF.Identity, [('X', c, tb)] + rd_ada, [('H', c, tb)], scale=sc, bias=sh)
                else:
                    self.dve(lambda e, out=out, in_=in_, sc=sc, sh=sh: e.tensor_scalar(
                        out=out, in0=in_, scalar1=sc, scalar2=sh, op0=ALU.mult, op1=ALU.add),
                        [('X', c, tb)] + rd_ada, [('H', c, tb)])

    def _ln_args(self, tb, src, skey, dst, dkey, func):
        sl = slice(tb * TB, (tb + 1) * TB)
        if src is None:
            src = lambda c: self.X[:, c, sl]
            skey = lambda c: ('X', c, tb)
        if dst is None:
            dst, dkey = src, skey
        if func is None:
            func = AF.Identity
        return src, skey, dst, dkey, func

    def ln_stats(self, tb, nch=KC, src=None, skey=None):
        src, skey, _, _, _ = self._ln_args(tb, src, skey, None, None, None)
        bs, bq = self.bank(), self.bank()
        for c in range(nch):
            i = (self.xb_rr // 2 * 2) % 4
            self.xb_rr = (i + 2) % 4
            xq = self.XB[i + 1]
            self.act(xq, src(c), AF.Square, [skey(c)], [('xb', i + 1)])
            self.mm(self.ps[bs], self.ONESF, src(c), c == 0, c == nch - 1, ['onesf', skey(c)], [('ps', bs)], inc=True)
            self.mm(self.ps[bq], self.ONES, xq, c == 0, c == nch - 1, ['ones', ('xb', i + 1)], [('ps', bq)], inc=True)
        return bs, bq

    def ln_apply(self, tb, banks, g_col, b_col, nch=KC, src=None, skey=None, dst=None, dkey=None, func=None):
        src, skey, dst, dkey, func = self._ln_args(tb, src, skey, dst, dkey, func)
        bs, bq = banks
        mean, msq, var, rstd = self.ST
        n = float(nch * 128)
        self.dve(lambda e: e.tensor_scalar_mul(out=mean, in0=self.ps[bs], scalar1=1.0 / n), [], [('ps', bs), ('st', 0)])
        self.dve(lambda e: e.tensor_tensor(out=msq, in0=mean, in1=mean, op=ALU.mult), [('st', 0)], [('st', 1)])
        self.dve(lambda e: e.scalar_tensor_tensor(out=var, in0=self.ps[bq], scalar=1.0 / n, in1=msq,
                                                 op0=ALU.mult, op1=ALU.subtract), [('st', 1)], [('ps', bq), ('st', 2)])
        self.act(var, var, AF.Ln, ['eps'], [('st', 2)], bias=self.EPS_LN, scale=1.0)
        self.act(rstd, var, AF.Exp, [('st', 2)], [('st', 3)], scale=-0.5)
        for c in range(nch):
            xs = src(c)
            t = self.tmp()
            tt_ = self.TMP[t]
            self.dve(lambda e, xs=xs, tt_=tt_: e.tensor_tensor(out=tt_, in0=xs, in1=mean, op=ALU.subtract),
                     [skey(c), ('st', 0)], [('tmp', t)])
            self.dve(lambda e, tt_=tt_: e.tensor_tensor(out=tt_, in0=tt_, in1=rstd, op=ALU.mult),
                     [('st', 3)], [('tmp', t)])
            self.act(dst(c), tt_, func, [('tmp', t), 'vec'], [dkey(c)],
                     scale=self.vcol(g_col + c), bias=self.vcol(b_col + c))

    def layernorm_all(self, g_col, b_col, post=None, nch=KC, mk=None):
        kw = [(mk(tb) if mk is not None else {}) for tb in range(NTB)]
        banks = {}

        def stats(tb):
            banks[tb] = self.ln_stats(tb, nch=nch, src=kw[tb].get('src'), skey=kw[tb].get('skey'))
            self.live_banks.update(banks[tb])

        def apply(tb):
            self.ln_apply(tb, banks[tb], g_col, b_col, nch=nch, **kw[tb])
            self.live_banks.difference_update(banks[tb])
            if post is not None:
                post(tb)
        stats(0); stats(1); apply(0); stats(2); apply(1); apply(2)

    def ffn(self, l, f, m0, ln_idx, ada_hook=None, parts=3, premod=False, post=None):
        if not premod:
            self.modulate(l, m0, m0 + 1)
        w1 = self.dr['ffn_w1'][l, f]
        w3 = self.dr['ffn_w3'][l, f]
        w2 = self.dr['ffn_w2'][l, f]
        loads = []
        for ng in range(NJ // 2):
            loads.append((w1[:, ng * 256:(ng + 1) * 256].rearrange("(k p) n -> p k n", p=128), 8, 256))
            loads.append((w3[:, ng * 256:(ng + 1) * 256].rearrange("(k p) n -> p k n", p=128), 8, 256))
        for c in range(KC):
            loads.append((w2[0:11 * 128, c * 128:(c + 1) * 128].rearrange("(j p) n -> p j n", p=128), 11, 128))
            loads.append((w2[11 * 128:22 * 128, c * 128:(c + 1) * 128].rearrange("(j p) n -> p j n", p=128), 11, 128))
        slots = []

        def issue(upto):
            while len(slots) <= min(upto, len(loads) - 1):
                a, rws, cls = loads[len(slots)]
                slots.append(self.wload(a, rows=rws, cols=cls))
        for ng in range(NJ // 2):
            issue(min(2 * (ng + 2) + 1, 21 if parts < 2 else 99))
            s1, s3 = slots[2 * ng], slots[2 * ng + 1]
            for jl in range(2):
                j = ng * 2 + jl
                for tb in range(NTB):
                    sl = slice(tb * TB, (tb + 1) * TB)
                    ba, bb = self.bank(), self.bank()
                    for (bk, sw) in ((ba, s1), (bb, s3)):
                        for k in range(KC):
                            self.mm(self.ps[bk], sw[1][:, k, jl * 128:(jl + 1) * 128], self.H[:, k, sl],
                                    k == 0, k == KC - 1, [('wf', sw[0]), ('H', k, tb)], [('ps', bk)])
                    t = self.tmp()
                    s_ = self.TMP[t]
                    self.act(s_, self.ps[ba], AF.Silu, [], [('ps', ba), ('tmp', t)])
                    out = self.G[:, j, sl]
                    self.dve(lambda e, out=out, s_=s_, bb=bb: e.tensor_tensor(out=out, in0=self.ps[bb], in1=s_, op=ALU.mult),
                             [('tmp', t)], [('ps', bb), ('G', j, tb)])
            if ada_hook is not None:
                ada_hook()
        if parts < 2:
            return
        rd_gate = self.ada_res(l, m0 + 2)
        for c in range(KC):
            issue(22 + 2 * (c + 2) + 1)
            sa, sb = slots[22 + 2 * c], slots[22 + 2 * c + 1]
            for tb in range(NTB):
                sl = slice(tb * TB, (tb + 1) * TB)
                r = self.cond_of(tb)
                b = self.bank()
                for j in range(NJ):
                    sw = sa if j < 11 else sb
                    self.mm(self.ps[b], sw[1][:, j % 11, :], self.G[:, j, sl],
                            j == 0, j == NJ - 1, [('wf', sw[0]), ('G', j, tb)], [('ps', b)])
                t = self.tmp()
                y_ = self.TMP[t]
                self.act(y_, self.ps[b], AF.Identity, rd_gate, [('ps', b), ('tmp', t)], scale=self.mod(l, m0 + 2, c, r))
                xs = self.X[:, c, sl]
                self.dve(lambda e, xs=xs, y_=y_: e.scalar_tensor_tensor(out=xs, in0=xs, scalar=ALPHA, in1=y_,
                                                                       op0=ALU.mult, op1=ALU.add),
                         [('tmp', t)], [('X', c, tb)])
            if ada_hook is not None:
                ada_hook()
        if parts < 3:
            return
        self.layernorm_all(CO_LNG + ln_idx * 8, CO_LNB + ln_idx * 8, post=post)

    def resid_ln(self, l, m_gate, ps_b, c, tb, rd_gate):
        sl = slice(tb * TB, (tb + 1) * TB)
        r = self.cond_of(tb)
        t = self.tmp()
        y_ = self.TMP[t]
        self.act(y_, self.ps[ps_b], AF.Identity, rd_gate, [('ps', ps_b), ('tmp', t)], scale=self.mod(l, m_gate, c, r))
        xs = self.X[:, c, sl]
        self.dve(lambda e, xs=xs, y_=y_: e.scalar_tensor_tensor(out=xs, in0=xs, scalar=ALPHA, in1=y_,
                                                               op0=ALU.mult, op1=ALU.add),
                 [('tmp', t)], [('X', c, tb)])

    def mixer0(self, l=0, premod=False, post=None):
        if not premod:
            self.modulate(l, 3, 4)
        go = self.g_off
        CB = self.view(go, (4, 6, 286), BF16)
        DT = self.view(go + 13824, (4, T), BF16)
        CO = self.view(go + 26112, (4, T), F32)
        DG = [self.view(go + 50688 + i * 7936, (31, 128), BF16) for i in range(2)]
        PA = self.view(go + 26112, (2, 6, 271), F32)
        PBf = self.view(go + 26112 + 13024, (2, 6, 271), F32)
        HPs = [self.view(go + 26112 + 26048 + i * 6144, (6, 256), F32) for i in range(2)]
        AB = self.H
        flag = self.vcol(CO_FLAG)
        w_in = self.dr['cp_w_in'][0]

        def win(c0):
            return self.wload(w_in[:, c0:c0 + 256].rearrange("(k p) n -> p k n", p=128))
        self.dve(lambda e: e.memset(CB, 0.0), [], ['CB'])
        for pair in range(2):
            su = win(pair * 256)
            sg = win(512 + pair * 256)
            for il in range(2):
                i = pair * 2 + il
                for tb in range(NTB):
                    sl = slice(tb * TB, (tb + 1) * TB)
                    bu, bg = self.bank(), self.bank()
                    for (bk, sw) in ((bu, su), (bg, sg)):
                        for k in range(KC):
                            self.mm(self.ps[bk], sw[1][:, k, il * 128:(il + 1) * 128], self.H[:, k, sl],
                                    k == 0, k == KC - 1, [('wf', sw[0]), ('H', k, tb)], [('ps', bk)])
                    t = self.tmp()
                    s_ = self.TMP[t]
                    self.act(s_, self.ps[bg], AF.Sigmoid, [], [('ps', bg), ('tmp', t)])
                    out = CB[:, i, 2 * tb:2 * tb + 2, 15:271]
                    in0 = self.ps[bu].rearrange("p (s u) -> p s u", s=2)
                    in1 = s_.rearrange("p (s u) -> p s u", s=2)
                    self.dve(lambda e, out=out, in0=in0, in1=in1: e.tensor_tensor(out=out, in0=in0, in1=in1, op=ALU.mult),
                             [('tmp', t), 'CB'], [('ps', bu), ('CBc', i, tb)])
        for i in range(4):
            cbk = [('CBc', i, tb) for tb in range(NTB)]
            o1, i1 = CB[:, i, 1:4, 0:15], CB[:, i, 0:3, 256:271]
            o2, i2 = CB[:, i, 0:3, 271:286], CB[:, i, 1:4, 15:30]
            self.dve(lambda e, o1=o1, i1=i1: e.tensor_scalar_mul(out=o1, in0=i1, scalar1=flag), cbk + ['vec'], [('CBh', i, 0)])
            self.dve(lambda e, o2=o2, i2=i2: e.tensor_scalar_mul(out=o2, in0=i2, scalar1=flag), cbk + ['vec'], [('CBh', i, 1)])
        spw = self.wload(self.dr['pool_w'][0].rearrange("g p n -> p g n"), rows=4, cols=128)
        PDs = [[PA[:, 0], PBf[:, 0]], [PA[:, 1], PBf[:, 1]]]
        ICE = self.ST[0][:, 0:384].rearrange("p (g s e) -> p g s e", g=4, s=6)
        STEPS = [(1, 0, 1, 271, 0, 1), (0, 1, 2, 270, 1, 3), (1, 0, 4, 268, 2, 6), (0, 1, 8, 264, 4, 12)]

        def halo_zero(buf, key):
            self.dve(lambda e: e.memset(buf[:, :, 0:8], 0.0), [], [key])
            self.dve(lambda e: e.memset(buf[:, :, 264:271], 0.0), [], [key])

        def halo_flag(buf, rd, key):
            self.dve(lambda e: e.tensor_scalar_mul(out=buf[:, 1:4, 0:8], in0=buf[:, 0:3, 256:264], scalar1=flag), rd + ['vec'], [key])
            self.dve(lambda e: e.tensor_scalar_mul(out=buf[:, 0:3, 264:271], in0=buf[:, 1:4, 8:15], scalar1=flag), rd + ['vec'], [key])

        def run_steps(pr, nsteps, first_rd, on_step=None):
            PD = PDs[pr]
            rd = first_rd
            for si in range(nsteps):
                di, si_, lo, hi, a0_, a1_ = STEPS[si]
                n = hi - lo
                o, x0, x1 = PD[di][:, :, lo:hi], PD[si_][:, :, a0_:a0_ + n], PD[si_][:, :, a1_:a1_ + n]
                key = ('PS', pr, di)
                self.dve(lambda e, o=o, x0=x0, x1=x1: e.tensor_tensor(out=o, in0=x0, in1=x1, op=ALU.add), rd, [key])
                rd = [key]
                if on_step is not None:
                    on_step(si, PD[di], key)
            return rd
        PD = PDs[1]
        self.dve(lambda e: e.memset(PD[0], 0.0), [], [('PS', 1, 0)])
        self.dve(lambda e: e.memset(PD[0][:, :, 8:264], 1.0), [], [('PS', 1, 0)])
        halo_flag(PD[0], [], ('PS', 1, 0))

        def grab(si, buf, key):
            self.dve(lambda e: e.reciprocal(out=ICE[:, si, :, 0:8], in_=buf[:, :, 8:16]), [key], [('ICE', si)])
            self.dve(lambda e: e.reciprocal(out=ICE[:, si, :, 8:16], in_=buf[:, :, 256:264]), [key], [('ICE', si)])
        run_steps(1, 4, [('PS', 1, 0)], on_step=grab)
        shs = {}

        def pool_A(gi):
                pair, il = divmod(gi, 2)
                if pair not in shs:
                    shs[pair] = win(1024 + pair * 256)
                sh = shs[pair]
                pr = gi % 2
                PD = PDs[pr]
                HP = HPs[pr]
                halo_zero(PD[0], ('PS', pr, 0))
                for tb in range(NTB):
                    sl = slice(tb * TB, (tb + 1) * TB)
                    b = self.bank()
                    for k in range(KC):
                        self.mm(self.ps[b], sh[1][:, k, il * 128:(il + 1) * 128], self.H[:, k, sl],
                                k == 0, k == KC - 1, [('wf', sh[0]), ('H', k, tb)], [('ps', b)])
                    src = self.ps[b].rearrange("p (s u) -> p s u", s=2)
                    self.act(PD[0][:, 2 * tb:2 * tb + 2, 8:264], src, AF.Copy, [], [('ps', b), ('PS', pr, 0)])
                    self.act(HP[:, 2 * tb:2 * tb + 2, :], src, AF.Copy, [], [('ps', b), ('HP', pr, tb)])
                halo_flag(PD[0], [], ('PS', pr, 0))
                rd = run_steps(pr, gi + 1, [('PS', pr, 0)])
                fin = PD[STEPS[gi][0]]
                fkey = ('PS', pr, STEPS[gi][0])
                w_ = float(2 ** (gi + 1))
                sd = fin[:, :, 8:264]
                dt = DT[:, gi, :].rearrange("p (s u) -> p s u", s=6)
                hpk = [('HP', pr, tb) for tb in range(NTB)]
                self.dve(lambda e, dt=dt, sd=sd, w_=w_, HP=HP: e.scalar_tensor_tensor(out=dt, in0=sd, scalar=1.0 / w_, in1=HP,
                                                                            op0=ALU.mult, op1=ALU.subtract),
                         [fkey] + hpk, [('DT', gi)])
                for (c0, e0) in ((0, 0), (248, 8)):
                    se = fin[:, :, 8 + c0:16 + c0]
                    ie = ICE[:, gi, :, e0:e0 + 8]
                    de = dt[:, :, c0:c0 + 8]
                    he = HP[:, :, c0:c0 + 8]
                    self.dve(lambda e, se=se, ie=ie: e.tensor_tensor(out=se, in0=se, in1=ie, op=ALU.mult), [('ICE', gi)], [fkey])
                    self.dve(lambda e, de=de, se=se, he=he: e.tensor_tensor(out=de, in0=se, in1=he, op=ALU.subtract), hpk, [fkey, ('DT', gi)])

        def pool_B(gi):
                for tb in range(NTB):
                    sl = slice(tb * TB, (tb + 1) * TB)
                    b = self.bank()
                    self.mm(self.ps[b], spw[1][:, gi, :], DT[:, gi, sl], True, True, [('wf', spw[0]), ('DT', gi)], [('ps', b)])
                    self.act(DT[:, gi, sl], self.ps[b], AF.Identity, ['vec'], [('ps', b), ('DTB', gi, tb)],
                             scale=self.vcol(CO_PSC + gi))
        pool_A(0); pool_A(1); pool_B(0); pool_A(2); pool_B(1); pool_A(3); pool_B(2); pool_B(3)
        for i in range(4):
            dg = DG[i % 2]
            for k in range(31):
                o = dg[:, k, :]
                wc = self.vcol(CO_CW + i * 31 + k)
                self.dve(lambda e, o=o, wc=wc: e.tensor_scalar_mul(out=o, in0=self.IDB, scalar1=wc),
                         ['idb', 'vec'], [('DG', i % 2)])
            for tb in range(NTB):
                sl = slice(tb * TB, (tb + 1) * TB)
                b = self.bank()
                for k in range(31):
                    self.mm(self.ps[b], dg[:, k, :], CB[:, i, 2 * tb:2 * tb + 2, k:k + 256], k == 0, k == 30,
                            [('DG', i % 2), ('CBh', i, 0), ('CBh', i, 1)] + [('CBc', i, t2) for t2 in range(NTB)], [('ps', b)])
                self.act(CO[:, i, sl], self.ps[b], AF.Identity, ['vec'], [('ps', b), ('CO', i, tb)],
                         bias=self.vcol(CO_CB + i))
        def conv_ln_args(tb):
            sl = slice(tb * TB, (tb + 1) * TB)
            return dict(src=lambda c, sl=sl: CO[:, c, sl], skey=lambda c, tb=tb: ('CO', c, tb),
                        dst=lambda c, sl=sl: AB[:, c, sl], dkey=lambda c, tb=tb: ('H', c, tb), func=AF.Silu)
        self.layernorm_all(CO_CNG, CO_CNB, nch=4, mk=conv_ln_args)
        w_out = self.dr['cp_w_out'][0]
        rd_gate = self.ada_res(l, 5)
        for pair in range(4):
            so = self.wload(w_out[:, pair * 256:(pair + 1) * 256].rearrange("(k p) n -> p k n", p=128))
            for il in range(2):
                c = pair * 2 + il
                for tb in range(NTB):
                    sl = slice(tb * TB, (tb + 1) * TB)
                    b = self.bank()
                    for j in range(KC):
                        rhs = AB[:, j, sl] if j < 4 else DT[:, j - 4, sl]
                        rk = ('H', j, tb) if j < 4 else ('DTB', j - 4, tb)
                        self.mm(self.ps[b], so[1][:, j, il * 128:(il + 1) * 128], rhs, j == 0, j == KC - 1,
                                [('wf', so[0]), rk], [('ps', b)])
                    self.resid_ln(l, 5, b, c, tb, rd_gate)
        self.layernorm_all(CO_LNG + (l * 3 + 1) * 8, CO_LNB + (l * 3 + 1) * 8, post=post)

    def rms_block(self, tb, nch, wslices, g_col, dst_f32=None, dst_bf=None, key=None, alt=0):
        sl = slice(tb * TB, (tb + 1) * TB)
        QD = self.mla_bufs['QD'][alt]
        qk = 'QD%d' % alt
        bq = self.bank()
        for c in range(nch):
            b = self.bank()
            slot, lf = wslices[c]
            for k in range(KC):
                self.mm(self.ps[b], lf(k), self.H[:, k, sl], k == 0, k == KC - 1, [('wf', slot), ('H', k, tb)], [('ps', b)])
            self.act(QD[:, c, :], self.ps[b], AF.Copy, [], [('ps', b), (qk, c)])
            i = self.xb_rr
            self.xb_rr = (i + 1) % 4
            self.act(self.XB[i], QD[:, c, :], AF.Square, [(qk, c)], [('xb', i)])
            self.mm(self.ps[bq], self.ONES, self.XB[i], c == 0, c == nch - 1, ['ones', ('xb', i)], [('ps', bq)])
        var, rstd = (self.ST[2], self.ST[3]) if alt == 0 else (self.ST[0], self.ST[1])
        k2, k3 = (('st', 2), ('st', 3)) if alt == 0 else (('st', 0), ('st', 1))
        self.act(var, self.ps[bq], AF.Sqrt, ['eps2'], [('ps', bq), k2], bias=self.EPS_RMS, scale=1.0 / (nch * 128))
        self.dve(lambda e: e.reciprocal(out=rstd, in_=var), [k2], [k3])
        for c in range(nch):
            qd = QD[:, c, :]
            g = self.vcol(g_col + c)
            if dst_f32 is not None:
                o = dst_f32[:, c, sl]
                self.dve(lambda e, o=o, qd=qd, g=g: e.scalar_tensor_tensor(out=o, in0=qd, scalar=g, in1=rstd,
                                                                          op0=ALU.mult, op1=ALU.mult),
                         [(qk, c), k3, 'vec'], [(key + 'f', c, tb)])
                ob = dst_bf[:, c, sl]
                self.act(ob, o, AF.Copy, [(key + 'f', c, tb)], [(key, c, tb)])
            else:
                ob = dst_bf[:, c, sl]
                self.dve(lambda e, ob=ob, qd=qd, g=g: e.scalar_tensor_tensor(out=ob, in0=qd, scalar=g, in1=rstd,
                                                                            op0=ALU.mult, op1=ALU.mult),
                         [(qk, c), k3, 'vec'], [(key, c, tb)])

    def mla(self, l=1, premod=False, post=None):
        if not premod:
            self.modulate(l, 3, 4)
        self.wf_rr = 0
        self.nbank_rr = 8
        go = self.g_off
        QN = self.view(go, (3, T), BF16)
        CKVb = self.view(go + 9216, (2, 2048), BF16)
        KRb = self.view(go + 17408, (2048,), BF16)
        KT = [self.view(go + 21504 + i * 4096, (2048,), BF16) for i in range(2)]
        VH = [self.view(go + 29696 + i * 4096, (16, 128), BF16) for i in range(2)]
        QT = self.view(go + 37888, (T,), BF16)
        QR = [self.view(go + 40960 + i * 3072, (T,), BF16) for i in range(2)]
        PP = [self.view(go + 47104 + i * 3072, (T,), BF16) for i in range(2)]
        PT = self.view(go + 53248, (12, 512), BF16)
        RB = self.view(go + 65536, (512,), F32)
        CKVf = self.view(go + 21504, (2, T), F32)
        KRraw = self.view(go + 33792, (T,), F32)
        QDall = self.view(go + 39936, (3, 3, TB), F32)
        OST = [self.view(go + 58368 + i * 1280, (320,), F32) for i in range(2)]
        CST = [self.view(go + 60928 + i * 1280, (320,), F32) for i in range(4)]
        ROC = self.WA[0].rearrange("p a b -> p (a b)").bitcast(F32)
        ROS = self.WA[1].rearrange("p a b -> p (a b)").bitcast(F32)
        OT = self.H
        P = self.P
        P.op('sp', lambda e: e.dma_start(out=ROC[0:64, :], in_=self.dr['ropeC']), writes=[('wa', 0)], dma='c0')
        P.op('sp', lambda e: e.dma_start(out=ROS[0:64, :], in_=self.dr['ropeS']), writes=[('wa', 1)], dma='c1')
        P.op('pool', lambda e: e.dma_start(out=KRb[64:68, 0:1024], in_=self.dr['maskk'][:, 0:1024]), writes=['KRm0'], dma='m0')
        P.op('pool', lambda e: e.dma_start(out=KRb[64:68, 1536:2048], in_=self.dr['maskk'][:, 1024:1536]), writes=['KRm1'], dma='m1')
        for ct in range(4):
            P.op('sp', lambda e, ct=ct: e.dma_start(out=CST[ct][:, 0:256], in_=self.dr['cache_ckv'][ct * 128:(ct + 1) * 128, :]),
                 writes=[('cst', ct, 0)], dma='cs%d' % ct)
            P.op('sp', lambda e, ct=ct: e.dma_start(out=CST[ct][:, 256:320], in_=self.dr['cache_kr'][ct * 128:(ct + 1) * 128, :]),
                 writes=[('cst', ct, 1)], dma='ck%d' % ct)
        for ct in range(4):
            b = self.bank()
            for c in range(2):
                self.tr(self.ps[b][:, c * 128:(c + 1) * 128], CST[ct][:, c * 128:(c + 1) * 128], self.IDF,
                        [('cst', ct, 0), 'idf'], [('ps', b)], inc=False)
            self.tr(self.ps[b][0:64, 256:384], CST[ct][:, 256:320], self.IDF, [('cst', ct, 1), 'idf'], [('ps', b)])
            ks = slice(1536 + ct * 128, 1536 + (ct + 1) * 128)
            self.act(CKVb[:, :, ks], self.ps[b][:, 0:256].rearrange("p (c t) -> p c t", c=2), AF.Copy, [],
                     [('ps', b), ('CKVb', 3)])
            self.dve(lambda e, ks=ks, b=b: e.tensor_copy(out=KRb[0:64, ks], in_=self.ps[b][0:64, 256:384]), [],
                     [('ps', b), ('KRb', 3)])
        wdq = self.dr['mla_w_dq'][0]
        wdkv = self.dr['mla_w_dkv'][0]
        sq0 = self.wload(wdq[:, 0:256].rearrange("(k p) n -> p k n", p=128))
        sq1 = self.wload(wdq[:, 256:384].rearrange("(k p) n -> p k n", p=128), rows=8, cols=128)
        sk0 = self.wload(wdkv[:, 0:256].rearrange("(k p) n -> p k n", p=128))
        sk1 = self.wload(wdkv[:, 256:320].rearrange("(k p) n -> p k n", p=128), rows=8, cols=64)
        sk2 = self.wload(self.dr['w_dkv_sw'].rearrange("(k p) n -> p k n", p=128), rows=8, cols=64)
        qsl = [(sq0[0], lambda k: sq0[1][:, k, 0:128]), (sq0[0], lambda k: sq0[1][:, k, 128:256]), (sq1[0], lambda k: sq1[1][:, k, :])]
        ksl = [(sk0[0], lambda k: sk0[1][:, k, 0:128]), (sk0[0], lambda k: sk0[1][:, k, 128:256])]
        pp_ = [6]

        def pbank():
            pp_[0] = 13 - pp_[0]
            return pp_[0]
        for tb in range(NTB):
            sl = slice(tb * TB, (tb + 1) * TB)
            for (nch, wsl, raw, rk, sbank) in ((3, qsl, lambda c: QDall[:, tb, c, :], 'QDr', tb),
                                               (2, ksl, lambda c: CKVf[:, c, sl], 'CKVr', 3 + tb)):
                for c in range(nch):
                    b = pbank()
                    slot, lf = wsl[c]
                    for k in range(KC):
                        self.mm(self.ps[b], lf(k), self.H[:, k, sl], k == 0, k == KC - 1, [('wf', slot), ('H', k, tb)], [('ps', b)])
                    self.act(raw(c), self.ps[b], AF.Copy, [], [('ps', b), (rk, c, tb)])
                    i = self.xb_rr
                    self.xb_rr = (i + 1) % 4
                    self.act(self.XB[i], raw(c), AF.Square, [(rk, c, tb)], [('xb', i)])
                    self.mm(self.ps[sbank], self.ONES, self.XB[i], c == 0, c == nch - 1, ['ones', ('xb', i)], [('ps', sbank)], inc=True)
            br = pbank()
            for k in range(KC):
                self.mm(self.ps[br][0:64, :], sk1[1][:, k, :], self.H[:, k, sl], k == 0, k == KC - 1,
                        [('wf', sk1[0]), ('H', k, tb)], [('ps', br)])
            self.act(KRraw[0:64, sl], self.ps[br][0:64, :], AF.Copy, [], [('ps', br), ('KRraw', tb)])
            if tb < 2:
                bs = pbank()
                for k in range(KC):
                    self.mm(self.ps[bs][0:64, :], sk2[1][:, k, :], self.H[:, k, sl], k == 0, k == KC - 1,
                            [('wf', sk2[0]), ('H', k, tb)], [('ps', bs)])
                t1, t2 = self.tmp(), self.tmp()
                a1, a2 = self.TMP[t1][0:64, :], self.TMP[t2][0:64, :]
                self.dve(lambda e, a1=a1, sl=sl: e.tensor_tensor(out=a1, in0=KRraw[0:64, sl], in1=ROC[0:64, sl], op=ALU.mult),
                         [('KRraw', tb), ('wa', 0)], [('tmp', t1)])
                self.dve(lambda e, a2=a2, sl=sl, bs=bs: e.tensor_tensor(out=a2, in0=self.ps[bs][0:64, :], in1=ROS[0:64, sl], op=ALU.mult),
                         [('wa', 1)], [('ps', bs), ('tmp', t2)])
                self.dve(lambda e, a1=a1, a2=a2, sl=sl: e.tensor_tensor(out=KRb[0:64, sl], in0=a1, in1=a2, op=ALU.add),
                         [('tmp', t1), ('tmp', t2)], [('KRb', tb)])
            else:
                self.dve(lambda e, sl=sl: e.tensor_copy(out=KRb[0:64, sl], in_=KRraw[0:64, sl]), [('KRraw', tb)], [('KRb', tb)])
        for tb in range(NTB):
            sl = slice(tb * TB, (tb + 1) * TB)
            for (nch, g_col, raw, rk, sbank, vi) in ((3, CO_QG, lambda c: QDall[:, tb, c, :], 'QDr', tb, 0),
                                                     (2, CO_KVG, lambda c: CKVf[:, c, sl], 'CKVr', 3 + tb, 2)):
                var, rstd = self.ST[vi], self.ST[vi + 1]
                kv_, kr_ = ('st', vi), ('st', vi + 1)
                self.act(var, self.ps[sbank], AF.Identity, ['eps2'], [('ps', sbank), kv_], bias=self.EPS_RMS, scale=1.0 / (nch * 128))
                self.act(var, var, AF.Ln, [], [kv_])
                self.act(rstd, var, AF.Exp, [kv_], [kr_], scale=-0.5)
                for c in range(nch):
                    g = self.vcol(g_col + c)
                    x_ = raw(c)
                    if nch == 3:
                        ob = QN[:, c, sl]
                        self.dve(lambda e, ob=ob, x_=x_, g=g, rstd=rstd: e.scalar_tensor_tensor(out=ob, in0=x_, scalar=g, in1=rstd,
                                                                                              op0=ALU.mult, op1=ALU.mult),
                                 [(rk, c, tb), kr_, 'vec'], [('QN', c, tb)])
                    else:
                        self.dve(lambda e, x_=x_, g=g, rstd=rstd: e.scalar_tensor_tensor(out=x_, in0=x_, scalar=g, in1=rstd,
                                                                                        op0=ALU.mult, op1=ALU.mult),
                                 [kr_, 'vec'], [(rk, c, tb), ('CKVf', c, tb)])
                        self.act(CKVb[:, c, sl], x_, AF.Copy, [('CKVf', c, tb)], [('CKV', c, tb)])
        for tt in range(T // 128):
            s = tt % 2
            ts_ = slice(tt * 128, (tt + 1) * 128)
            tb = tt // 4
            b = self.bank()
            for c in range(2):
                self.tr(self.ps[b][:, c * 128:(c + 1) * 128], CKVf[:, c, ts_], self.IDF, [('CKVf', c, tb), 'idf'], [('ps', b)], inc=False)
            self.tr(self.ps[b][:, 256:320], KRraw[0:64, ts_], self.IDF[0:64, 0:64], [('KRraw', tb), 'idf'], [('ps', b)])
            self.act(OST[s], self.ps[b][:, 0:320], AF.Copy, [], [('ps', b), ('ost', s)])
            P.op('sp', lambda e, s=s, ts_=ts_: e.dma_start(out=self.dr['ockv'][ts_, :], in_=OST[s][:, 0:256]), reads=[('ost', s)], dma='oa%d' % s)
            P.op('sp', lambda e, s=s, ts_=ts_: e.dma_start(out=self.dr['okr'][ts_, :], in_=OST[s][:, 256:320]), reads=[('ost', s)], dma='ob%d' % s)
        wuq = self.dr['mla_w_uq'][0]
        wukv = self.dr['mla_w_ukv'][0]
        suq = [self.wload(wuq[:, j * 512:(j + 1) * 512].rearrange("(k p) n -> p k n", p=128), rows=3, cols=512, slot=j) for j in range(3)]
        ssw = self.wload(self.dr['w_uq_sw'].rearrange("(k p) n -> p k n", p=128), rows=3, cols=512, slot=3)
        P.fence(('pool',))
        for i in range(2):
            P.op('pool', lambda e, i=i: e.dma_start(out=QR[i][64:68, 0:1024], in_=self.dr['onehot']), writes=[('QRm', i)], dma='m%d' % (2 + i))

        def uq(col0, width):
            j, o = divmod(col0, 512)
            assert o + width <= 512
            return suq[j][0], (lambda k: suq[j][1][:, k, o:o + width])
        scale = float((128 + 64) ** -0.5)
        groups = [(0, 8, [(0, 512), (512, 512), (1536, 512)], 68),
                  (8, 2, [(1024, 256)], 64), (10, 2, [(1280, 256)], 64)]
        for h in range(8):
            hb = h % 2
            swv = self.wload(wukv[:, h * 256:(h + 1) * 256].rearrange("(k p) n -> p k n", p=128), rows=2, cols=256, slot=4 + hb)
            for kb in range(4):
                b = self.bank()
                ks = slice(kb * 512, (kb + 1) * 512)
                for c in range(2):
                    self.mm(self.ps[b], swv[1][:, c, 0:128], CKVb[:, c, ks], c == 0, c == 1,
                            [('wf', swv[0])] + [('CKVb', i) for i in range(4)] + [('CKV', c, i) for i in range(3)], [('ps', b)])
                if kb % 2 == 0:
                    self.act(KT[hb][:, ks], self.ps[b], AF.Copy, [], [('ps', b), ('KT', hb)])
                else:
                    self.dve(lambda e, hb=hb, ks=ks, b=b: e.tensor_copy(out=KT[hb][:, ks], in_=self.ps[b]), [], [('ps', b), ('KT', hb)])
            for kq in range(4):
                b = self.bank()
                for q in range(4):
                    kt = kq * 4 + q
                    for c in range(2):
                        self.mm(self.ps[b][:, q * 128:(q + 1) * 128], CKVb[:, c, kt * 128:(kt + 1) * 128], swv[1][:, c, 128:256],
                                c == 0, c == 1, [('wf', swv[0])], [('ps', b)])
                src = self.ps[b].rearrange("p (q d) -> p q d", q=4)
                if kq % 2 == 0:
                    self.dve(lambda e, hb=hb, kq=kq, src=src: e.tensor_copy(out=VH[hb][:, kq * 4:(kq + 1) * 4, :], in_=src), [],
                             [('ps', b), ('VH', hb)])
                else:
                    self.act(VH[hb][:, kq * 4:(kq + 1) * 4, :], src, AF.Copy, [], [('ps', b), ('VH', hb)])
            for tb in range(NTB):
                sl = slice(tb * TB, (tb + 1) * TB)
                b = self.bank()
                slot, lf = uq(h * 192, 128) if (h * 192) % 512 + 128 <= 512 else (None, None)
                for c in range(3):
                    if slot is not None:
                        self.mm(self.ps[b], lf(c), QN[:, c, sl], c == 0, c == 2, [('wf', slot), ('QN', c, tb)], [('ps', b)])
                    else:
                        j0, o0 = divmod(h * 192, 512)
                        w0 = 512 - o0
                        self.mm(self.ps[b][0:w0, :], suq[j0][1][:, c, o0:512], QN[:, c, sl], c == 0, c == 2,
                                [('wf', suq[j0][0]), ('QN', c, tb)], [('ps', b)])
                if slot is None:
                    for c in range(3):
                        self.mm(self.ps[b][w0:128, :], suq[j0 + 1][1][:, c, 0:128 - w0], QN[:, c, sl], c == 0, c == 2,
                                [('wf', suq[j0 + 1][0]), ('QN', c, tb)], [('ps', b)])
                self.act(QT[:, sl], self.ps[b], AF.Copy, [], [('ps', b), ('QT', tb)])
                br = self.bank()
                sr, lr = uq(h * 192 + 128, 64)
                for c in range(3):
                    self.mm(self.ps[br][0:64, :], lr(c), QN[:, c, sl], c == 0, c == 2, [('wf', sr), ('QN', c, tb)], [('ps', br)])
                if tb < 2:
                    bsw = self.bank()
                    for c in range(3):
                        self.mm(self.ps[bsw][0:64, :], ssw[1][:, c, h * 64:(h + 1) * 64], QN[:, c, sl], c == 0, c == 2,
                                [('wf', ssw[0]), ('QN', c, tb)], [('ps', bsw)])
                    t1, t2 = self.tmp(), self.tmp()
                    a1, a2 = self.TMP[t1][0:64, :], self.TMP[t2][0:64, :]
                    self.dve(lambda e, a1=a1, sl=sl, br=br: e.tensor_tensor(out=a1, in0=self.ps[br][0:64, :], in1=ROC[0:64, sl], op=ALU.mult),
                             [('wa', 0)], [('ps', br), ('tmp', t1)])
                    self.dve(lambda e, a2=a2, sl=sl, bsw=bsw: e.tensor_tensor(out=a2, in0=self.ps[bsw][0:64, :], in1=ROS[0:64, sl], op=ALU.mult),
                             [('wa', 1)], [('ps', bsw), ('tmp', t2)])
                    self.dve(lambda e, a1=a1, a2=a2, sl=sl, hb=hb: e.tensor_tensor(out=QR[hb][0:64, sl], in0=a1, in1=a2, op=ALU.add),
                             [('tmp', t1), ('tmp', t2)], [('QR', hb, tb)])
                else:
                    self.act(QR[hb][0:64, sl], self.ps[br][0:64, :], AF.Copy, [], [('ps', br), ('QR', hb, tb)])
            items = []
            kbA = [(0, 512), (512, 512), (1536, 512)]
            ktA = [0, 1, 2, 3, 4, 5, 6, 7, 12, 13, 14, 15]
            for qb0 in (0, 4):
                for qi in range(4):
                    items.append(dict(qt=qb0 + qi, qi=qi, nq=4, qb0=qb0, kblocks=kbA, krows=68, ptoff=0, ktiles=ktA,
                                      first=(qi == 0), last=(qi == 3), zero=False))
            for qi in range(4):
                kb = [(1024, 256)] if qi < 2 else [(1280, 256)]
                items.append(dict(qt=8 + qi, qi=qi, nq=4, qb0=8, kblocks=kb, krows=64, ptoff=0 if qi < 2 else 2,
                                  ktiles=[8, 9, 10, 11], first=(qi == 0), last=(qi == 3), zero=(qi == 0)))

            def stage_qk(it):
                qt = it['qt']
                qs = slice(qt * 128, (qt + 1) * 128)
                tbq = qt // 4
                it['banks'] = []
                for bi_, (k0, kw) in enumerate(it['kblocks']):
                    b = (qt % 2) * 3 + bi_
                    it['banks'].append(b)
                    self.mm(self.ps[b][:, 0:kw], QT[:, qs], KT[hb][:, k0:k0 + kw], True, False,
                            [('QT', tbq), ('KT', hb)], [('ps', b)])
                    rdm = ['KRm0', 'KRm1', ('QRm', hb)] if it['krows'] == 68 else []
                    self.mm(self.ps[b][:, 0:kw], QR[hb][0:it['krows'], qs], KRb[0:it['krows'], k0:k0 + kw], False, True,
                            [('QR', hb, tbq)] + [('KRb', i) for i in range(4)] + rdm, [('ps', b)])

            def stage_max(it):
                qt = it['qt']
                par = qt % 2
                so = par * 8
                banks = it['banks']
                nb = len(banks)
                for bi, b in enumerate(banks):
                    kw = it['kblocks'][bi][1]
                    self.dve(lambda e, bi=bi, b=b, kw=kw, so=so: e.reduce_max(out=self.SM[:, so + bi:so + bi + 1], in_=self.ps[b][:, 0:kw],
                                                                           axis=mybir.AxisListType.X), [], [('ps', b), ('sm_mx', par)])
                if nb > 1:
                    self.dve(lambda e, so=so, nb=nb: e.reduce_max(out=self.SM[:, so + 3:so + 4], in_=self.SM[:, so:so + nb],
                                                                  axis=mybir.AxisListType.X), [], [('sm_mx', par)])
                    src_c = so + 3
                else:
                    src_c = so
                self.dve(lambda e, so=so, src_c=src_c: e.tensor_scalar_mul(out=self.SM[:, so + 4:so + 5], in0=self.SM[:, src_c:src_c + 1],
                                                                        scalar1=-scale), [], [('sm_mx', par), ('sm_nb', par)])

            def stage_exp(it):
                qt = it['qt']
                par = qt % 2
                so = par * 8
                pp = PP[par]
                ko = 0
                for bi, b in enumerate(it['banks']):
                    kw = it['kblocks'][bi][1]
                    self.act(pp[:, ko:ko + kw], self.ps[b][:, 0:kw], AF.Exp, [('sm_nb', par)], [('ps', b), ('PP', par, bi)],
                             bias=self.SM[:, so + 4:so + 5], scale=scale)
                    ko += kw

            def stage_transpose(it):
                qt, qi = it['qt'], it['qi']
                pp = PP[qt % 2]
                nkt = sum(w for _, w in it['kblocks']) // 128
                po = it['ptoff']
                if it['zero']:
                    z1, z2 = PT[:, 2:4, 0:256], PT[:, 0:2, 256:512]
                    self.dve(lambda e, z1=z1: e.memset(z1, 0.0), [], [('PT', 0, 0), ('PT', 1, 0)])
                    self.dve(lambda e, z2=z2: e.memset(z2, 0.0), [], [('PT', 2, 0), ('PT', 3, 0)])
                for kq in range(0, nkt, 8):
                    b = 6 + kq // 8
                    pb = self.ps[b].bitcast(BF16)
                    n8 = min(8, nkt - kq)
                    for q in range(n8):
                        kt = kq + q
                        self.tr(pb[:, q * 128:(q + 1) * 128], pp[:, kt * 128:(kt + 1) * 128], self.IDB,
                                [('PP', qt % 2, kt // 4), 'idb'], [('ps', b)], inc=(q == n8 - 1))
                    src = pb[:, 0:n8 * 128].rearrange("p (q t) -> p q t", q=n8)
                    dst = PT[:, po + kq:po + kq + n8, qi * 128:(qi + 1) * 128]
                    if (kq // 8) % 2 == 0:
                        self.act(dst, src, AF.Copy, [], [('ps', b), ('PT', qi, kq // 8)])
                    else:
                        self.dve(lambda e, dst=dst, src=src: e.tensor_copy(out=dst, in_=src), [], [('ps', b), ('PT', qi, kq // 8)])

            def stage_pv(it):
                nq, qb0 = it['nq'], it['qb0']
                nqc = nq * 128
                ktiles = it['ktiles']
                bo, bsum = 6, 7
                rdpt = [('PT', q, g) for q in range(nq) for g in range((len(ktiles) + 7) // 8)]
                for i, ktg in enumerate(ktiles):
                    self.mm(self.ps[bo][:, 0:nqc], VH[hb][:, ktg, :], PT[:, i, 0:nqc], i == 0, i == len(ktiles) - 1,
                            [('VH', hb)] + rdpt, [('ps', bo)])
                for i, ktg in enumerate(ktiles):
                    self.mm(self.ps[bsum][:, 0:nqc], self.ONES, PT[:, i, 0:nqc], i == 0, i == len(ktiles) - 1,
                            ['ones'] + rdpt, [('ps', bsum)])
                self.act(RB[:, 0:nqc], self.ps[bsum][:, 0:nqc], AF.Copy, [], [('ps', bsum), 'RB'])
                self.act(RB[:, 0:nqc], RB[:, 0:nqc], AF.Ln, [], ['RB'])
                self.act(RB[:, 0:nqc], RB[:, 0:nqc], AF.Exp, [], ['RB'], scale=-1.0)
                q0 = qb0 * 128
                tbo = q0 // TB
                dst = OT[:, h, q0:q0 + nqc]
                self.dve(lambda e, dst=dst, bo=bo, nqc=nqc: e.tensor_tensor(out=dst, in0=self.ps[bo][:, 0:nqc], in1=RB[:, 0:nqc], op=ALU.mult),
                         ['RB'], [('ps', bo), ('H', h, tbo)])

            stage_qk(items[0])
            stage_max(items[0])
            for i, it in enumerate(items):
                if i + 1 < len(items):
                    stage_qk(items[i + 1])
                stage_exp(it)
                if i + 1 < len(items):
                    stage_max(items[i + 1])
                stage_transpose(it)
                if it['last']:
                    stage_pv(it)
        if self.debug == 91:
            for c in range(KC):
                for tb in range(NTB):
                    sl = slice(tb * TB, (tb + 1) * TB)
                    self.act(self.X[:, c, sl], OT[:, c, sl], AF.Copy, [('H', c, tb)], [('X', c, tb)])
            return
        w_o = self.dr['mla_w_o'][0]
        rd_gate = self.ada_res(l, 5)
        self.wf_rr = 6
        for pair in range(4):
            so = self.wload(w_o[:, pair * 256:(pair + 1) * 256].rearrange("(k p) n -> p k n", p=128))
            for il in range(2):
                c = pair * 2 + il
                for tb in range(NTB):
                    sl = slice(tb * TB, (tb + 1) * TB)
                    b = self.bank()
                    for j in range(KC):
                        self.mm(self.ps[b], so[1][:, j, il * 128:(il + 1) * 128], OT[:, j, sl], j == 0, j == KC - 1,
                                [('wf', so[0]), ('H', j, tb)], [('ps', b)])
                    self.resid_ln(l, 5, b, c, tb, rd_gate)
        self.layernorm_all(CO_LNG + (l * 3 + 1) * 8, CO_LNB + (l * 3 + 1) * 8, post=post)

    def store_x(self, tbs=None):
        xo = [self.view(self.stage_off, (D,), F32), self.view(self.stage_off + 4096, (D,), F32)]
        tiles = range(T // 128) if tbs is None else [tt for tb in tbs for tt in range(4 * tb, 4 * tb + 4)]
        for tt in tiles:
            s = tt % 2
            for half in range(2):
                b = self.bank()
                for q in range(4):
                    c = half * 4 + q
                    self.tr(self.ps[b][:, q * 128:(q + 1) * 128], self.X[:, c, tt * 128:(tt + 1) * 128], self.IDF,
                            [('X', c, tt // 4), 'idf'], [('ps', b)], inc=(q == 3))
                out = xo[s][:, half * 512:(half + 1) * 512]
                if half == 0:
                    self.act(out, self.ps[b], AF.Copy, [], [('ps', b), ('xo', s, 0)])
                else:
                    self.dve(lambda e, out=out, b=b: e.tensor_copy(out=out, in_=self.ps[b]), [], [('ps', b), ('xo', s, 1)])
            dst = self.dr['y'][tt * 128:(tt + 1) * 128, :]
            src = xo[s]
            self.P.op('sp', lambda e, dst=dst, src=src: e.dma_start(out=dst, in_=src),
                      reads=[('xo', s, 0), ('xo', s, 1)], dma='yo%d' % s)

    def build(self):
        st = self.debug or 99
        self.constants()
        nada = [0]

        def ada_next(n=1):
            while n > 0 and nada[0] < 72:
                k = min(n, 72 - nada[0], 2)
                self.ada_units([((nada[0] + j) // 36, (nada[0] + j) % 36) for j in range(k)])
                nada[0] += k
                n -= k
        self.load_x(hook=lambda: ada_next(2))
        hook = lambda: ada_next(2)
        if st < 99:
            self.ffn(0, 0, 0, 0, ada_hook=hook)
            if st >= 7:
                self.P.fence()
                self.mixer0(0)
                self.P.fence()
            if st >= 8:
                self.ffn(0, 1, 6, 2, ada_hook=hook)
                ada_next(72)
                self.ffn(1, 0, 0, 3)
            if st >= 9:
                self.P.fence(('pe', 'act', 'dve', 'sp', 'pool'))
                self.mla(1)
                self.P.fence()
            if st >= 10 and st != 91:
                self.ffn(1, 1, 6, 5)
        else:
            self.ffn(0, 0, 0, 0, ada_hook=hook, post=lambda tb: self.modulate(0, 3, 4, [tb]))
            self.P.fence()
            self.mixer0(0, premod=True, post=lambda tb: self.modulate(0, 6, 7, [tb]))
            self.P.fence()
            self.ffn(0, 1, 6, 2, ada_hook=hook, premod=True, post=lambda tb: self.modulate(1, 0, 1, [tb]))
            ada_next(72)
            self.ffn(1, 0, 0, 3, premod=True, post=lambda tb: self.modulate(1, 3, 4, [tb]))
            self.P.fence(('pe', 'act', 'dve', 'sp', 'pool'))
            self.mla(1, premod=True, post=lambda tb: self.modulate(1, 6, 7, [tb]))
            self.P.fence()
            self.ffn(1, 1, 6, 5, premod=True, post=lambda tb: self.store_x([tb]))
            self.emit()
            return self.nc
        self.store_x()
        self.emit()
        return self.nc

    def emit(self):
        nc = self.nc
        P = self.P
        names = sorted(P.cnt.keys())
        sems = {}
        import contextlib
        with contextlib.ExitStack() as st:
            for n in names:
                sems[n] = st.enter_context(nc.semaphore("s_" + n))
            block = st.enter_context(nc.Block())
            finals = [(n, v) for n, v in P.cnt.items() if n not in ENGS]

            def run(e, key, final=False):
                for waits, fn, incspec in P.ops[key]:
                    for s, v in waits:
                        e.wait_ge(sems[s], v)
                    if fn is None:
                        continue
                    ins = fn(e)
                    if incspec is not None:
                        ins.then_inc(sems[incspec[0]], incspec[1])
                if final:
                    for n, v in finals:
                        e.wait_ge(sems[n], v)

            @block.tensor
            def _(e):
                run(e, 'pe')

            @block.scalar
            def _(e):
                run(e, 'act')

            @block.vector
            def _(e):
                run(e, 'dve')

            @block.gpsimd
            def _(e):
                run(e, 'pool')

            @block.sync
            def _(e):
                run(e, 'sp', final=True)
        print("ops:", {k: len(v) for k, v in P.ops.items()}, "sems:", len(names))


def _pack_vecs(inp, cond2, flag):
    v = np.zeros((128, NV), np.float32)

    def put(col, vec):
        vec = np.asarray(vec, np.float32)
        n = vec.shape[0] // 128
        v[:, col:col + n] = vec.reshape(n, 128).T
    for r in range(2):
        put(CO_COND + r * 8, cond2[r])
    for l in range(2):
        put(CO_BADA + l * 72, inp['b_ada'][l])
        for s in range(3):
            put(CO_LNG + (l * 3 + s) * 8, inp['ln_g'][l, s])
            put(CO_LNB + (l * 3 + s) * 8, inp['ln_b'][l, s])
    cw = np.asarray(inp['conv_w'][0], np.float32)
    for i in range(4):
        v[:, CO_CW + i * 31: CO_CW + (i + 1) * 31] = cw[:, i * 128:(i + 1) * 128].T
    put(CO_CB, inp['conv_b'][0]); put(CO_CNG, inp['conv_norm_g'][0]); put(CO_CNB, inp['conv_norm_b'][0])
    put(CO_PSC, inp['pool_scale'][0])
    put(CO_QG, inp['mla_q_norm_g'][0]); put(CO_KVG, inp['mla_kv_norm_g'][0])
    v[:, CO_FLAG] = flag
    return v


def _rope_tables(real):
    C = np.ones((64, 1024), np.float32)
    S = np.zeros((64, 1024), np.float32)
    if real:
        n = 1024
        row = np.repeat(np.arange(n // 64), 64).astype(np.float32)
        col = np.tile(np.arange(64), n // 64).astype(np.float32)
        inv = (10000.0 ** (-np.arange(16, dtype=np.float32) / 16)).astype(np.float32)
        ang = np.concatenate([row[:, None] * inv, col[:, None] * inv], -1).astype(np.float32)
        cos, sin = np.cos(ang), np.sin(ang)
        for a in range(2):
            for j in range(2):
                for p in range(16):
                    dd = a * 32 + j * 16 + p
                    C[dd] = cos[:, a * 16 + p]
                    S[dd] = (-sin[:, a * 16 + p]) if j == 0 else sin[:, a * 16 + p]
    return C, S


_NC_CACHE = {}


def _prep_inputs(inp):
    inp = {k: np.asarray(v) for k, v in inp.items()}
    xp = inp['x_prompt'].astype(np.float32)
    xs = inp['x_sample'].astype(np.float32)
    ident = np.eye(128, dtype=np.float32)
    onehot = np.zeros((4, 1024), np.float32)
    for j in range(4):
        onehot[j, j * 256:(j + 1) * 256] = 1.0
    perm = np.arange(64).reshape(2, 2, 16)[:, ::-1, :].reshape(64)
    w_uq = inp['mla_w_uq'][0]
    uq_r = w_uq.reshape(384, 8, 192)[:, :, 128:]
    w_uq_sw = np.ascontiguousarray(uq_r[:, :, perm].reshape(384, 512))
    w_dkv_sw = np.ascontiguousarray(inp['mla_w_dkv'][0][:, 256:][:, perm])
    shared = {k: np.ascontiguousarray(inp[k], dtype=np.float32) for k in
              ('w_ada', 'ffn_w1', 'ffn_w3', 'ffn_w2', 'cp_w_in', 'pool_w', 'cp_w_out', 'mla_w_dq', 'mla_w_uq',
               'mla_w_dkv', 'mla_w_ukv', 'mla_w_o')}
    shared.update(ident=ident, onehot=onehot, w_uq_sw=w_uq_sw, w_dkv_sw=w_dkv_sw)
    in_maps = []
    for r in range(8):
        if r < 4:
            x = np.concatenate([xs[r], xp[2 * r], xp[2 * r + 1]], 0)
            cond2 = np.stack([inp['c_ctx'], inp['c'][r]], 0)
            flag = 1.0
            cckv = inp['cache_mla_ckv'][r, 0]
            ckr = inp['cache_mla_krope'][r, 0]
            maskk = np.zeros((4, 1536), np.float32)
            C, S = _rope_tables(True)
        else:
            p0 = 8 + 6 * (r - 4)
            x = xp[p0:p0 + 6].reshape(T, D)
            cond2 = np.stack([inp['c_ctx'], inp['c_ctx']], 0)
            flag = 0.0
            cckv = np.zeros((512, 256), np.float32)
            ckr = np.zeros((512, 64), np.float32)
            maskk = np.full((4, 1536), NEG, np.float32)
            for j in range(4):
                maskk[j, j * 256:(j + 1) * 256] = 0.0
            C, S = _rope_tables(False)
        m = dict(shared)
        m.update(x=np.ascontiguousarray(x), vecs=_pack_vecs(inp, cond2, flag),
                 cache_ckv=np.ascontiguousarray(cckv, dtype=np.float32),
                 cache_kr=np.ascontiguousarray(ckr, dtype=np.float32), ropeC=C, ropeS=S, maskk=maskk)
        in_maps.append(m)
    return in_maps


def _assemble(results):
    y_p = np.zeros((32, 256, D), np.float32)
    y_s = np.zeros((4, 1024, D), np.float32)
    ckv = np.zeros((32, 1, 256, 256), np.float32)
    kr = np.zeros((32, 1, 256, 64), np.float32)
    for r in range(8):
        y = results[r]['y']
        ok = results[r]['ockv']
        okr = results[r]['okr']
        if r < 4:
            y_s[r] = y[:1024]
            for i in range(2):
                sl = slice(1024 + 256 * i, 1024 + 256 * (i + 1))
                y_p[2 * r + i] = y[sl]; ckv[2 * r + i, 0] = ok[sl]; kr[2 * r + i, 0] = okr[sl]
        else:
            p0 = 8 + 6 * (r - 4)
            for i in range(6):
                sl = slice(256 * i, 256 * (i + 1))
                y_p[p0 + i] = y[sl]; ckv[p0 + i, 0] = ok[sl]; kr[p0 + i, 0] = okr[sl]
    return y_p, y_s, ckv, kr


def kernel(**inputs):
    in_maps = _prep_inputs(inputs)
    if 'nc' not in _NC_CACHE:
        _NC_CACHE['nc'] = Builder().build()
    res = run_bass_kernel_spmd(_NC_CACHE['nc'], in_maps, core_ids=list(range(8)))
    return _assemble(res.results)
```

```python
import numpy as np
import concourse.bass as bass
import concourse.mybir as mybir
from concourse.bass_utils import run_bass_kernel_spmd

F32 = mybir.dt.float32
BF16 = mybir.dt.bfloat16
AF = mybir.ActivationFunctionType
ALU = mybir.AluOpType

T = 1536
D = 1024
KC = 8
TB = 512
NTB = 3
DFF = 2816
NJ = 22
ALPHA = float((2 * 2) ** 0.25)
LN_EPS = 1e-5
RMS_EPS = 1e-6
NEG = -30000.0

CO_COND = 0
CO_BADA = CO_COND + 16
CO_LNG = CO_BADA + 144
CO_LNB = CO_LNG + 48
CO_CW = CO_LNB + 48
CO_CB = CO_CW + 124
CO_CNG = CO_CB + 4
CO_CNB = CO_CNG + 4
CO_PSC = CO_CNB + 4
CO_QG = CO_PSC + 4
CO_KVG = CO_QG + 3
CO_FLAG = CO_KVG + 2
NV = CO_FLAG + 1

ENGS = ('pe', 'act', 'dve', 'pool', 'sp')


class Prog:
    def __init__(self):
        self.ops = {e: [] for e in ENGS}
        self.cnt = {}
        self.seen = {e: {} for e in ENGS}
        self.res = {}

    def op(self, eng, fn, reads=(), writes=(), inc=True, dma=None):
        need = {}

        def add(tok):
            if tok is not None:
                s, v = tok
                if need.get(s, 0) < v:
                    need[s] = v
        for r in reads:
            e = self.res.get(r)
            if e is not None:
                add(e[0])
        for w in writes:
            e = self.res.get(w)
            if e is not None:
                add(e[0])
                for s, v in e[1].items():
                    add((s, v))
        if dma is not None and self.cnt.get(dma, 0) > 0:
            add((dma, self.cnt[dma]))
        waits = []
        for s, v in need.items():
            if eng == 'pe' and s == 'pe':
                continue
            if self.seen[eng].get(s, 0) >= v:
                continue
            self.seen[eng][s] = v
            waits.append((s, v))
        if dma is not None:
            self.cnt[dma] = self.cnt.get(dma, 0) + 16
            tok = (dma, self.cnt[dma])
            incspec = (dma, 16)
        else:
            before = self.cnt.get(eng, 0)
            tok = (eng, before + 1)
            if inc:
                self.cnt[eng] = before + 1
                incspec = (eng, 1)
            else:
                incspec = None
        for r in reads:
            e = self.res.setdefault(r, [None, {}])
            if e[1].get(tok[0], 0) < tok[1]:
                e[1][tok[0]] = tok[1]
        for w in writes:
            self.res[w] = [tok, {}]
        self.ops[eng].append((waits, fn, incspec))


    def fence(self, engines=('pe', 'act', 'dve', 'sp')):
        for eng in engines:
            waits = []
            for s, v in self.cnt.items():
                if v == 0 or (eng == 'pe' and s == 'pe'):
                    continue
                if self.seen[eng].get(s, 0) >= v:
                    continue
                self.seen[eng][s] = v
                waits.append((s, v))
            if waits:
                self.ops[eng].append((waits, None, None))


class Builder:
    def __init__(self, debug=None):
        self.debug = debug
        nc = self.nc = bass.Bass("TRN2", target_bir_lowering=False)
        self.P = Prog()
        self.dr = {}

        def din(name, shape):
            self.dr[name] = nc.dram_tensor(name, list(shape), F32, kind="ExternalInput").ap()

        def dout(name, shape):
            self.dr[name] = nc.dram_tensor(name, list(shape), F32, kind="ExternalOutput").ap()
        din('x', [T, D]); din('vecs', [128, NV]); din('ident', [128, 128])
        din('cache_ckv', [512, 256]); din('cache_kr', [512, 64])
        din('ropeC', [64, 1024]); din('ropeS', [64, 1024])
        din('maskk', [4, 1536]); din('onehot', [4, 1024])
        din('w_ada', [2, 1024, 9216])
        din('ffn_w1', [2, 2, 1024, DFF]); din('ffn_w3', [2, 2, 1024, DFF]); din('ffn_w2', [2, 2, DFF, 1024])
        din('cp_w_in', [1, 1024, 1536]); din('pool_w', [1, 4, 128, 128]); din('cp_w_out', [1, 1024, 1024])
        din('mla_w_dq', [1, 1024, 384]); din('mla_w_uq', [1, 384, 1536]); din('w_uq_sw', [384, 512])
        din('mla_w_dkv', [1, 1024, 320]); din('w_dkv_sw', [1024, 64])
        din('mla_w_ukv', [1, 256, 2048]); din('mla_w_o', [1, 1024, 1024])
        dout('y', [T, D]); dout('ockv', [T, 256]); dout('okr', [T, 64])

        self.big = nc.alloc_sbuf_tensor("big", [128, 103 * 1024], BF16)
        self.off = 0
        self.X = self.alloc((KC, T), F32)
        self.H = self.alloc((KC, T), BF16)
        self.g_off = self.off
        self.G = self.alloc((NJ, T), BF16)
        self.g_end = self.off
        self.WF = [self.alloc((2048,), BF16) for _ in range(8)]
        self.WA = [self.alloc((8, 256), BF16) for _ in range(2)]
        self.VEC = self.alloc((NV,), F32)
        self.ADAT = self.alloc((2, 72, 2), F32)
        self.IDF = self.alloc((128,), F32)
        self.IDB = self.alloc((128,), BF16)
        self.ONES = self.alloc((128,), BF16)
        self.SC = self.alloc((8, 2), BF16)
        self.SM = self.alloc((16,), F32)
        self.DRF = self.alloc((128,), F32)
        self.ONESF = self.alloc((128,), F32)
        self.rb_bank = 7
        self.EPS_LN = self.alloc((1,), F32)
        self.EPS_RMS = self.alloc((1,), F32)
        self.ST = [self.alloc((TB,), F32) for _ in range(4)]
        self.TMP = [self.alloc((TB,), F32) for _ in range(4)]
        self.XB = [self.alloc((TB,), BF16) for _ in range(4)]
        print("sbuf used bytes/partition:", self.off)
        assert self.off <= 206 * 1024
        self.ps = [nc.alloc_psum_tensor("ps%d" % b, [128, 512], F32)[:, :] for b in range(8)]
        self.stage_off = self.g_end - 8192
        self.bank_rr = 0
        self.nbank_rr = 7
        self.live_banks = set()
        self.wf_rr = 0
        self.wa_rr = 0
        self.tmp_rr = 0
        self.xb_rr = 0
        self.out_rr = 0

    def view(self, off, shape, dtype):
        n = int(np.prod(shape))
        esz = 4 if dtype == F32 else 2
        assert off % 4 == 0
        ap = self.big[:, off // 2: (off + n * esz) // 2]
        if dtype == F32:
            ap = ap.bitcast(F32)
        if len(shape) == 2:
            ap = ap.rearrange("p (a b) -> p a b", a=shape[0])
        elif len(shape) == 3:
            ap = ap.rearrange("p (a b c) -> p a b c", a=shape[0], b=shape[1])
        return ap

    def alloc(self, shape, dtype):
        n = int(np.prod(shape))
        esz = 4 if dtype == F32 else 2
        off = (self.off + 31) // 32 * 32
        ap = self.view(off, shape, dtype)
        self.off = off + n * esz
        return ap

    def bank(self):
        while True:
            b = self.bank_rr
            self.bank_rr = (b + 1) % self.nbank_rr
            if b not in self.live_banks:
                return b

    def tmp(self):
        i = self.tmp_rr
        self.tmp_rr = (i + 1) % 4
        return i

    def act(self, out, in_, func, reads, writes, **kw):
        self.P.op('act', lambda e: e.activation(out=out, in_=in_, func=func, **kw), reads, writes)

    def dve(self, fn, reads, writes):
        self.P.op('dve', fn, reads, writes)

    def mm(self, out, lhsT, rhs, start, stop, reads, writes, inc=None):
        self.P.op('pe', lambda e: e.matmul(out, lhsT=lhsT, rhs=rhs, start=start, stop=stop),
                  reads, writes, inc=(stop if inc is None else inc))

    def tr(self, out, in_, ident, reads, writes, inc=True):
        self.P.op('pe', lambda e: e.transpose(out=out, in_=in_, identity=ident), reads, writes, inc=inc)

    def wload(self, src_ap, rows=8, cols=256, slot=None):
        if slot is None:
            i = self.wf_rr
            self.wf_rr = (i + 1) % 8
        else:
            i = slot
        dst = self.WF[i][:, 0:rows * cols].rearrange("p (a b) -> p a b", a=rows)
        self.P.op('pool', lambda e: e.dma_start(out=dst, in_=src_ap), reads=(), writes=[('wf', i)], dma='wf%d' % i)
        return (i, dst)

    def vcol(self, col, n=1):
        return self.VEC[:, col:col + n]

    def constants(self):
        P = self.P
        P.op('sp', lambda e: e.dma_start(out=self.VEC, in_=self.dr['vecs']), writes=['vec'], dma='c0')
        P.op('sp', lambda e: e.dma_start(out=self.IDF, in_=self.dr['ident']), writes=['idf'], dma='c1')
        self.dve(lambda e: e.tensor_copy(out=self.IDB, in_=self.IDF), ['idf'], ['idb'])
        self.dve(lambda e: e.memset(self.ONES, 1.0), [], ['ones'])
        self.dve(lambda e: e.memset(self.ONESF, 1.0), [], ['onesf'])
        self.dve(lambda e: e.memset(self.EPS_LN, LN_EPS), [], ['eps'])
        self.dve(lambda e: e.memset(self.EPS_RMS, RMS_EPS), [], ['eps2'])
        cond = self.VEC[:, CO_COND:CO_COND + 16].rearrange("p (r c) -> p c r", r=2)
        self.act(self.SC, cond, AF.Silu, ['vec'], ['sc'])

    def load_x(self, hook=None):
        xin = [self.view(self.stage_off - 8192 + i * 4096, (D,), F32) for i in range(4)]
        for tt in range(T // 128):
            if hook is not None and tt in (1, 4, 7, 10):
                hook()
            s = tt % 4
            src = self.dr['x'][tt * 128:(tt + 1) * 128, :]
            dst = xin[s]
            self.P.op('sp', lambda e, dst=dst, src=src: e.dma_start(out=dst, in_=src), writes=[('xin', s)], dma='xin%d' % s)
            for half in range(2):
                b = self.bank()
                for q in range(4):
                    c = half * 4 + q
                    self.tr(self.ps[b][:, q * 128:(q + 1) * 128], xin[s][:, c * 128:(c + 1) * 128], self.IDF,
                            [('xin', s), 'idf'], [('ps', b)], inc=(q == 3))
                src_ps = self.ps[b][:, :].rearrange("p (q t) -> p q t", q=4)
                out = self.X[:, half * 4:half * 4 + 4, tt * 128:(tt + 1) * 128]
                wr = [('X', half * 4 + q, tt // 4) for q in range(4)]
                if half == 0:
                    self.act(out, src_ps, AF.Copy, [('ps', b)], [('ps', b)] + wr)
                else:
                    self.dve(lambda e, out=out, src_ps=src_ps: e.tensor_copy(out=out, in_=src_ps), [], [('ps', b)] + wr)

    def ada_units(self, units):
        b = 7
        for (l, u) in units:
            i = self.wa_rr
            self.wa_rr = (i + 1) % 2
            src = self.dr['w_ada'][l, :, u * 256:(u + 1) * 256].rearrange("(k p) n -> p k n", p=128)
            dst = self.WA[i]
            self.P.op('pool', lambda e, dst=dst, src=src: e.dma_start(out=dst, in_=src), writes=[('wa', i)], dma='wa%d' % i)
            for fl in range(2):
                fc = u * 2 + fl
                for k in range(KC):
                    self.mm(self.ps[b][:, fc * 2:fc * 2 + 2], self.WA[i][:, k, fl * 128:(fl + 1) * 128], self.SC[:, k, :],
                            k == 0, k == KC - 1, [('wa', i), 'sc'], [('ps', b)])
        for (l, u) in units:
            m = (u * 2) // 8
            for r in range(2):
                out = self.ADAT[:, l, u * 2:u * 2 + 2, r]
                in0 = self.ps[b][:, u * 4:u * 4 + 4].rearrange("p (f r) -> p f r", r=2)[:, :, r]
                in1 = self.VEC[:, CO_BADA + l * 72 + u * 2: CO_BADA + l * 72 + u * 2 + 2]
                self.dve(lambda e, out=out, in0=in0, in1=in1: e.tensor_tensor(out=out, in0=in0, in1=in1, op=ALU.add),
                         ['vec'], [('ps', b), ('ada', l, u)])
            out = self.ADAT[:, l, u * 2:u * 2 + 2, :]
            if m in (1, 4, 7):
                self.dve(lambda e, out=out: e.tensor_scalar_add(out=out, in0=out, scalar1=1.0), [], [('ada', l, u)])
            elif m in (2, 8):
                self.dve(lambda e, out=out: e.tensor_scalar_mul(out=out, in0=out, scalar1=0.5), [], [('ada', l, u)])

    def ada_res(self, l, m):
        return [('ada', l, m * 4 + j) for j in range(4)]

    def mod(self, l, m, c, r):
        return self.ADAT[:, l, m * 8 + c, r:r + 1]

    @staticmethod
    def cond_of(tb):
        return 1 if tb < 2 else 0

    def modulate(self, l, m_shift, m_scale, tbs=None):
        rd_ada = self.ada_res(l, m_shift) + self.ada_res(l, m_scale)
        for tb in (range(NTB) if tbs is None else tbs):
            r = self.cond_of(tb)
            for c in range(KC):
                out = self.H[:, c, tb * TB:(tb + 1) * TB]
                in_ = self.X[:, c, tb * TB:(tb + 1) * TB]
                sc = self.mod(l, m_scale, c, r)
                sh = self.mod(l, m_shift, c, r)
                if c % 2 == 0:
                    self.act(out, in_, AF.Identity, [('X', c, tb)] + rd_ada, [('H', c, tb)], scale=sc, bias=sh)
                else:
                    self.dve(lambda e, out=out, in_=in_, sc=sc, sh=sh: e.tensor_scalar(
                        out=out, in0=in_, scalar1=sc, scalar2=sh, op0=ALU.mult, op1=ALU.add),
                        [('X', c, tb)] + rd_ada, [('H', c, tb)])

    def _ln_args(self, tb, src, skey, dst, dkey, func):
        sl = slice(tb * TB, (tb + 1) * TB)
        if src is None:
            src = lambda c: self.X[:, c, sl]
            skey = lambda c: ('X', c, tb)
        if dst is None:
            dst, dkey = src, skey
        if func is None:
            func = AF.Identity
        return src, skey, dst, dkey, func

    def ln_stats(self, tb, nch=KC, src=None, skey=None):
        src, skey, _, _, _ = self._ln_args(tb, src, skey, None, None, None)
        bs, bq = self.bank(), self.bank()
        for c in range(nch):
            i = (self.xb_rr // 2 * 2) % 4
            self.xb_rr = (i + 2) % 4
            xq = self.XB[i + 1]
            self.act(xq, src(c), AF.Square, [skey(c)], [('xb', i + 1)])
            self.mm(self.ps[bs], self.ONESF, src(c), c == 0, c == nch - 1, ['onesf', skey(c)], [('ps', bs)], inc=True)
            self.mm(self.ps[bq], self.ONES, xq, c == 0, c == nch - 1, ['ones', ('xb', i + 1)], [('ps', bq)], inc=True)
        return bs, bq

    def ln_apply(self, tb, banks, g_col, b_col, nch=KC, src=None, skey=None, dst=None, dkey=None, func=None):
        src, skey, dst, dkey, func = self._ln_args(tb, src, skey, dst, dkey, func)
        bs, bq = banks
        mean, msq, var, rstd = self.ST
        n = float(nch * 128)
        self.dve(lambda e: e.tensor_scalar_mul(out=mean, in0=self.ps[bs], scalar1=1.0 / n), [], [('ps', bs), ('st', 0)])
        self.dve(lambda e: e.tensor_tensor(out=msq, in0=mean, in1=mean, op=ALU.mult), [('st', 0)], [('st', 1)])
        self.dve(lambda e: e.scalar_tensor_tensor(out=var, in0=self.ps[bq], scalar=1.0 / n, in1=msq,
                                                 op0=ALU.mult, op1=ALU.subtract), [('st', 1)], [('ps', bq), ('st', 2)])
        self.act(var, var, AF.Ln, ['eps'], [('st', 2)], bias=self.EPS_LN, scale=1.0)
        self.act(rstd, var, AF.Exp, [('st', 2)], [('st', 3)], scale=-0.5)
        for c in range(nch):
            xs = src(c)
            t = self.tmp()
            tt_ = self.TMP[t]
            self.dve(lambda e, xs=xs, tt_=tt_: e.tensor_tensor(out=tt_, in0=xs, in1=mean, op=ALU.subtract),
                     [skey(c), ('st', 0)], [('tmp', t)])
            self.dve(lambda e, tt_=tt_: e.tensor_tensor(out=tt_, in0=tt_, in1=rstd, op=ALU.mult),
                     [('st', 3)], [('tmp', t)])
            self.act(dst(c), tt_, func, [('tmp', t), 'vec'], [dkey(c)],
                     scale=self.vcol(g_col + c), bias=self.vcol(b_col + c))

    def layernorm_all(self, g_col, b_col, post=None, nch=KC, mk=None):
        kw = [(mk(tb) if mk is not None else {}) for tb in range(NTB)]
        banks = {}

        def stats(tb):
            banks[tb] = self.ln_stats(tb, nch=nch, src=kw[tb].get('src'), skey=kw[tb].get('skey'))
            self.live_banks.update(banks[tb])

        def apply(tb):
            self.ln_apply(tb, banks[tb], g_col, b_col, nch=nch, **kw[tb])
            self.live_banks.difference_update(banks[tb])
            if post is not None:
                post(tb)
        stats(0); stats(1); apply(0); stats(2); apply(1); apply(2)

    def ffn(self, l, f, m0, ln_idx, ada_hook=None, parts=3, premod=False, post=None):
        if not premod:
            self.modulate(l, m0, m0 + 1)
        w1 = self.dr['ffn_w1'][l, f]
        w3 = self.dr['ffn_w3'][l, f]
        w2 = self.dr['ffn_w2'][l, f]
        loads = []
        for ng in range(NJ // 2):
            loads.append((w1[:, ng * 256:(ng + 1) * 256].rearrange("(k p) n -> p k n", p=128), 8, 256))
            loads.append((w3[:, ng * 256:(ng + 1) * 256].rearrange("(k p) n -> p k n", p=128), 8, 256))
        for c in range(KC):
            loads.append((w2[0:11 * 128, c * 128:(c + 1) * 128].rearrange("(j p) n -> p j n", p=128), 11, 128))
            loads.append((w2[11 * 128:22 * 128, c * 128:(c + 1) * 128].rearrange("(j p) n -> p j n", p=128), 11, 128))
        slots = []

        def issue(upto):
            while len(slots) <= min(upto, len(loads) - 1):
                a, rws, cls = loads[len(slots)]
                slots.append(self.wload(a, rows=rws, cols=cls))
        for ng in range(NJ // 2):
            issue(min(2 * (ng + 2) + 1, 21 if parts < 2 else 99))
            s1, s3 = slots[2 * ng], slots[2 * ng + 1]
            for jl in range(2):
                j = ng * 2 + jl
                for tb in range(NTB):
                    sl = slice(tb * TB, (tb + 1) * TB)
                    ba, bb = self.bank(), self.bank()
                    for (bk, sw) in ((ba, s1), (bb, s3)):
                        for k in range(KC):
                            self.mm(self.ps[bk], sw[1][:, k, jl * 128:(jl + 1) * 128], self.H[:, k, sl],
                                    k == 0, k == KC - 1, [('wf', sw[0]), ('H', k, tb)], [('ps', bk)])
                    t = self.tmp()
                    s_ = self.TMP[t]
                    self.act(s_, self.ps[ba], AF.Silu, [], [('ps', ba), ('tmp', t)])
                    out = self.G[:, j, sl]
                    self.dve(lambda e, out=out, s_=s_, bb=bb: e.tensor_tensor(out=out, in0=self.ps[bb], in1=s_, op=ALU.mult),
                             [('tmp', t)], [('ps', bb), ('G', j, tb)])
            if ada_hook is not None:
                ada_hook()
        if parts < 2:
            return
        rd_gate = self.ada_res(l, m0 + 2)
        for c in range(KC):
            issue(22 + 2 * (c + 2) + 1)
            sa, sb = slots[22 + 2 * c], slots[22 + 2 * c + 1]
            for tb in range(NTB):
                sl = slice(tb * TB, (tb + 1) * TB)
                r = self.cond_of(tb)
                b = self.bank()
                for j in range(NJ):
                    sw = sa if j < 11 else sb
                    self.mm(self.ps[b], sw[1][:, j % 11, :], self.G[:, j, sl],
                            j == 0, j == NJ - 1, [('wf', sw[0]), ('G', j, tb)], [('ps', b)])
                t = self.tmp()
                y_ = self.TMP[t]
                self.act(y_, self.ps[b], AF.Identity, rd_gate, [('ps', b), ('tmp', t)], scale=self.mod(l, m0 + 2, c, r))
                xs = self.X[:, c, sl]
                self.dve(lambda e, xs=xs, y_=y_: e.scalar_tensor_tensor(out=xs, in0=xs, scalar=ALPHA, in1=y_,
                                                                       op0=ALU.mult, op1=ALU.add),
                         [('tmp', t)], [('X', c, tb)])
            if ada_hook is not None:
                ada_hook()
        if parts < 3:
            return
        self.layernorm_all(CO_LNG + ln_idx * 8, CO_LNB + ln_idx * 8, post=post)

    def resid_ln(self, l, m_gate, ps_b, c, tb, rd_gate):
        sl = slice(tb * TB, (tb + 1) * TB)
        r = self.cond_of(tb)
        t = self.tmp()
        y_ = self.TMP[t]
        self.act(y_, self.ps[ps_b], AF.Identity, rd_gate, [('ps', ps_b), ('tmp', t)], scale=self.mod(l, m_gate, c, r))
        xs = self.X[:, c, sl]
        self.dve(lambda e, xs=xs, y_=y_: e.scalar_tensor_tensor(out=xs, in0=xs, scalar=ALPHA, in1=y_,
                                                               op0=ALU.mult, op1=ALU.add),
                 [('tmp', t)], [('X', c, tb)])

    def mixer0(self, l=0, premod=False, post=None):
        if not premod:
            self.modulate(l, 3, 4)
        go = self.g_off
        CB = self.view(go, (4, 6, 286), BF16)
        DT = self.view(go + 13824, (4, T), BF16)
        CO = self.view(go + 26112, (4, T), F32)
        DG = [self.view(go + 50688 + i * 7936, (31, 128), BF16) for i in range(2)]
        PA = self.view(go + 26112, (2, 6, 271), F32)
        PBf = self.view(go + 26112 + 13024, (2, 6, 271), F32)
        HPs = [self.view(go + 26112 + 26048 + i * 6144, (6, 256), F32) for i in range(2)]
        AB = self.H
        flag = self.vcol(CO_FLAG)
        w_in = self.dr['cp_w_in'][0]

        def win(c0):
            return self.wload(w_in[:, c0:c0 + 256].rearrange("(k p) n -> p k n", p=128))
        self.dve(lambda e: e.memset(CB, 0.0), [], ['CB'])
        for pair in range(2):
            su = win(pair * 256)
            sg = win(512 + pair * 256)
            for il in range(2):
                i = pair * 2 + il
                for tb in range(NTB):
                    sl = slice(tb * TB, (tb + 1) * TB)
                    bu, bg = self.bank(), self.bank()
                    for (bk, sw) in ((bu, su), (bg, sg)):
                        for k in range(KC):
                            self.mm(self.ps[bk], sw[1][:, k, il * 128:(il + 1) * 128], self.H[:, k, sl],
                                    k == 0, k == KC - 1, [('wf', sw[0]), ('H', k, tb)], [('ps', bk)])
                    t = self.tmp()
                    s_ = self.TMP[t]
                    self.act(s_, self.ps[bg], AF.Sigmoid, [], [('ps', bg), ('tmp', t)])
                    out = CB[:, i, 2 * tb:2 * tb + 2, 15:271]
                    in0 = self.ps[bu].rearrange("p (s u) -> p s u", s=2)
                    in1 = s_.rearrange("p (s u) -> p s u", s=2)
                    self.dve(lambda e, out=out, in0=in0, in1=in1: e.tensor_tensor(out=out, in0=in0, in1=in1, op=ALU.mult),
                             [('tmp', t), 'CB'], [('ps', bu), ('CBc', i, tb)])
        for i in range(4):
            cbk = [('CBc', i, tb) for tb in range(NTB)]
            o1, i1 = CB[:, i, 1:4, 0:15], CB[:, i, 0:3, 256:271]
            o2, i2 = CB[:, i, 0:3, 271:286], CB[:, i, 1:4, 15:30]
            self.dve(lambda e, o1=o1, i1=i1: e.tensor_scalar_mul(out=o1, in0=i1, scalar1=flag), cbk + ['vec'], [('CBh', i, 0)])
            self.dve(lambda e, o2=o2, i2=i2: e.tensor_scalar_mul(out=o2, in0=i2, scalar1=flag), cbk + ['vec'], [('CBh', i, 1)])
        spw = self.wload(self.dr['pool_w'][0].rearrange("g p n -> p g n"), rows=4, cols=128)
        PDs = [[PA[:, 0], PBf[:, 0]], [PA[:, 1], PBf[:, 1]]]
        ICE = self.ST[0][:, 0:384].rearrange("p (g s e) -> p g s e", g=4, s=6)
        STEPS = [(1, 0, 1, 271, 0, 1), (0, 1, 2, 270, 1, 3), (1, 0, 4, 268, 2, 6), (0, 1, 8, 264, 4, 12)]

        def halo_zero(buf, key):
            self.dve(lambda e: e.memset(buf[:, :, 0:8], 0.0), [], [key])
            self.dve(lambda e: e.memset(buf[:, :, 264:271], 0.0), [], [key])

        def halo_flag(buf, rd, key):
            self.dve(lambda e: e.tensor_scalar_mul(out=buf[:, 1:4, 0:8], in0=buf[:, 0:3, 256:264], scalar1=flag), rd + ['vec'], [key])
            self.dve(lambda e: e.tensor_scalar_mul(out=buf[:, 0:3, 264:271], in0=buf[:, 1:4, 8:15], scalar1=flag), rd + ['vec'], [key])

        def run_steps(pr, nsteps, first_rd, on_step=None):
            PD = PDs[pr]
            rd = first_rd
            for si in range(nsteps):
                di, si_, lo, hi, a0_, a1_ = STEPS[si]
                n = hi - lo
                o, x0, x1 = PD[di][:, :, lo:hi], PD[si_][:, :, a0_:a0_ + n], PD[si_][:, :, a1_:a1_ + n]
                key = ('PS', pr, di)
                self.dve(lambda e, o=o, x0=x0, x1=x1: e.tensor_tensor(out=o, in0=x0, in1=x1, op=ALU.add), rd, [key])
                rd = [key]
                if on_step is not None:
                    on_step(si, PD[di], key)
            return rd
        PD = PDs[1]
        self.dve(lambda e: e.memset(PD[0], 0.0), [], [('PS', 1, 0)])
        self.dve(lambda e: e.memset(PD[0][:, :, 8:264], 1.0), [], [('PS', 1, 0)])
        halo_flag(PD[0], [], ('PS', 1, 0))

        def grab(si, buf, key):
            self.dve(lambda e: e.reciprocal(out=ICE[:, si, :, 0:8], in_=buf[:, :, 8:16]), [key], [('ICE', si)])
            self.dve(lambda e: e.reciprocal(out=ICE[:, si, :, 8:16], in_=buf[:, :, 256:264]), [key], [('ICE', si)])
        run_steps(1, 4, [('PS', 1, 0)], on_step=grab)
        shs = {}

        def pool_A(gi):
                pair, il = divmod(gi, 2)
                if pair not in shs:
                    shs[pair] = win(1024 + pair * 256)
                sh = shs[pair]
                pr = gi % 2
                PD = PDs[pr]
                HP = HPs[pr]
                halo_zero(PD[0], ('PS', pr, 0))
                for tb in range(NTB):
                    sl = slice(tb * TB, (tb + 1) * TB)
                    b = self.bank()
                    for k in range(KC):
                        self.mm(self.ps[b], sh[1][:, k, il * 128:(il + 1) * 128], self.H[:, k, sl],
                                k == 0, k == KC - 1, [('wf', sh[0]), ('H', k, tb)], [('ps', b)])
                    src = self.ps[b].rearrange("p (s u) -> p s u", s=2)
                    self.act(PD[0][:, 2 * tb:2 * tb + 2, 8:264], src, AF.Copy, [], [('ps', b), ('PS', pr, 0)])
                    self.act(HP[:, 2 * tb:2 * tb + 2, :], src, AF.Copy, [], [('ps', b), ('HP', pr, tb)])
                halo_flag(PD[0], [], ('PS', pr, 0))
                rd = run_steps(pr, gi + 1, [('PS', pr, 0)])
                fin = PD[STEPS[gi][0]]
                fkey = ('PS', pr, STEPS[gi][0])
                w_ = float(2 ** (gi + 1))
                sd = fin[:, :, 8:264]
                dt = DT[:, gi, :].rearrange("p (s u) -> p s u", s=6)
                hpk = [('HP', pr, tb) for tb in range(NTB)]
                self.dve(lambda e, dt=dt, sd=sd, w_=w_, HP=HP: e.scalar_tensor_tensor(out=dt, in0=sd, scalar=1.0 / w_, in1=HP,
                                                                            op0=ALU.mult, op1=ALU.subtract),
                         [fkey] + hpk, [('DT', gi)])
                for (c0, e0) in ((0, 0), (248, 8)):
                    se = fin[:, :, 8 + c0:16 + c0]
                    ie = ICE[:, gi, :, e0:e0 + 8]
                    de = dt[:, :, c0:c0 + 8]
                    he = HP[:, :, c0:c0 + 8]
                    self.dve(lambda e, se=se, ie=ie: e.tensor_tensor(out=se, in0=se, in1=ie, op=ALU.mult), [('ICE', gi)], [fkey])
                    self.dve(lambda e, de=de, se=se, he=he: e.tensor_tensor(out=de, in0=se, in1=he, op=ALU.subtract), hpk, [fkey, ('DT', gi)])

        def pool_B(gi):
                for tb in range(NTB):
                    sl = slice(tb * TB, (tb + 1) * TB)
                    b = self.bank()
                    self.mm(self.ps[b], spw[1][:, gi, :], DT[:, gi, sl], True, True, [('wf', spw[0]), ('DT', gi)], [('ps', b)])
                    self.act(DT[:, gi, sl], self.ps[b], AF.Identity, ['vec'], [('ps', b), ('DTB', gi, tb)],
                             scale=self.vcol(CO_PSC + gi))
        pool_A(0); pool_A(1); pool_B(0); pool_A(2); pool_B(1); pool_A(3); pool_B(2); pool_B(3)
        for i in range(4):
            dg = DG[i % 2]
            for k in range(31):
                o = dg[:, k, :]
                wc = self.vcol(CO_CW + i * 31 + k)
                self.dve(lambda e, o=o, wc=wc: e.tensor_scalar_mul(out=o, in0=self.IDB, scalar1=wc),
                         ['idb', 'vec'], [('DG', i % 2)])
            for tb in range(NTB):
                sl = slice(tb * TB, (tb + 1) * TB)
                b = self.bank()
                for k in range(31):
                    self.mm(self.ps[b], dg[:, k, :], CB[:, i, 2 * tb:2 * tb + 2, k:k + 256], k == 0, k == 30,
                            [('DG', i % 2), ('CBh', i, 0), ('CBh', i, 1)] + [('CBc', i, t2) for t2 in range(NTB)], [('ps', b)])
                self.act(CO[:, i, sl], self.ps[b], AF.Identity, ['vec'], [('ps', b), ('CO', i, tb)],
                         bias=self.vcol(CO_CB + i))
        def conv_ln_args(tb):
            sl = slice(tb * TB, (tb + 1) * TB)
            return dict(src=lambda c, sl=sl: CO[:, c, sl], skey=lambda c, tb=tb: ('CO', c, tb),
                        dst=lambda c, sl=sl: AB[:, c, sl], dkey=lambda c, tb=tb: ('H', c, tb), func=AF.Silu)
        self.layernorm_all(CO_CNG, CO_CNB, nch=4, mk=conv_ln_args)
        w_out = self.dr['cp_w_out'][0]
        rd_gate = self.ada_res(l, 5)
        for pair in range(4):
            so = self.wload(w_out[:, pair * 256:(pair + 1) * 256].rearrange("(k p) n -> p k n", p=128))
            for il in range(2):
                c = pair * 2 + il
                for tb in range(NTB):
                    sl = slice(tb * TB, (tb + 1) * TB)
                    b = self.bank()
                    for j in range(KC):
                        rhs = AB[:, j, sl] if j < 4 else DT[:, j - 4, sl]
                        rk = ('H', j, tb) if j < 4 else ('DTB', j - 4, tb)
                        self.mm(self.ps[b], so[1][:, j, il * 128:(il + 1) * 128], rhs, j == 0, j == KC - 1,
                                [('wf', so[0]), rk], [('ps', b)])
                    self.resid_ln(l, 5, b, c, tb, rd_gate)
        self.layernorm_all(CO_LNG + (l * 3 + 1) * 8, CO_LNB + (l * 3 + 1) * 8, post=post)

    def rms_block(self, tb, nch, wslices, g_col, dst_f32=None, dst_bf=None, key=None, alt=0):
        sl = slice(tb * TB, (tb + 1) * TB)
        QD = self.mla_bufs['QD'][alt]
        qk = 'QD%d' % alt
        bq = self.bank()
        for c in range(nch):
            b = self.bank()
            slot, lf = wslices[c]
            for k in range(KC):
                self.mm(self.ps[b], lf(k), self.H[:, k, sl], k == 0, k == KC - 1, [('wf', slot), ('H', k, tb)], [('ps', b)])
            self.act(QD[:, c, :], self.ps[b], AF.Copy, [], [('ps', b), (qk, c)])
            i = self.xb_rr
            self.xb_rr = (i + 1) % 4
            self.act(self.XB[i], QD[:, c, :], AF.Square, [(qk, c)], [('xb', i)])
            self.mm(self.ps[bq], self.ONES, self.XB[i], c == 0, c == nch - 1, ['ones', ('xb', i)], [('ps', bq)])
        var, rstd = (self.ST[2], self.ST[3]) if alt == 0 else (self.ST[0], self.ST[1])
        k2, k3 = (('st', 2), ('st', 3)) if alt == 0 else (('st', 0), ('st', 1))
        self.act(var, self.ps[bq], AF.Sqrt, ['eps2'], [('ps', bq), k2], bias=self.EPS_RMS, scale=1.0 / (nch * 128))
        self.dve(lambda e: e.reciprocal(out=rstd, in_=var), [k2], [k3])
        for c in range(nch):
            qd = QD[:, c, :]
            g = self.vcol(g_col + c)
            if dst_f32 is not None:
                o = dst_f32[:, c, sl]
                self.dve(lambda e, o=o, qd=qd, g=g: e.scalar_tensor_tensor(out=o, in0=qd, scalar=g, in1=rstd,
                                                                          op0=ALU.mult, op1=ALU.mult),
                         [(qk, c), k3, 'vec'], [(key + 'f', c, tb)])
                ob = dst_bf[:, c, sl]
                self.act(ob, o, AF.Copy, [(key + 'f', c, tb)], [(key, c, tb)])
            else:
                ob = dst_bf[:, c, sl]
                self.dve(lambda e, ob=ob, qd=qd, g=g: e.scalar_tensor_tensor(out=ob, in0=qd, scalar=g, in1=rstd,
                                                                            op0=ALU.mult, op1=ALU.mult),
                         [(qk, c), k3, 'vec'], [(key, c, tb)])

    def mla(self, l=1, premod=False, post=None):
        if not premod:
            self.modulate(l, 3, 4)
        self.wf_rr = 0
        self.nbank_rr = 8
        go = self.g_off
        QN = self.view(go, (3, T), BF16)
        CKVb = self.view(go + 9216, (2, 2048), BF16)
        KRb = self.view(go + 17408, (2048,), BF16)
        KT = [self.view(go + 21504 + i * 4096, (2048,), BF16) for i in range(2)]
        VH = [self.view(go + 29696 + i * 4096, (16, 128), BF16) for i in range(2)]
        QT = self.view(go + 37888, (T,), BF16)
        QR = [self.view(go + 40960 + i * 3072, (T,), BF16) for i in range(2)]
        PP = [self.view(go + 47104 + i * 3072, (T,), BF16) for i in range(2)]
        PT = self.view(go + 53248, (12, 512), BF16)
        RB = self.view(go + 65536, (512,), F32)
        CKVf = self.view(go + 21504, (2, T), F32)
        KRraw = self.view(go + 33792, (T,), F32)
        QDall = self.view(go + 39936, (3, 3, TB), F32)
        OST = [self.view(go + 58368 + i * 1280, (320,), F32) for i in range(2)]
        CST = [self.view(go + 60928 + i * 1280, (320,), F32) for i in range(4)]
        ROC = self.WA[0].rearrange("p a b -> p (a b)").bitcast(F32)
        ROS = self.WA[1].rearrange("p a b -> p (a b)").bitcast(F32)
        OT = self.H
        P = self.P
        P.op('sp', lambda e: e.dma_start(out=ROC[0:64, :], in_=self.dr['ropeC']), writes=[('wa', 0)], dma='c0')
        P.op('sp', lambda e: e.dma_start(out=ROS[0:64, :], in_=self.dr['ropeS']), writes=[('wa', 1)], dma='c1')
        P.op('pool', lambda e: e.dma_start(out=KRb[64:68, 0:1024], in_=self.dr['maskk'][:, 0:1024]), writes=['KRm0'], dma='m0')
        P.op('pool', lambda e: e.dma_start(out=KRb[64:68, 1536:2048], in_=self.dr['maskk'][:, 1024:1536]), writes=['KRm1'], dma='m1')
        for ct in range(4):
            P.op('sp', lambda e, ct=ct: e.dma_start(out=CST[ct][:, 0:256], in_=self.dr['cache_ckv'][ct * 128:(ct + 1) * 128, :]),
                 writes=[('cst', ct, 0)], dma='cs%d' % ct)
            P.op('sp', lambda e, ct=ct: e.dma_start(out=CST[ct][:, 256:320], in_=self.dr['cache_kr'][ct * 128:(ct + 1) * 128, :]),
                 writes=[('cst', ct, 1)], dma='ck%d' % ct)
        for ct in range(4):
            b = self.bank()
            for c in range(2):
                self.tr(self.ps[b][:, c * 128:(c + 1) * 128], CST[ct][:, c * 128:(c + 1) * 128], self.IDF,
                        [('cst', ct, 0), 'idf'], [('ps', b)], inc=False)
            self.tr(self.ps[b][0:64, 256:384], CST[ct][:, 256:320], self.IDF, [('cst', ct, 1), 'idf'], [('ps', b)])
            ks = slice(1536 + ct * 128, 1536 + (ct + 1) * 128)
            self.act(CKVb[:, :, ks], self.ps[b][:, 0:256].rearrange("p (c t) -> p c t", c=2), AF.Copy, [],
                     [('ps', b), ('CKVb', 3)])
            self.dve(lambda e, ks=ks, b=b: e.tensor_copy(out=KRb[0:64, ks], in_=self.ps[b][0:64, 256:384]), [],
                     [('ps', b), ('KRb', 3)])
        wdq = self.dr['mla_w_dq'][0]
        wdkv = self.dr['mla_w_dkv'][0]
        sq0 = self.wload(wdq[:, 0:256].rearrange("(k p) n -> p k n", p=128))
        sq1 = self.wload(wdq[:, 256:384].rearrange("(k p) n -> p k n", p=128), rows=8, cols=128)
        sk0 = self.wload(wdkv[:, 0:256].rearrange("(k p) n -> p k n", p=128))
        sk1 = self.wload(wdkv[:, 256:320].rearrange("(k p) n -> p k n", p=128), rows=8, cols=64)
        sk2 = self.wload(self.dr['w_dkv_sw'].rearrange("(k p) n -> p k n", p=128), rows=8, cols=64)
        qsl = [(sq0[0], lambda k: sq0[1][:, k, 0:128]), (sq0[0], lambda k: sq0[1][:, k, 128:256]), (sq1[0], lambda k: sq1[1][:, k, :])]
        ksl = [(sk0[0], lambda k: sk0[1][:, k, 0:128]), (sk0[0], lambda k: sk0[1][:, k, 128:256])]
        pp_ = [6]

        def pbank():
            pp_[0] = 13 - pp_[0]
            return pp_[0]
        for tb in range(NTB):
            sl = slice(tb * TB, (tb + 1) * TB)
            for (nch, wsl, raw, rk, sbank) in ((3, qsl, lambda c: QDall[:, tb, c, :], 'QDr', tb),
                                               (2, ksl, lambda c: CKVf[:, c, sl], 'CKVr', 3 + tb)):
                for c in range(nch):
                    b = pbank()
                    slot, lf = wsl[c]
                    for k in range(KC):
                        self.mm(self.ps[b], lf(k), self.H[:, k, sl], k == 0, k == KC - 1, [('wf', slot), ('H', k, tb)], [('ps', b)])
                    self.act(raw(c), self.ps[b], AF.Copy, [], [('ps', b), (rk, c, tb)])
                    i = self.xb_rr
                    self.xb_rr = (i + 1) % 4
                    self.act(self.XB[i], raw(c), AF.Square, [(rk, c, tb)], [('xb', i)])
                    self.mm(self.ps[sbank], self.ONES, self.XB[i], c == 0, c == nch - 1, ['ones', ('xb', i)], [('ps', sbank)], inc=True)
            br = pbank()
            for k in range(KC):
                self.mm(self.ps[br][0:64, :], sk1[1][:, k, :], self.H[:, k, sl], k == 0, k == KC - 1,
                        [('wf', sk1[0]), ('H', k, tb)], [('ps', br)])
            self.act(KRraw[0:64, sl], self.ps[br][0:64, :], AF.Copy, [], [('ps', br), ('KRraw', tb)])
            if tb < 2:
                bs = pbank()
                for k in range(KC):
                    self.mm(self.ps[bs][0:64, :], sk2[1][:, k, :], self.H[:, k, sl], k == 0, k == KC - 1,
                            [('wf', sk2[0]), ('H', k, tb)], [('ps', bs)])
                t1, t2 = self.tmp(), self.tmp()
                a1, a2 = self.TMP[t1][0:64, :], self.TMP[t2][0:64, :]
                self.dve(lambda e, a1=a1, sl=sl: e.tensor_tensor(out=a1, in0=KRraw[0:64, sl], in1=ROC[0:64, sl], op=ALU.mult),
                         [('KRraw', tb), ('wa', 0)], [('tmp', t1)])
                self.dve(lambda e, a2=a2, sl=sl, bs=bs: e.tensor_tensor(out=a2, in0=self.ps[bs][0:64, :], in1=ROS[0:64, sl], op=ALU.mult),
                         [('wa', 1)], [('ps', bs), ('tmp', t2)])
                self.dve(lambda e, a1=a1, a2=a2, sl=sl: e.tensor_tensor(out=KRb[0:64, sl], in0=a1, in1=a2, op=ALU.add),
                         [('tmp', t1), ('tmp', t2)], [('KRb', tb)])
            else:
                self.dve(lambda e, sl=sl: e.tensor_copy(out=KRb[0:64, sl], in_=KRraw[0:64, sl]), [('KRraw', tb)], [('KRb', tb)])
        for tb in range(NTB):
            sl = slice(tb * TB, (tb + 1) * TB)
            for (nch, g_col, raw, rk, sbank, vi) in ((3, CO_QG, lambda c: QDall[:, tb, c, :], 'QDr', tb, 0),
                                                     (2, CO_KVG, lambda c: CKVf[:, c, sl], 'CKVr', 3 + tb, 2)):
                var, rstd = self.ST[vi], self.ST[vi + 1]
                kv_, kr_ = ('st', vi), ('st', vi + 1)
                self.act(var, self.ps[sbank], AF.Identity, ['eps2'], [('ps', sbank), kv_], bias=self.EPS_RMS, scale=1.0 / (nch * 128))
                self.act(var, var, AF.Ln, [], [kv_])
                self.act(rstd, var, AF.Exp, [kv_], [kr_], scale=-0.5)
                for c in range(nch):
                    g = self.vcol(g_col + c)
                    x_ = raw(c)
                    if nch == 3:
                        ob = QN[:, c, sl]
                        self.dve(lambda e, ob=ob, x_=x_, g=g, rstd=rstd: e.scalar_tensor_tensor(out=ob, in0=x_, scalar=g, in1=rstd,
                                                                                              op0=ALU.mult, op1=ALU.mult),
                                 [(rk, c, tb), kr_, 'vec'], [('QN', c, tb)])
                    else:
                        self.dve(lambda e, x_=x_, g=g, rstd=rstd: e.scalar_tensor_tensor(out=x_, in0=x_, scalar=g, in1=rstd,
                                                                                        op0=ALU.mult, op1=ALU.mult),
                                 [kr_, 'vec'], [(rk, c, tb), ('CKVf', c, tb)])
                        self.act(CKVb[:, c, sl], x_, AF.Copy, [('CKVf', c, tb)], [('CKV', c, tb)])
        for tt in range(T // 128):
            s = tt % 2
            ts_ = slice(tt * 128, (tt + 1) * 128)
            tb = tt // 4
            b = self.bank()
            for c in range(2):
                self.tr(self.ps[b][:, c * 128:(c + 1) * 128], CKVf[:, c, ts_], self.IDF, [('CKVf', c, tb), 'idf'], [('ps', b)], inc=False)
            self.tr(self.ps[b][:, 256:320], KRraw[0:64, ts_], self.IDF[0:64, 0:64], [('KRraw', tb), 'idf'], [('ps', b)])
            self.act(OST[s], self.ps[b][:, 0:320], AF.Copy, [], [('ps', b), ('ost', s)])
            P.op('sp', lambda e, s=s, ts_=ts_: e.dma_start(out=self.dr['ockv'][ts_, :], in_=OST[s][:, 0:256]), reads=[('ost', s)], dma='oa%d' % s)
            P.op('sp', lambda e, s=s, ts_=ts_: e.dma_start(out=self.dr['okr'][ts_, :], in_=OST[s][:, 256:320]), reads=[('ost', s)], dma='ob%d' % s)
        wuq = self.dr['mla_w_uq'][0]
        wukv = self.dr['mla_w_ukv'][0]
        suq = [self.wload(wuq[:, j * 512:(j + 1) * 512].rearrange("(k p) n -> p k n", p=128), rows=3, cols=512, slot=j) for j in range(3)]
        ssw = self.wload(self.dr['w_uq_sw'].rearrange("(k p) n -> p k n", p=128), rows=3, cols=512, slot=3)
        P.fence(('pool',))
        for i in range(2):
            P.op('pool', lambda e, i=i: e.dma_start(out=QR[i][64:68, 0:1024], in_=self.dr['onehot']), writes=[('QRm', i)], dma='m%d' % (2 + i))

        def uq(col0, width):
            j, o = divmod(col0, 512)
            assert o + width <= 512
            return suq[j][0], (lambda k: suq[j][1][:, k, o:o + width])
        scale = float((128 + 64) ** -0.5)
        groups = [(0, 8, [(0, 512), (512, 512), (1536, 512)], 68),
                  (8, 2, [(1024, 256)], 64), (10, 2, [(1280, 256)], 64)]
        for h in range(8):
            hb = h % 2
            swv = self.wload(wukv[:, h * 256:(h + 1) * 256].rearrange("(k p) n -> p k n", p=128), rows=2, cols=256, slot=4 + hb)
            for kb in range(4):
                b = self.bank()
                ks = slice(kb * 512, (kb + 1) * 512)
                for c in range(2):
                    self.mm(self.ps[b], swv[1][:, c, 0:128], CKVb[:, c, ks], c == 0, c == 1,
                            [('wf', swv[0])] + [('CKVb', i) for i in range(4)] + [('CKV', c, i) for i in range(3)], [('ps', b)])
                if kb % 2 == 0:
                    self.act(KT[hb][:, ks], self.ps[b], AF.Copy, [], [('ps', b), ('KT', hb)])
                else:
                    self.dve(lambda e, hb=hb, ks=ks, b=b: e.tensor_copy(out=KT[hb][:, ks], in_=self.ps[b]), [], [('ps', b), ('KT', hb)])
            for kq in range(4):
                b = self.bank()
                for q in range(4):
                    kt = kq * 4 + q
                    for c in range(2):
                        self.mm(self.ps[b][:, q * 128:(q + 1) * 128], CKVb[:, c, kt * 128:(kt + 1) * 128], swv[1][:, c, 128:256],
                                c == 0, c == 1, [('wf', swv[0])], [('ps', b)])
                src = self.ps[b].rearrange("p (q d) -> p q d", q=4)
                if kq % 2 == 0:
                    self.dve(lambda e, hb=hb, kq=kq, src=src: e.tensor_copy(out=VH[hb][:, kq * 4:(kq + 1) * 4, :], in_=src), [],
                             [('ps', b), ('VH', hb)])
                else:
                    self.act(VH[hb][:, kq * 4:(kq + 1) * 4, :], src, AF.Copy, [], [('ps', b), ('VH', hb)])
            for tb in range(NTB):
                sl = slice(tb * TB, (tb + 1) * TB)
                b = self.bank()
                slot, lf = uq(h * 192, 128) if (h * 192) % 512 + 128 <= 512 else (None, None)
                for c in range(3):
                    if slot is not None:
                        self.mm(self.ps[b], lf(c), QN[:, c, sl], c == 0, c == 2, [('wf', slot), ('QN', c, tb)], [('ps', b)])
                    else:
                        j0, o0 = divmod(h * 192, 512)
                        w0 = 512 - o0
                        self.mm(self.ps[b][0:w0, :], suq[j0][1][:, c, o0:512], QN[:, c, sl], c == 0, c == 2,
                                [('wf', suq[j0][0]), ('QN', c, tb)], [('ps', b)])
                if slot is None:
                    for c in range(3):
                        self.mm(self.ps[b][w0:128, :], suq[j0 + 1][1][:, c, 0:128 - w0], QN[:, c, sl], c == 0, c == 2,
                                [('wf', suq[j0 + 1][0]), ('QN', c, tb)], [('ps', b)])
                self.act(QT[:, sl], self.ps[b], AF.Copy, [], [('ps', b), ('QT', tb)])
                br = self.bank()
                sr, lr = uq(h * 192 + 128, 64)
                for c in range(3):
                    self.mm(self.ps[br][0:64, :], lr(c), QN[:, c, sl], c == 0, c == 2, [('wf', sr), ('QN', c, tb)], [('ps', br)])
                if tb < 2:
                    bsw = self.bank()
                    for c in range(3):
                        self.mm(self.ps[bsw][0:64, :], ssw[1][:, c, h * 64:(h + 1) * 64], QN[:, c, sl], c == 0, c == 2,
                                [('wf', ssw[0]), ('QN', c, tb)], [('ps', bsw)])
                    t1, t2 = self.tmp(), self.tmp()
                    a1, a2 = self.TMP[t1][0:64, :], self.TMP[t2][0:64, :]
                    self.dve(lambda e, a1=a1, sl=sl, br=br: e.tensor_tensor(out=a1, in0=self.ps[br][0:64, :], in1=ROC[0:64, sl], op=ALU.mult),
                             [('wa', 0)], [('ps', br), ('tmp', t1)])
                    self.dve(lambda e, a2=a2, sl=sl, bsw=bsw: e.tensor_tensor(out=a2, in0=self.ps[bsw][0:64, :], in1=ROS[0:64, sl], op=ALU.mult),
                             [('wa', 1)], [('ps', bsw), ('tmp', t2)])
                    self.dve(lambda e, a1=a1, a2=a2, sl=sl, hb=hb: e.tensor_tensor(out=QR[hb][0:64, sl], in0=a1, in1=a2, op=ALU.add),
                             [('tmp', t1), ('tmp', t2)], [('QR', hb, tb)])
                else:
                    self.act(QR[hb][0:64, sl], self.ps[br][0:64, :], AF.Copy, [], [('ps', br), ('QR', hb, tb)])
            items = []
            kbA = [(0, 512), (512, 512), (1536, 512)]
            ktA = [0, 1, 2, 3, 4, 5, 6, 7, 12, 13, 14, 15]
            for qb0 in (0, 4):
                for qi in range(4):
                    items.append(dict(qt=qb0 + qi, qi=qi, nq=4, qb0=qb0, kblocks=kbA, krows=68, ptoff=0, ktiles=ktA,
                                      first=(qi == 0), last=(qi == 3), zero=False))
            for qi in range(4):
                kb = [(1024, 256)] if qi < 2 else [(1280, 256)]
                items.append(dict(qt=8 + qi, qi=qi, nq=4, qb0=8, kblocks=kb, krows=64, ptoff=0 if qi < 2 else 2,
                                  ktiles=[8, 9, 10, 11], first=(qi == 0), last=(qi == 3), zero=(qi == 0)))

            def stage_qk(it):
                qt = it['qt']
                qs = slice(qt * 128, (qt + 1) * 128)
                tbq = qt // 4
                it['banks'] = []
                for bi_, (k0, kw) in enumerate(it['kblocks']):
                    b = (qt % 2) * 3 + bi_
                    it['banks'].append(b)
                    self.mm(self.ps[b][:, 0:kw], QT[:, qs], KT[hb][:, k0:k0 + kw], True, False,
                            [('QT', tbq), ('KT', hb)], [('ps', b)])
                    rdm = ['KRm0', 'KRm1', ('QRm', hb)] if it['krows'] == 68 else []
                    self.mm(self.ps[b][:, 0:kw], QR[hb][0:it['krows'], qs], KRb[0:it['krows'], k0:k0 + kw], False, True,
                            [('QR', hb, tbq)] + [('KRb', i) for i in range(4)] + rdm, [('ps', b)])

            def stage_max(it):
                qt = it['qt']
                par = qt % 2
                so = par * 8
                banks = it['banks']
                nb = len(banks)
                for bi, b in enumerate(banks):
                    kw = it['kblocks'][bi][1]
                    self.dve(lambda e, bi=bi, b=b, kw=kw, so=so: e.reduce_max(out=self.SM[:, so + bi:so + bi + 1], in_=self.ps[b][:, 0:kw],
                                                                           axis=mybir.AxisListType.X), [], [('ps', b), ('sm_mx', par)])
                if nb > 1:
                    self.dve(lambda e, so=so, nb=nb: e.reduce_max(out=self.SM[:, so + 3:so + 4], in_=self.SM[:, so:so + nb],
                                                                  axis=mybir.AxisListType.X), [], [('sm_mx', par)])
                    src_c = so + 3
                else:
                    src_c = so
                self.dve(lambda e, so=so, src_c=src_c: e.tensor_scalar_mul(out=self.SM[:, so + 4:so + 5], in0=self.SM[:, src_c:src_c + 1],
                                                                        scalar1=-scale), [], [('sm_mx', par), ('sm_nb', par)])

            def stage_exp(it):
                qt = it['qt']
                par = qt % 2
                so = par * 8
                pp = PP[par]
                ko = 0
                for bi, b in enumerate(it['banks']):
                    kw = it['kblocks'][bi][1]
                    self.act(pp[:, ko:ko + kw], self.ps[b][:, 0:kw], AF.Exp, [('sm_nb', par)], [('ps', b), ('PP', par, bi)],
                             bias=self.SM[:, so + 4:so + 5], scale=scale)
                    ko += kw

            def stage_transpose(it):
                qt, qi = it['qt'], it['qi']
                pp = PP[qt % 2]
                nkt = sum(w for _, w in it['kblocks']) // 128
                po = it['ptoff']
                if it['zero']:
                    z1, z2 = PT[:, 2:4, 0:256], PT[:, 0:2, 256:512]
                    self.dve(lambda e, z1=z1: e.memset(z1, 0.0), [], [('PT', 0, 0), ('PT', 1, 0)])
                    self.dve(lambda e, z2=z2: e.memset(z2, 0.0), [], [('PT', 2, 0), ('PT', 3, 0)])
                for kq in range(0, nkt, 8):
                    b = 6 + kq // 8
                    pb = self.ps[b].bitcast(BF16)
                    n8 = min(8, nkt - kq)
                    for q in range(n8):
                        kt = kq + q
                        self.tr(pb[:, q * 128:(q + 1) * 128], pp[:, kt * 128:(kt + 1) * 128], self.IDB,
                                [('PP', qt % 2, kt // 4), 'idb'], [('ps', b)], inc=(q == n8 - 1))
                    src = pb[:, 0:n8 * 128].rearrange("p (q t) -> p q t", q=n8)
                    dst = PT[:, po + kq:po + kq + n8, qi * 128:(qi + 1) * 128]
                    if (kq // 8) % 2 == 0:
                        self.act(dst, src, AF.Copy, [], [('ps', b), ('PT', qi, kq // 8)])
                    else:
                        self.dve(lambda e, dst=dst, src=src: e.tensor_copy(out=dst, in_=src), [], [('ps', b), ('PT', qi, kq // 8)])

            def stage_pv(it):
                nq, qb0 = it['nq'], it['qb0']
                nqc = nq * 128
                ktiles = it['ktiles']
                bo, bsum = 6, 7
                rdpt = [('PT', q, g) for q in range(nq) for g in range((len(ktiles) + 7) // 8)]
                for i, ktg in enumerate(ktiles):
                    self.mm(self.ps[bo][:, 0:nqc], VH[hb][:, ktg, :], PT[:, i, 0:nqc], i == 0, i == len(ktiles) - 1,
                            [('VH', hb)] + rdpt, [('ps', bo)])
                for i, ktg in enumerate(ktiles):
                    self.mm(self.ps[bsum][:, 0:nqc], self.ONES, PT[:, i, 0:nqc], i == 0, i == len(ktiles) - 1,
                            ['ones'] + rdpt, [('ps', bsum)])
                self.act(RB[:, 0:nqc], self.ps[bsum][:, 0:nqc], AF.Copy, [], [('ps', bsum), 'RB'])
                self.act(RB[:, 0:nqc], RB[:, 0:nqc], AF.Ln, [], ['RB'])
                self.act(RB[:, 0:nqc], RB[:, 0:nqc], AF.Exp, [], ['RB'], scale=-1.0)
                q0 = qb0 * 128
                tbo = q0 // TB
                dst = OT[:, h, q0:q0 + nqc]
                self.dve(lambda e, dst=dst, bo=bo, nqc=nqc: e.tensor_tensor(out=dst, in0=self.ps[bo][:, 0:nqc], in1=RB[:, 0:nqc], op=ALU.mult),
                         ['RB'], [('ps', bo), ('H', h, tbo)])

            stage_qk(items[0])
            stage_max(items[0])
            for i, it in enumerate(items):
                if i + 1 < len(items):
                    stage_qk(items[i + 1])
                stage_exp(it)
                if i + 1 < len(items):
                    stage_max(items[i + 1])
                stage_transpose(it)
                if it['last']:
                    stage_pv(it)
        if self.debug == 91:
            for c in range(KC):
                for tb in range(NTB):
                    sl = slice(tb * TB, (tb + 1) * TB)
                    self.act(self.X[:, c, sl], OT[:, c, sl], AF.Copy, [('H', c, tb)], [('X', c, tb)])
            return
        w_o = self.dr['mla_w_o'][0]
        rd_gate = self.ada_res(l, 5)
        self.wf_rr = 6
        for pair in range(4):
            so = self.wload(w_o[:, pair * 256:(pair + 1) * 256].rearrange("(k p) n -> p k n", p=128))
            for il in range(2):
                c = pair * 2 + il
                for tb in range(NTB):
                    sl = slice(tb * TB, (tb + 1) * TB)
                    b = self.bank()
                    for j in range(KC):
                        self.mm(self.ps[b], so[1][:, j, il * 128:(il + 1) * 128], OT[:, j, sl], j == 0, j == KC - 1,
                                [('wf', so[0]), ('H', j, tb)], [('ps', b)])
                    self.resid_ln(l, 5, b, c, tb, rd_gate)
        self.layernorm_all(CO_LNG + (l * 3 + 1) * 8, CO_LNB + (l * 3 + 1) * 8, post=post)

    def store_x(self, tbs=None):
        xo = [self.view(self.stage_off, (D,), F32), self.view(self.stage_off + 4096, (D,), F32)]
        tiles = range(T // 128) if tbs is None else [tt for tb in tbs for tt in range(4 * tb, 4 * tb + 4)]
        for tt in tiles:
            s = tt % 2
            for half in range(2):
                b = self.bank()
                for q in range(4):
                    c = half * 4 + q
                    self.tr(self.ps[b][:, q * 128:(q + 1) * 128], self.X[:, c, tt * 128:(tt + 1) * 128], self.IDF,
                            [('X', c, tt // 4), 'idf'], [('ps', b)], inc=(q == 3))
                out = xo[s][:, half * 512:(half + 1) * 512]
                if half == 0:
                    self.act(out, self.ps[b], AF.Copy, [], [('ps', b), ('xo', s, 0)])
                else:
                    self.dve(lambda e, out=out, b=b: e.tensor_copy(out=out, in_=self.ps[b]), [], [('ps', b), ('xo', s, 1)])
            dst = self.dr['y'][tt * 128:(tt + 1) * 128, :]
            src = xo[s]
            self.P.op('sp', lambda e, dst=dst, src=src: e.dma_start(out=dst, in_=src),
                      reads=[('xo', s, 0), ('xo', s, 1)], dma='yo%d' % s)

    def build(self):
        st = self.debug or 99
        self.constants()
        nada = [0]

        def ada_next(n=1):
            while n > 0 and nada[0] < 72:
                k = min(n, 72 - nada[0], 2)
                self.ada_units([((nada[0] + j) // 36, (nada[0] + j) % 36) for j in range(k)])
                nada[0] += k
                n -= k
        self.load_x(hook=lambda: ada_next(2))
        hook = lambda: ada_next(2)
        if st < 99:
            self.ffn(0, 0, 0, 0, ada_hook=hook)
            if st >= 7:
                self.P.fence()
                self.mixer0(0)
                self.P.fence()
            if st >= 8:
                self.ffn(0, 1, 6, 2, ada_hook=hook)
                ada_next(72)
                self.ffn(1, 0, 0, 3)
            if st >= 9:
                self.P.fence(('pe', 'act', 'dve', 'sp', 'pool'))
                self.mla(1)
                self.P.fence()
            if st >= 10 and st != 91:
                self.ffn(1, 1, 6, 5)
        else:
            self.ffn(0, 0, 0, 0, ada_hook=hook, post=lambda tb: self.modulate(0, 3, 4, [tb]))
            self.P.fence()
            self.mixer0(0, premod=True, post=lambda tb: self.modulate(0, 6, 7, [tb]))
            self.ffn(0, 1, 6, 2, ada_hook=hook, premod=True, post=lambda tb: self.modulate(1, 0, 1, [tb]))
            ada_next(72)
            self.ffn(1, 0, 0, 3, premod=True, post=lambda tb: self.modulate(1, 3, 4, [tb]))
            self.P.fence(('pe', 'act', 'dve', 'sp', 'pool'))
            self.mla(1, premod=True, post=lambda tb: self.modulate(1, 6, 7, [tb]))
            self.ffn(1, 1, 6, 5, premod=True, post=lambda tb: self.store_x([tb]))
            self.emit()
            return self.nc
        self.store_x()
        self.emit()
        return self.nc

    def emit(self):
        nc = self.nc
        P = self.P
        names = sorted(P.cnt.keys())
        sems = {}
        import contextlib
        with contextlib.ExitStack() as st:
            for n in names:
                sems[n] = st.enter_context(nc.semaphore("s_" + n))
            block = st.enter_context(nc.Block())
            finals = [(n, v) for n, v in P.cnt.items() if n not in ENGS]

            def run(e, key, final=False):
                for waits, fn, incspec in P.ops[key]:
                    for s, v in waits:
                        e.wait_ge(sems[s], v)
                    if fn is None:
                        continue
                    ins = fn(e)
                    if incspec is not None:
                        ins.then_inc(sems[incspec[0]], incspec[1])
                if final:
                    for n, v in finals:
                        e.wait_ge(sems[n], v)

            @block.tensor
            def _(e):
                run(e, 'pe')

            @block.scalar
            def _(e):
                run(e, 'act')

            @block.vector
            def _(e):
                run(e, 'dve')

            @block.gpsimd
            def _(e):
                run(e, 'pool')

            @block.sync
            def _(e):
                run(e, 'sp', final=True)
        print("ops:", {k: len(v) for k, v in P.ops.items()}, "sems:", len(names))


def _pack_vecs(inp, cond2, flag):
    v = np.zeros((128, NV), np.float32)

    def put(col, vec):
        vec = np.asarray(vec, np.float32)
        n = vec.shape[0] // 128
        v[:, col:col + n] = vec.reshape(n, 128).T
    for r in range(2):
        put(CO_COND + r * 8, cond2[r])
    for l in range(2):
        put(CO_BADA + l * 72, inp['b_ada'][l])
        for s in range(3):
            put(CO_LNG + (l * 3 + s) * 8, inp['ln_g'][l, s])
            put(CO_LNB + (l * 3 + s) * 8, inp['ln_b'][l, s])
    cw = np.asarray(inp['conv_w'][0], np.float32)
    for i in range(4):
        v[:, CO_CW + i * 31: CO_CW + (i + 1) * 31] = cw[:, i * 128:(i + 1) * 128].T
    put(CO_CB, inp['conv_b'][0]); put(CO_CNG, inp['conv_norm_g'][0]); put(CO_CNB, inp['conv_norm_b'][0])
    put(CO_PSC, inp['pool_scale'][0])
    put(CO_QG, inp['mla_q_norm_g'][0]); put(CO_KVG, inp['mla_kv_norm_g'][0])
    v[:, CO_FLAG] = flag
    return v


def _rope_tables(real):
    C = np.ones((64, 1024), np.float32)
    S = np.zeros((64, 1024), np.float32)
    if real:
        n = 1024
        row = np.repeat(np.arange(n // 64), 64).astype(np.float32)
        col = np.tile(np.arange(64), n // 64).astype(np.float32)
        inv = (10000.0 ** (-np.arange(16, dtype=np.float32) / 16)).astype(np.float32)
        ang = np.concatenate([row[:, None] * inv, col[:, None] * inv], -1).astype(np.float32)
        cos, sin = np.cos(ang), np.sin(ang)
        for a in range(2):
            for j in range(2):
                for p in range(16):
                    dd = a * 32 + j * 16 + p
                    C[dd] = cos[:, a * 16 + p]
                    S[dd] = (-sin[:, a * 16 + p]) if j == 0 else sin[:, a * 16 + p]
    return C, S


_NC_CACHE = {}


def _prep_inputs(inp):
    inp = {k: np.asarray(v) for k, v in inp.items()}
    xp = inp['x_prompt'].astype(np.float32)
    xs = inp['x_sample'].astype(np.float32)
    ident = np.eye(128, dtype=np.float32)
    onehot = np.zeros((4, 1024), np.float32)
    for j in range(4):
        onehot[j, j * 256:(j + 1) * 256] = 1.0
    perm = np.arange(64).reshape(2, 2, 16)[:, ::-1, :].reshape(64)
    w_uq = inp['mla_w_uq'][0]
    uq_r = w_uq.reshape(384, 8, 192)[:, :, 128:]
    w_uq_sw = np.ascontiguousarray(uq_r[:, :, perm].reshape(384, 512))
    w_dkv_sw = np.ascontiguousarray(inp['mla_w_dkv'][0][:, 256:][:, perm])
    shared = {k: np.ascontiguousarray(inp[k], dtype=np.float32) for k in
              ('w_ada', 'ffn_w1', 'ffn_w3', 'ffn_w2', 'cp_w_in', 'pool_w', 'cp_w_out', 'mla_w_dq', 'mla_w_uq',
               'mla_w_dkv', 'mla_w_ukv', 'mla_w_o')}
    shared.update(ident=ident, onehot=onehot, w_uq_sw=w_uq_sw, w_dkv_sw=w_dkv_sw)
    in_maps = []
    for r in range(8):
        if r < 4:
            x = np.concatenate([xs[r], xp[2 * r], xp[2 * r + 1]], 0)
            cond2 = np.stack([inp['c_ctx'], inp['c'][r]], 0)
            flag = 1.0
            cckv = inp['cache_mla_ckv'][r, 0]
            ckr = inp['cache_mla_krope'][r, 0]
            maskk = np.zeros((4, 1536), np.float32)
            C, S = _rope_tables(True)
        else:
            p0 = 8 + 6 * (r - 4)
            x = xp[p0:p0 + 6].reshape(T, D)
            cond2 = np.stack([inp['c_ctx'], inp['c_ctx']], 0)
            flag = 0.0
            cckv = np.zeros((512, 256), np.float32)
            ckr = np.zeros((512, 64), np.float32)
            maskk = np.full((4, 1536), NEG, np.float32)
            for j in range(4):
                maskk[j, j * 256:(j + 1) * 256] = 0.0
            C, S = _rope_tables(False)
        m = dict(shared)
        m.update(x=np.ascontiguousarray(x), vecs=_pack_vecs(inp, cond2, flag),
                 cache_ckv=np.ascontiguousarray(cckv, dtype=np.float32),
                 cache_kr=np.ascontiguousarray(ckr, dtype=np.float32), ropeC=C, ropeS=S, maskk=maskk)
        in_maps.append(m)
    return in_maps


def _assemble(results):
    y_p = np.zeros((32, 256, D), np.float32)
    y_s = np.zeros((4, 1024, D), np.float32)
    ckv = np.zeros((32, 1, 256, 256), np.float32)
    kr = np.zeros((32, 1, 256, 64), np.float32)
    for r in range(8):
        y = results[r]['y']
        ok = results[r]['ockv']
        okr = results[r]['okr']
        if r < 4:
            y_s[r] = y[:1024]
            for i in range(2):
                sl = slice(1024 + 256 * i, 1024 + 256 * (i + 1))
                y_p[2 * r + i] = y[sl]; ckv[2 * r + i, 0] = ok[sl]; kr[2 * r + i, 0] = okr[sl]
        else:
            p0 = 8 + 6 * (r - 4)
            for i in range(6):
                sl = slice(256 * i, 256 * (i + 1))
                y_p[p0 + i] = y[sl]; ckv[p0 + i, 0] = ok[sl]; kr[p0 + i, 0] = okr[sl]
    return y_p, y_s, ckv, kr


def kernel(**inputs):
    in_maps = _prep_inputs(inputs)
    if 'nc' not in _NC_CACHE:
        _NC_CACHE['nc'] = Builder().build()
    res = run_bass_kernel_spmd(_NC_CACHE['nc'], in_maps, core_ids=list(range(8)))
    return _assemble(res.results)
```
